# Optimizing a Trainium2 kernel written in Bass

```python
import jax, jax.numpy as jnp
from jax import lax
import numpy as np

D_MODEL = 1024
BATCH = 8
SEQ = 2048
DEPTH = 1

D_MIX = D_MODEL
HEAD_DIM = 64
N_HEADS = 8
N_KV_HEADS = 2
GQA_GROUP = N_HEADS // N_KV_HEADS
D_ATTN = N_HEADS * HEAD_DIM
D_KV = N_KV_HEADS * HEAD_DIM
D_POOL = D_MIX - D_ATTN
POOL_WINDOWS = (2, 4, 8, 16)
N_POOL_GROUPS = len(POOL_WINDOWS)
POOL_GROUP_DIM = D_POOL // N_POOL_GROUPS
D_IN = D_ATTN + 2 * D_KV + D_POOL
WINDOW = 128
BLOCK = 128
N_BUCKETS = 32
MAX_DISTANCE = 128
D_FF = 2816
EPS = 1e-6
NEG = -1e30

kernel_name = "hybrid_swa_sink_pool_macaron"


def _rmsnorm(x, g):
    x32 = x.astype(jnp.float32)
    y = x32 * lax.rsqrt(jnp.mean(x32 * x32, axis=-1, keepdims=True) + EPS)
    return (y * g.astype(jnp.float32)).astype(x.dtype)


def _swiglu(h, w_gate, w_up, w_down):
    return (jax.nn.silu(h @ w_gate) * (h @ w_up)) @ w_down


def _t5_bucket(dist):
    n = np.maximum(dist, 0)
    max_exact = N_BUCKETS // 2
    large = max_exact + (np.log(np.maximum(n, 1) / max_exact)
                         / np.log(MAX_DISTANCE / max_exact)
                         * (N_BUCKETS - max_exact)).astype(np.int32)
    large = np.minimum(large, N_BUCKETS - 1)
    return np.where(n < max_exact, n, large).astype(np.int32)


def _band_structure(n_blocks):
    ql = np.arange(BLOCK)[:, None]
    kl = np.arange(2 * BLOCK)[None, :]
    dist = ql + BLOCK - kl
    bucket = _t5_bucket(dist)
    blk = np.arange(n_blocks)[:, None, None]
    k_abs = blk * BLOCK - BLOCK + kl[None]
    mask = (dist[None] >= 0) & (dist[None] < WINDOW) & (k_abs >= 0)
    return bucket, mask


def _sliding_window_attention(q, k, v, q_gain, k_gain, sinks, rel_bias):
    B, S = q.shape[0], q.shape[1]
    nb = S // BLOCK
    q = _rmsnorm(q, q_gain)
    k = _rmsnorm(k, k_gain)
    bucket, mask = _band_structure(nb)
    bias = jnp.transpose(rel_bias[jnp.asarray(bucket)], (2, 0, 1)).astype(jnp.float32)
    bias = bias.reshape(N_KV_HEADS, GQA_GROUP, BLOCK, 2 * BLOCK)
    mask = jnp.asarray(mask)[None, :, None, None]

    qb = q.reshape(B, nb, BLOCK, N_KV_HEADS, GQA_GROUP, HEAD_DIM)
    pad = ((0, 0), (BLOCK, 0), (0, 0), (0, 0))
    kp = jnp.pad(k, pad).reshape(B, nb + 1, BLOCK, N_KV_HEADS, HEAD_DIM)
    vp = jnp.pad(v, pad).reshape(B, nb + 1, BLOCK, N_KV_HEADS, HEAD_DIM)
    kb = jnp.concatenate([kp[:, :-1], kp[:, 1:]], axis=2)
    vb = jnp.concatenate([vp[:, :-1], vp[:, 1:]], axis=2)

    logits = jnp.einsum('bnqkgd,bnskd->bnkgqs', qb, kb).astype(jnp.float32)
    logits = logits * (HEAD_DIM ** -0.5) + bias[None, None]
    logits = jnp.where(mask, logits, NEG)
    sink = sinks.astype(jnp.float32).reshape(N_KV_HEADS, GQA_GROUP)[None, None, :, :, None, None]
    m = jnp.maximum(jnp.max(logits, axis=-1, keepdims=True), sink)
    e = jnp.exp(logits - m)
    probs = e / (jnp.sum(e, axis=-1, keepdims=True) + jnp.exp(sink - m))
    out = jnp.einsum('bnkgqs,bnskd->bnqkgd', probs.astype(vb.dtype), vb)
    return out.reshape(B, S, D_ATTN)


def _pool_mixer(u, w_pool, scale):
    B, S = u.shape[0], u.shape[1]
    ug = u.reshape(B, S, N_POOL_GROUPS, POOL_GROUP_DIM)
    csum = jnp.cumsum(ug.astype(jnp.float32), axis=1)
    pos = jnp.arange(1, S + 1, dtype=jnp.float32)
    means = []
    for g, w in enumerate(POOL_WINDOWS):
        cg = csum[:, :, g]
        prev = jnp.pad(cg, ((0, 0), (w, 0), (0, 0)))[:, :S]
        cnt = jnp.minimum(pos, float(w))[None, :, None]
        means.append((cg - prev) / cnt)
    pooled = jnp.stack(means, axis=2).astype(u.dtype) - ug
    y = jnp.einsum('bsgc,gcd->bsgd', pooled, w_pool).reshape(B, S, D_POOL)
    return y * scale


def setup_inputs(seed: int = 0) -> dict:
    key = jax.random.key(seed)
    ks = jax.random.split(key, 20)
    nrm = lambda k, shape, fan_in: jax.random.normal(k, shape, jnp.float32) * fan_in ** -0.5
    gain = lambda k, shape: 1.0 + 0.02 * jax.random.normal(k, shape, jnp.float32)
    L = DEPTH
    return {
        "x": jax.random.normal(ks[0], (BATCH, SEQ, D_MODEL), jnp.float32),
        "ffn1_norm": gain(ks[1], (L, D_MODEL)),
        "ffn1_w_gate": nrm(ks[2], (L, D_MODEL, D_FF), D_MODEL),
        "ffn1_w_up": nrm(ks[3], (L, D_MODEL, D_FF), D_MODEL),
        "ffn1_w_down": nrm(ks[4], (L, D_FF, D_MODEL), D_FF),
        "mix_norm": gain(ks[5], (L, D_MODEL)),
        "w_in": nrm(ks[6], (L, D_MODEL, D_IN), D_MODEL),
        "q_norm": gain(ks[7], (L, HEAD_DIM)),
        "k_norm": gain(ks[8], (L, HEAD_DIM)),
        "attn_sinks": 0.5 * jax.random.normal(ks[9], (L, N_HEADS), jnp.float32),
        "rel_bias": 0.1 * jax.random.normal(ks[10], (N_BUCKETS, N_HEADS), jnp.float32),
        "pool_w": nrm(ks[11], (L, N_POOL_GROUPS, POOL_GROUP_DIM, POOL_GROUP_DIM), POOL_GROUP_DIM),
        "pool_scale": gain(ks[12], (L, D_POOL)),
        "w_out": nrm(ks[13], (L, D_MIX, D_MODEL), D_MIX),
        "ffn2_norm": gain(ks[14], (L, D_MODEL)),
        "ffn2_w_gate": nrm(ks[15], (L, D_MODEL, D_FF), D_MODEL),
        "ffn2_w_up": nrm(ks[16], (L, D_MODEL, D_FF), D_MODEL),
        "ffn2_w_down": nrm(ks[17], (L, D_FF, D_MODEL), D_FF),
    }


def reference(x, ffn1_norm, ffn1_w_gate, ffn1_w_up, ffn1_w_down, mix_norm, w_in,
              q_norm, k_norm, attn_sinks, rel_bias, pool_w, pool_scale, w_out,
              ffn2_norm, ffn2_w_gate, ffn2_w_up, ffn2_w_down):
    B, S = x.shape[0], x.shape[1]
    for l in range(DEPTH):
        h = _rmsnorm(x, ffn1_norm[l])
        x = x + 0.5 * _swiglu(h, ffn1_w_gate[l], ffn1_w_up[l], ffn1_w_down[l])
        h = _rmsnorm(x, mix_norm[l])
        z = h @ w_in[l]
        q = z[..., :D_ATTN].reshape(B, S, N_HEADS, HEAD_DIM)
        k = z[..., D_ATTN:D_ATTN + D_KV].reshape(B, S, N_KV_HEADS, HEAD_DIM)
        v = z[..., D_ATTN + D_KV:D_ATTN + 2 * D_KV].reshape(B, S, N_KV_HEADS, HEAD_DIM)
        u = z[..., D_ATTN + 2 * D_KV:]
        y_attn = _sliding_window_attention(q, k, v, q_norm[l], k_norm[l],
                                           attn_sinks[l], rel_bias)
        y_pool = _pool_mixer(u, pool_w[l], pool_scale[l])
        x = x + jnp.concatenate([y_attn, y_pool], axis=-1) @ w_out[l]
        h = _rmsnorm(x, ffn2_norm[l])
        x = x + 0.5 * _swiglu(h, ffn2_w_gate[l], ffn2_w_up[l], ffn2_w_down[l])
    return x
```

```python
import numpy as np
from contextlib import ExitStack
import concourse.bass as bass
import concourse.mybir as mybir
from concourse.bass_utils import run_bass_kernel_spmd

F32 = mybir.dt.float32
BF16 = mybir.dt.bfloat16
AF = mybir.ActivationFunctionType
ALU = mybir.AluOpType

S = 2048
D = 1024
DFF = 2816
NF = DFF // 128
NT = S // 128
EPS = 1e-6
RING = 3
NEGM = -30000.0


class Buf:
    __slots__ = ("name", "w", "r")

    def __init__(self, name, fence=None):
        self.name = name
        self.w = None
        self.r = dict(fence) if fence else {}


class Prog:
    ENG = ("pe", "act", "dve", "pool", "sp")

    def __init__(self, nc, stack):
        self.nc = nc
        self.stack = stack
        self.q = {k: [] for k in self.ENG}
        self.sem = {}
        self.cnt = {}
        self.inc = {}
        self.waited = {}
        for k in ("pe", "act", "dve", "pool"):
            self.new_sem(k, 1)

    def new_sem(self, key, inc):
        self.sem[key] = self.stack.enter_context(self.nc.semaphore("s_" + key))
        self.cnt[key] = 0
        self.inc[key] = inc

    def fence(self):
        return {k: self.cnt[k] for k in ("pe", "act", "dve", "pool") if self.cnt[k] > 0}

    def op(self, eng, meth, kw, reads=(), writes=(), signal=True, dma=None):
        fn = (meth, kw)
        deps = {}

        def need(tag, war=False):
            key, val = tag
            if key == eng and (eng == "pe" or war):
                return
            if deps.get(key, 0) < val:
                deps[key] = val

        for b in reads:
            if b.w:
                need(b.w)
        for b in writes:
            if b.w:
                need(b.w)
            for k, v in b.r.items():
                need((k, v), war=True)
        for key, val in deps.items():
            if self.waited.get((eng, key), 0) >= val:
                continue
            self.waited[(eng, key)] = val
            self.q[eng].append(("wait", key, val))
        comp = dma if dma else eng
        if signal:
            self.cnt[comp] += self.inc[comp]
            tag = (comp, self.cnt[comp])
            self.q[eng].append(("op", fn, comp))
        else:
            tag = (comp, self.cnt[comp] + self.inc[comp])
            self.q[eng].append(("op", fn, None))
        for b in reads:
            if b.r.get(tag[0], 0) < tag[1]:
                b.r[tag[0]] = tag[1]
        for b in writes:
            b.w = tag
            b.r = {}
        return tag

    def wait_all(self, eng, tags):
        for key, val in tags:
            if self.waited.get((eng, key), 0) >= val:
                continue
            self.waited[(eng, key)] = val
            self.q[eng].append(("wait", key, val))

    def replay(self, eng, e):
        for item in self.q[eng]:
            if item[0] == "wait":
                e.wait_ge(self.sem[item[1]], item[2])
            else:
                ins = getattr(e, item[1][0])(**item[1][1])
                if item[2] is not None:
                    ins.then_inc(self.sem[item[2]], self.inc[item[2]])


def build(stage=99):
    nc = bass.Bass("TRN2", target_bir_lowering=False)

    def din(name, shape, dt=F32):
        return nc.dram_tensor(name, list(shape), dt, kind="ExternalInput").ap()

    x_d = din("x", [S, D])
    out_d = nc.dram_tensor("out", [S, D], F32, kind="ExternalOutput").ap()
    wgu_d = [din(f"wgu{i}", [NF, 128, 2048]) for i in (1, 2)]
    wd_d = [din(f"wd{i}", [2, 11, 128, 1024]) for i in (1, 2)]
    gains_d = din("gains", [3, D])
    win_d = din("win", [11, 128, 1024])
    wout_d = din("wout", [128, 8 * D])
    poolw_d = din("poolw", [128, 4 * 128])
    cols_d = din("cols", [128, 16])
    invc_d = din("invc", [128, 4 * 16])
    bm_d = din("bm", [128, 2 * 2 * 512])
    ident_d = din("ident", [128, 128])
    bones_d = din("bones", [128, 128])
    onz_d = din("onz", [128, 192])
    wu_d = din("wu", [128, 8 * 512])
    wband_d = din("wband", [128, 12 * 128])

    with ExitStack() as st:
        P = Prog(nc, st)

        def sb(name, shape, dt):
            return st.enter_context(nc.sbuf_tensor("sb_" + name, list(shape), dt))

        xs = sb("xs", [128, NT, D], F32)
        gbc = sb("gbc", [128, D], F32)
        hn = sb("hn", [128, 2, D], BF16)
        junk = sb("junk", [128, D], BF16)
        ident = sb("ident", [128, 128], BF16)
        bones = sb("bones", [128, 128], BF16)
        onz = sb("onz", [128, 192], BF16)
        wband = sb("wband", [128, 12, 128], BF16)
        cols = sb("cols", [128, 16], F32)
        gqs = sb("gqs", [128, 1], F32)
        epsc = sb("epsc", [128, 1], F32)
        invc = sb("invc", [128, 4, 16], F32)
        ss = sb("ss", [128, 48], F32)
        ring = sb("ring", [128, RING, 2048], BF16)
        NA = 115 * 512
        arena = sb("arena", [128, NA], BF16)
        psf = st.enter_context(nc.psum_tensor("psf", [128, 8, 512], F32))

        def av(off, n):
            return arena[:, off:off + n]

        o = 0
        hTA = av(o, 8 * 1024).rearrange("p (k t) -> p k t", k=8); o += 8 * 1024
        hT_f = hTA
        actT = av(o, NF * 1024).rearrange("p (f t) -> p f t", f=NF); o += NF * 1024
        wdb = av(o, 2 * NF * 512).rearrange("p (h f n) -> p h f n", h=2, f=NF); o += 2 * NF * 512
        sil = av(o, 2 * 1024).bitcast(F32).rearrange("p (s n) -> p s n", s=2); o += 2 * 1024
        assert o <= NA
        o = 0
        hTB = av(o + 8 * 1024, 8 * 1024).rearrange("p (k t) -> p k t", k=8)
        hTs = [hTA, hTB]
        woutb = av(o + 8 * 1024, 8 * D).rearrange("p (k n) -> p k n", k=8); o += 8 * S
        yT = av(o, 8 * S).rearrange("p (k t) -> p k t", k=8); o += 8 * S
        trt = av(o, 2 * 1024).bitcast(F32).rearrange("p (s n) -> p s n", s=4)
        eskf = av(o + 2048, 1024).bitcast(F32).rearrange("p (c n) -> p c n", c=2); o += 2 * S
        vaug = av(o, NT * 2 * 192).rearrange("p (t c n) -> p t c n", t=NT, c=2); o += NT * 2 * 192
        sq = av(o, 1024).rearrange("p (s n) -> p s n", s=2); o += 1024
        poolw = av(o, 512).rearrange("p (g n) -> p g n", g=4); o += 512
        rtmp = av(o, 2 * 1024).bitcast(F32).rearrange("p (s n) -> p s n", s=2); o += 2 * 1024
        wub = ring[:, 0:2, :].rearrange("p s (k n) -> p (s k) n", k=4)
        utok = av(o + 4096, 8192).rearrange("p (t n) -> p t n", t=NT)
        bm = av(o, 2048).rearrange("p (c j n) -> p c j n", c=2, j=2)
        probs = av(o + 2048, 2048).rearrange("p (s j n) -> p s j n", s=2, j=2)
        kpad = av(o + 4096, 4 * S).rearrange("p (c r t) -> p c r t", c=2, r=2)
        o += 3 * 4096
        assert o <= NA, o

        B_xs = [[Buf(f"xs{t}_{h}") for h in range(2)] for t in range(NT)]
        B_gbc = Buf("gbc")
        B_hn = [Buf("hn0"), Buf("hn1")]
        B_hTA = [Buf(f"hTA{i}") for i in range(8)]
        B_ss0 = Buf("ss0")
        B_cst = {k: Buf(k) for k in ("ident", "bones", "cols", "invc", "esk", "gqs", "eps", "onz", "wband")}
        B_ring = [Buf(f"ring{i}") for i in range(RING)]
        B_ps = [Buf(f"ps{i}") for i in range(8)]
        ring_n = [0]
        norm_n = [0]

        def dma_sem(key):
            if key not in P.sem:
                P.new_sem(key, 16)
            return key

        def DMA(eng, key, out, in_, reads=(), writes=()):
            return P.op(eng, "dma_start", dict(out=out, in_=in_), reads=reads, writes=writes, dma=dma_sem(key))

        def ACT(out, in_, func, reads, writes, **kw):
            return P.op("act", "activation", dict(out=out, in_=in_, func=func, **kw), reads=reads, writes=writes)

        def MM(out, lhsT, rhs, start, stop, reads, writes, signal=True):
            return P.op("pe", "matmul", dict(out=out, lhsT=lhsT, rhs=rhs, start=start, stop=stop),
                        reads=reads, writes=writes, signal=signal)

        def TS(out, in0, s1, s2, op0, op1, reads, writes, eng="dve"):
            kw = dict(out=out, in0=in0, scalar1=s1, scalar2=s2, op0=op0)
            if op1 is not None:
                kw["op1"] = op1
            return P.op(eng, "tensor_scalar", kw, reads=reads, writes=writes)

        def STT(out, in0, scalar, in1, op0, op1, reads, writes):
            return P.op("dve", "scalar_tensor_tensor", dict(out=out, in0=in0, scalar=scalar, in1=in1, op0=op0, op1=op1),
                        reads=reads, writes=writes)

        def TT(out, in0, in1, op, reads, writes, eng="dve"):
            return P.op(eng, "tensor_tensor", dict(out=out, in0=in0, in1=in1, op=op), reads=reads, writes=writes)

        def CP(out, in_, reads, writes, eng="dve"):
            return P.op(eng, "tensor_copy", dict(out=out, in_=in_), reads=reads, writes=writes)

        def load_gain(i):
            DMA("sp", "dma_gbc", gbc[:], gains_d[i:i + 1, :].broadcast_to([128, D]), writes=[B_gbc])

        def load_x(t0, nt, key, extra_reads=()):
            DMA("sp", key, xs[:, t0:t0 + nt, :], x_d[128 * t0:128 * (t0 + nt), :].rearrange("(t p) d -> p t d", p=128),
                reads=list(extra_reads), writes=[B_xs[t][h] for t in range(t0, t0 + nt) for h in range(2)])

        load_x(0, 1, "dma_x0")
        load_x(1, 1, "dma_x1")
        DMA("pool", "dma_ident", ident[:], ident_d[:, :], writes=[B_cst["ident"]])

        def late_consts():
            DMA("pool", "dma_bones", bones[:], bones_d[:, :], writes=[B_cst["bones"]])
            DMA("pool", "dma_onz", onz[:], onz_d[:, :], writes=[B_cst["onz"]])
            DMA("pool", "dma_wband", wband[:].rearrange("p m n -> p (m n)"), wband_d[:, :], writes=[B_cst["wband"]])
        load_gain(0)
        load_x(2, 2, "dma_x2")
        load_x(4, 4, "dma_x3")
        DMA("sp", "dma_cols", cols[:], cols_d[:, :], writes=[B_cst["cols"]])
        DMA("sp", "dma_invc", invc[:].rearrange("p g n -> p (g n)"), invc_d[:, :], writes=[B_cst["invc"]])

        def norm_pre_begin(tiles):
            n = len(tiles)
            base = norm_n[0]
            norm_n[0] += n
            B_ss = Buf(f"ss{base}")
            B_ss.w = B_ss0.w
            return dict(base=base, tiles=tiles, B_ss=B_ss)

        def norm_pre_square(ctx, i):
            tt = ctx["tiles"][i]
            base = ctx["base"]
            ACT(junk[:], xs[:, tt, :], AF.Square, reads=B_xs[tt], writes=[ctx["B_ss"]],
                scale=1.0 / 32.0, accum_out=ss[:, base + i:base + i + 1])

        def norm_pre_finish(ctx):
            base, n, B_ss = ctx["base"], len(ctx["tiles"]), ctx["B_ss"]
            scs = ss[:, base:base + n]
            ACT(scs, scs, AF.Ln, reads=[B_ss, B_cst["eps"]], writes=[B_ss], bias=epsc[:, 0:1])
            ACT(scs, scs, AF.Exp, reads=[B_ss], writes=[B_ss], scale=-0.5)

        def norm_pre(tiles):
            ctx = norm_pre_begin(tiles)
            for i in range(len(tiles)):
                norm_pre_square(ctx, i)
            norm_pre_finish(ctx)
            return ctx

        def norm_fin_a(ctx, i):
            tt = ctx["tiles"][i]
            slot = ctx["base"] + i
            B_ss = ctx["B_ss"]
            hs = slot % 2
            STT(hn[:, hs, :], xs[:, tt, :], ss[:, slot:slot + 1], gbc[:], ALU.mult, ALU.mult,
                reads=B_xs[tt] + [B_ss, B_gbc], writes=[B_hn[hs]])

        def norm_fin_b(ctx, i, hT, lt, B_hT_tile):
            slot = ctx["base"] + i
            hs = slot % 2
            pb = 6 + (slot % 2)
            pT = psf[:, pb, :].bitcast(BF16).rearrange("p (k t) -> p k t", k=8)
            for k in range(8):
                P.op("pe", "transpose", dict(out=pT[:, k, :], in_=hn[:, hs, k * 128:(k + 1) * 128], identity=ident[:]),
                     reads=[B_hn[hs], B_cst["ident"]], writes=[B_ps[pb]], signal=(k == 7))
            if slot % 2 == 0:
                ACT(hT[:, :, lt * 128:(lt + 1) * 128], pT, AF.Copy, reads=[B_ps[pb]], writes=[B_hT_tile])
            else:
                CP(hT[:, :, lt * 128:(lt + 1) * 128], pT, reads=[B_ps[pb]], writes=[B_hT_tile])

        def norm_group(tiles, hT, lts, B_hT_tiles):
            ctx = norm_pre(tiles)
            for i in range(len(tiles)):
                norm_fin_a(ctx, i)
                norm_fin_b(ctx, i, hT, lts[i], B_hT_tiles[i])

        def ffn(idx, final, first_norm_done=False, next_norm=None):
            fen = P.fence()
            B_hT = B_hTA
            B_act = [[Buf(f"act{f}_{h}", fen) for h in range(2)] for f in range(NF)]
            B_wd = [[Buf(f"wd{h}_{i}", fen) for i in range(11)] for h in range(2)]
            B_sil = [Buf("sil0", fen), Buf("sil1", fen)]
            wgu = wgu_d[idx]
            wd = wd_d[idx]
            if not first_norm_done and idx != 0:
                load_gain(2)
            out_tags = []
            setn = 0
            on = 0

            def issue_ring(f):
                slot = ring_n[0] % RING
                extra = []
                if idx == 0 and ring_n[0] == 0:
                    extra = [B_xs[3][0]]
                elif idx == 0 and ring_n[0] in (1, 2):
                    extra = [B_xs[7][0]]
                ring_n[0] += 1
                DMA("pool", f"dma_ring{slot}", ring[:, slot, :], wgu[f], reads=extra, writes=[B_ring[slot]])
                return slot

            def issue_wd(i):
                h, pi = divmod(i, 11)
                DMA("pool", f"dma_wd{h}_{pi}", wdb[:, h, 2 * pi:2 * pi + 2, :],
                    wd[h, pi].rearrange("p (f n) -> p f n", f=2), writes=[B_wd[h][pi]])

            nctx = None
            for sbk in range(2):
                slots = {}
                for f in range(RING):
                    slots[f] = issue_ring(f)
                if idx == 0 and sbk == 0:
                    late_consts()
                wd_next = 0
                if sbk == 0 and not first_norm_done:
                    ctxs = [None] * 8
                    for lt in range(9):
                        if lt < 8:
                            ctxs[lt] = norm_pre([lt])
                        if lt >= 1:
                            norm_fin_a(ctxs[lt - 1], 0)
                            norm_fin_b(ctxs[lt - 1], 0, hTA, lt - 1, B_hT[lt - 1])
                    if idx == 0:
                        load_x(8, 4, "dma_x4", extra_reads=[B_hTA[7]])
                        load_x(12, 4, "dma_x5")
                for f in range(NF):
                    slot = slots[f]
                    for half in range(2):
                        pg, pu = 2 * (setn % 2), 2 * (setn % 2) + 1
                        s_ = setn % 2
                        setn += 1
                        hts = B_hT[4 * half:4 * half + 4]
                        rhs_cols = slice(half * 512, (half + 1) * 512)
                        for k in range(8):
                            MM(psf[:, pg, :], ring[:, slot, k * 128:(k + 1) * 128], hT_f[:, k, rhs_cols],
                               k == 0, k == 7, reads=[B_ring[slot]] + hts, writes=[B_ps[pg]], signal=(k == 7))
                        for k in range(8):
                            MM(psf[:, pu, :], ring[:, slot, 1024 + k * 128:1024 + (k + 1) * 128], hT_f[:, k, rhs_cols],
                               k == 0, k == 7, reads=[B_ring[slot]] + hts, writes=[B_ps[pu]], signal=(k == 7))
                        ACT(sil[:, s_, :], psf[:, pg, :], AF.Silu, reads=[B_ps[pg]], writes=[B_sil[s_]])
                        TT(actT[:, f, rhs_cols], psf[:, pu, :], sil[:, s_, :], ALU.mult,
                           reads=[B_ps[pu], B_sil[s_]], writes=[B_act[f][half]])
                    if f + RING < NF:
                        slots[f + RING] = issue_ring(f + RING)
                    if f == NF - 11:
                        if sbk == 0:
                            nctx = norm_pre_begin([8 + lt for lt in range(8)])
                        elif next_norm is not None:
                            load_gain(next_norm[0])
                            nctx = norm_pre_begin(next_norm[1])
                        else:
                            nctx = None
                    if nctx is not None and NF - 10 <= f < NF - 2:
                        norm_pre_square(nctx, f - (NF - 10))
                    if nctx is not None and f == NF - 2:
                        norm_pre_finish(nctx)
                    if f >= 1 and wd_next < 22:
                        issue_wd(wd_next)
                        wd_next += 1
                        if wd_next < 22 and f >= 20:
                            issue_wd(wd_next)
                            wd_next += 1
                while wd_next < 22:
                    issue_wd(wd_next)
                    wd_next += 1
                for lt in range(8):
                    tt = sbk * 8 + lt
                    if nctx is not None:
                        norm_fin_a(nctx, lt)
                    for dh in range(2):
                        po = 4 + (on % 2)
                        on += 1
                        cs = slice(dh * 512, (dh + 1) * 512)
                        for f in range(NF):
                            MM(psf[:, po, :], actT[:, f, lt * 128:(lt + 1) * 128], wdb[:, dh, f, :], f == 0, f == NF - 1,
                               reads=[B_act[f][lt // 4], B_wd[dh][f // 2]], writes=[B_ps[po]], signal=(f == NF - 1))
                        STT(xs[:, tt, cs], psf[:, po, :], 0.5, xs[:, tt, cs], ALU.mult, ALU.add,
                            reads=[B_ps[po], B_xs[tt][dh]], writes=[B_xs[tt][dh]])
                        if final:
                            out_tags.append(DMA("sp", "dma_out", out_d[tt * 128:(tt + 1) * 128, cs], xs[:, tt, cs],
                                                reads=[B_xs[tt][dh]]))
                    if nctx is not None:
                        norm_fin_b(nctx, lt, hTA, lt, B_hTA[lt])
            return out_tags

        def dump_xs():
            tags = []
            for tt in range(NT):
                tags.append(DMA("sp", "dma_out", out_d[tt * 128:(tt + 1) * 128, :], xs[:, tt, :], reads=B_xs[tt]))
            return tags

        def mixer(sub=9, first_half_done=False, next_gain=None):
            fen = P.fence()
            B_hT = B_hTA + [Buf(f"hTB{i}", fen) for i in range(8)]
            B_y = [[Buf(f"y{k}_{t}", fen) for t in range(NT)] for k in range(8)]
            B_v = [Buf(f"v{t}", fen) for t in range(4)]
            B_pw = Buf("pw", fen)
            B_sq = [Buf("sq0", fen), Buf("sq1", fen)]
            B_rt = [Buf("rt0", fen), Buf("rt1", fen)]
            B_wout = Buf("wout", fen)
            if not first_half_done:
                load_gain(1)
            TS(gqs[:], cols[:, 0:1], 0.125, None, ALU.mult, None, reads=[B_cst["cols"]], writes=[B_cst["gqs"]])
            chunk_slots = {}

            def issue_chunk(m):
                slot = ring_n[0] % RING
                ring_n[0] += 1
                DMA("pool", f"dma_ring{slot}", ring[:, slot, 0:1024], win_d[m], writes=[B_ring[slot]])
                chunk_slots[m] = slot

            order = [6, 7, 8, 9, 4, 5, 10]
            B_ut = [Buf(f"ut{t}", fen) for t in range(NT)]
            DMA("pool", "dma_ring0", ring[:, 0:2, :].rearrange("p s n -> p (s n)"), wu_d[:, :], writes=[B_ring[0], B_ring[1]])
            B_ring[1].w = B_ring[0].w
            ring_n[0] = (ring_n[0] // RING + 1) * RING + 2
            issue_chunk(order[0])
            DMA("pool", "dma_pw", poolw[:].rearrange("p g n -> p (g n)"), poolw_d[:, :], writes=[B_pw])
            B_bm = Buf("bm", fen)
            B_probs = [Buf("pr0", fen), Buf("pr1", fen)]
            B_trt = [Buf(f"trt{i}", fen) for i in range(4)]
            B_eskf = Buf("eskf", fen)
            DMA("pool", "dma_bm", bm[:].rearrange("p c j n -> p (c j n)"), bm_d[:, :], writes=[B_bm])
            P.op("dve", "memset", dict(ap=eskf[:].rearrange("p c n -> p (c n)"), constant=0.0), writes=[B_eskf])
            for c in range(2):
                for i in range(2):
                    cc = slice(i * 128, (i + 1) * 128)
                    ACT(eskf[:, c, cc], eskf[:, c, cc], AF.Exp, reads=[B_eskf, B_cst["cols"]], writes=[B_eskf],
                        bias=cols[:, 6 + 2 * c + i:7 + 2 * c + i])
            P.op("pool", "memset", dict(ap=vaug[:].rearrange("p t c n -> p (t c n)"), constant=0.0), writes=B_v)
            pn = [0]
            evn = [0]

            def evac(out, in_, reads, writes, scale=None):
                evn[0] += 1
                if evn[0] % 2 == 0:
                    if scale is None:
                        ACT(out, in_, AF.Copy, reads=reads, writes=writes)
                    else:
                        ACT(out, in_, AF.Copy, reads=reads, writes=writes, scale=scale)
                else:
                    if scale is None:
                        CP(out, in_, reads=reads, writes=writes)
                    else:
                        TS(out, in_, scale, None, ALU.mult, None, reads=reads, writes=writes)

            def u_tile(tt):
                pb = pn[0] % 4
                pn[0] += 1
                for k in range(8):
                    MM(psf[:, pb, :], hTs[tt // 8][:, k, (tt % 8) * 128:(tt % 8 + 1) * 128], wub[:, k, :], k == 0, k == 7,
                       reads=[B_ring[0], B_ring[1], B_hT[tt]], writes=[B_ps[pb]], signal=(k == 7))
                evac(utok[:, tt, :], psf[:, pb, :], reads=[B_ps[pb]], writes=[B_ut[tt]])

            def pool_bank(g, tb):
                pb = pn[0] % 4
                pn[0] += 1
                for ti in range(4):
                    T = 4 * tb + ti
                    dst = psf[:, pb, ti * 128:(ti + 1) * 128]
                    gs = slice(g * 128, (g + 1) * 128)
                    if T == 0:
                        MM(dst, utok[:, 0, gs], wband[:, 3 * g + 2, :], True, True, reads=[B_ut[0], B_cst["wband"]],
                           writes=[B_ps[pb]], signal=False)
                    else:
                        MM(dst, utok[:, T - 1, gs], wband[:, 3 * g + 1, :], True, False, reads=[B_ut[T - 1], B_cst["wband"]],
                           writes=[B_ps[pb]], signal=False)
                        MM(dst, utok[:, T, gs], wband[:, 3 * g + 0, :], False, True, reads=[B_ut[T], B_cst["wband"]],
                           writes=[B_ps[pb]], signal=(ti == 3))
                evac(yT[:, 4 + g, tb * 512:(tb + 1) * 512], psf[:, pb, :], reads=[B_ps[pb]],
                     writes=[B_y[4 + g][4 * tb + t] for t in range(4)])

            def pool_linear_item(g, tb):
                pb = pn[0] % 4
                pn[0] += 1
                cs = slice(tb * 512, (tb + 1) * 512)
                yb = [B_y[4 + g][4 * tb + t] for t in range(4)]
                MM(psf[:, pb, :], poolw[:, g, :], yT[:, 4 + g, cs], True, True, reads=[B_pw] + yb, writes=[B_ps[pb]])
                evac(yT[:, 4 + g, cs], psf[:, pb, :], reads=[B_ps[pb], B_cst["cols"]], writes=yb, scale=cols[:, 2 + g:3 + g])

            pl_items = [(g, tb) for tb in range(4) for g in range(4)]

            if not first_half_done:
                norm_group(list(range(8)), hTA, list(range(8)), B_hT[0:8])
            nctx = norm_pre_begin(list(range(8, 16)))
            for i in range(8):
                u_tile(i)
                norm_pre_square(nctx, i)
            norm_pre_finish(nctx)
            for g in range(4):
                pool_bank(g, 0)
            norm_fin_a(nctx, 0)
            for i in range(8):
                if i + 1 < 8:
                    norm_fin_a(nctx, i + 1)
                if i < 4:
                    pool_bank(i, 1)
                norm_fin_b(nctx, i, hTB, i, B_hT[8 + i])
                if i >= 1:
                    u_tile(8 + i - 1)
            u_tile(NT - 1)
            for tb in (2, 3):
                for g in range(4):
                    pool_bank(g, tb)
            for m in order[1:RING]:
                issue_chunk(m)

            def proj_chunk(m, tb):
                slot = chunk_slots[m]
                pb = pn[0] % 4
                pn[0] += 1
                for k in range(8):
                    MM(psf[:, pb, :], ring[:, slot, k * 128:(k + 1) * 128], hTs[tb // 2][:, k, (tb % 2) * 512:(tb % 2 + 1) * 512],
                       k == 0, k == 7, reads=[B_ring[slot]] + B_hT[4 * tb:4 * tb + 4], writes=[B_ps[pb]], signal=(k == 7))
                return pb

            def after_chunk(m):
                i = order.index(m)
                if i + RING < len(order):
                    issue_chunk(order[i + RING])

            kq_n = [0]
            B_k = [None]

            kq_pend = []

            def kq_finish():
                m, tb, pb, s2 = kq_pend.pop(0)
                cs = slice(tb * 512, (tb + 1) * 512)
                pb2 = 4 + s2
                MM(psf[:, pb2, :], bones[:], sq[:, s2, :], True, True, reads=[B_cst["bones"], B_sq[s2]], writes=[B_ps[pb2]])
                ACT(rtmp[:, s2, :], psf[:, pb2, :], AF.Ln, reads=[B_ps[pb2], B_cst["eps"]], writes=[B_rt[s2]], bias=epsc[:, 0:1])
                ACT(rtmp[:, s2, :], rtmp[:, s2, :], AF.Exp, reads=[B_rt[s2]], writes=[B_rt[s2]], scale=-0.5)
                if m < 6:
                    c = m - 4
                    for par in range(2):
                        pr = slice(64 * par, 64 * par + 64)
                        STT(kpad[pr, c, par, cs], psf[pr, pb, :], cols[pr, 1:2], rtmp[pr, s2, :], ALU.mult, ALU.mult,
                            reads=[B_ps[pb], B_rt[s2], B_cst["cols"]], writes=[B_k[0][c][tb]])
                else:
                    j = m - 6
                    STT(yT[:, j, cs], psf[:, pb, :], gqs[:, 0:1], rtmp[:, s2, :], ALU.mult, ALU.mult,
                        reads=[B_ps[pb], B_rt[s2], B_cst["gqs"]], writes=[B_y[j][4 * tb + t] for t in range(4)])

            def do_kq(m):
                for tb in range(4):
                    pb = proj_chunk(m, tb)
                    s2 = kq_n[0] % 2
                    kq_n[0] += 1
                    ACT(sq[:, s2, :], psf[:, pb, :], AF.Square, reads=[B_ps[pb]], writes=[B_sq[s2]], scale=0.125)
                    kq_pend.append((m, tb, pb, s2))
                    if len(kq_pend) > 1:
                        kq_finish()
                while kq_pend:
                    kq_finish()
                after_chunk(m)

            fenk = P.fence()
            B_k[0] = [[Buf(f"k{c}_{t}", fenk) for t in range(4)] for c in range(2)]
            P.op("pool", "memset", dict(ap=kpad[:].rearrange("p c r t -> p (c r t)"), constant=0.0),
                 writes=[B_k[0][c][t] for c in range(2) for t in range(4)])
            while pl_items:
                pool_linear_item(*pl_items.pop(0))
            for m in order[:6]:
                do_kq(m)
            while kq_pend:
                kq_finish()
            B_k = B_k[0]
            slot = chunk_slots[10]
            for t4 in range(4):
                pb = pn[0] % 4
                pn[0] += 1
                for ti in range(4):
                    tt = 4 * t4 + ti
                    for k in range(8):
                        MM(psf[:, pb, ti * 128:(ti + 1) * 128], hTs[tt // 8][:, k, (tt % 8) * 128:(tt % 8 + 1) * 128],
                           ring[:, slot, k * 128:(k + 1) * 128], k == 0, k == 7,
                           reads=[B_ring[slot], B_hT[tt]], writes=[B_ps[pb]], signal=(k == 7 and ti == 3))
                ACT(vaug[:, 4 * t4:4 * t4 + 4, :, 64:128], psf[:, pb, :].rearrange("p (t c n) -> p t c n", t=4, c=2), AF.Copy,
                    reads=[B_ps[pb]], writes=[B_v[t4]])
            DMA("pool", "dma_wout", woutb[:].rearrange("p k n -> p (k n)"), wout_d[:, :], writes=[B_wout] + B_hT[8:16])
            iters = [(n, c) for n in range(NT) for c in range(2)]

            def att_logits(it):
                n, c = iters[it]
                s2 = it % 2
                js = [1] if n == 0 else [0, 1]
                qb = slice(n * 128, (n + 1) * 128)
                for j in js:
                    kt = n - 1 + j
                    pl = 2 * s2 + j
                    MM(psf[:, pl, :], ident[:], bm[:, c, j, :], True, False, reads=[B_cst["ident"], B_bm],
                       writes=[B_ps[pl]], signal=False)
                    for par in range(2):
                        MM(psf[:, pl, par * 256:(par + 1) * 256], kpad[:, c, par, kt * 128:(kt + 1) * 128],
                           yT[:, 2 * c:2 * c + 2, qb], False, par == 1,
                           reads=[B_k[c][kt // 4], B_y[2 * c][n], B_y[2 * c + 1][n]], writes=[B_ps[pl]], signal=(par == 1))
                    ACT(probs[:, s2, j, :], psf[:, pl, :], AF.Exp, reads=[B_ps[pl]], writes=[B_probs[s2]])

            def att_pv_norm(it):
                n, c = iters[it]
                s2 = it % 2
                s4 = it % 4
                js = [1] if n == 0 else [0, 1]
                pv = 4 + s4
                for (dst, is_v) in ((psf[:, pv, 0:256], True), (psf[:, pv, 256:512], False)):
                    k = 0
                    last = 2 * len(js) - 1
                    for j in js:
                        kt = n - 1 + j
                        for par in range(2):
                            c0 = 64 if par == 0 else 0
                            if is_v:
                                lhsT, rd = vaug[:, kt, c, c0:c0 + 128], [B_v[kt // 4], B_probs[s2]]
                            else:
                                lhsT, rd = onz[:, c0:c0 + 128], [B_cst["onz"], B_probs[s2]]
                            MM(dst, lhsT, probs[:, s2, j, par * 256:(par + 1) * 256], k == 0, k == last,
                               reads=rd, writes=[B_ps[pv]], signal=(k == last))
                            k += 1
                TT(trt[:, s4, :], psf[:, pv, 256:512], eskf[:, c, :], ALU.add, reads=[B_ps[pv], B_eskf], writes=[B_trt[s4]])
                ACT(trt[:, s4, :], trt[:, s4, :], AF.Ln, reads=[B_trt[s4]], writes=[B_trt[s4]])
                ACT(trt[:, s4, :], trt[:, s4, :], AF.Exp, reads=[B_trt[s4]], writes=[B_trt[s4]], scale=-1.0)

            def att_scale(it):
                n, c = iters[it]
                s4 = it % 4
                pv = 4 + s4
                qb = slice(n * 128, (n + 1) * 128)
                yb = [B_y[2 * c][n], B_y[2 * c + 1][n]]
                TT(yT[:, 2 * c:2 * c + 2, qb], psf[:, pv, 0:256].rearrange("p (i q) -> p i q", i=2),
                   trt[:, s4, :].rearrange("p (i q) -> p i q", i=2), ALU.mult, reads=[B_ps[pv], B_trt[s4]], writes=yb)

            att_logits(0)
            for it in range(len(iters)):
                if it + 1 < len(iters):
                    att_logits(it + 1)
                att_pv_norm(it)
                if it >= 1:
                    att_scale(it - 1)
            att_scale(len(iters) - 1)
            on = 0
            if next_gain is not None:
                load_gain(next_gain)
            pend = []
            for tt in range(NT):
                if pend and not pend[-1][2]:
                    norm_fin_a(pend[-1][0], 0)
                    pend[-1][2] = True
                for dh in range(2):
                    po = on % 4
                    on += 1
                    cs = slice(dh * 512, (dh + 1) * 512)
                    for k in range(8):
                        MM(psf[:, po, :], yT[:, k, tt * 128:(tt + 1) * 128], woutb[:, k, cs], k == 0, k == 7,
                           reads=[B_y[k][tt], B_wout], writes=[B_ps[po]], signal=(k == 7))
                    TT(xs[:, tt, cs], psf[:, po, :], xs[:, tt, cs], ALU.add, reads=[B_ps[po], B_xs[tt][dh]], writes=[B_xs[tt][dh]])
                if len(pend) >= 2 or (tt >= 8 and pend):
                    ctx, t0, _ = pend.pop(0)
                    norm_fin_b(ctx, 0, hTA, t0, B_hTA[t0])
                if next_gain is not None and tt < 8:
                    pend.append([norm_pre([tt]), tt, False])

        P.op("dve", "memset", dict(ap=ss[:], constant=0.0), writes=[B_ss0])
        P.op("dve", "memset", dict(ap=epsc[:], constant=EPS), writes=[B_cst["eps"]])
        if stage == 1:
            ffn(0, final=False)
            tags = dump_xs()
        elif stage == 2:
            ffn(0, final=False, next_norm=(1, list(range(8))))
            mixer(first_half_done=True)
            tags = dump_xs()
        else:
            ffn(0, final=False, next_norm=(1, list(range(8))))
            mixer(first_half_done=True, next_gain=2)
            tags = ffn(1, final=True, first_norm_done=True)
        P.wait_all("sp", [max(tags, key=lambda t: t[1])])

        with nc.Block() as block:
            @block.tensor
            def _(e):
                P.replay("pe", e)

            @block.scalar
            def _(e):
                P.replay("act", e)

            @block.vector
            def _(e):
                P.replay("dve", e)

            @block.gpsimd
            def _(e):
                P.replay("pool", e)

            @block.sync
            def _(e):
                P.replay("sp", e)
    return nc


def _t5_bucket(dist):
    n = np.maximum(dist, 0)
    max_exact = 16
    large = max_exact + (np.log(np.maximum(n, 1) / max_exact) / np.log(128 / max_exact) * (32 - max_exact)).astype(np.int32)
    large = np.minimum(large, 31)
    return np.where(n < max_exact, n, large).astype(np.int32)


def prep_shared(inp):
    f32 = np.float32
    sh = {}
    for i, pre in ((1, "ffn1"), (2, "ffn2")):
        wg = np.asarray(inp[pre + "_w_gate"], f32)[0]
        wu = np.asarray(inp[pre + "_w_up"], f32)[0]
        wd = np.asarray(inp[pre + "_w_down"], f32)[0]
        g = wg.reshape(8, 128, NF, 128).transpose(2, 1, 0, 3).reshape(NF, 128, 1024)
        u = wu.reshape(8, 128, NF, 128).transpose(2, 1, 0, 3).reshape(NF, 128, 1024)
        sh[f"wgu{i}"] = np.ascontiguousarray(np.concatenate([g, u], axis=2))
        d = wd.reshape(11, 2, 128, 2, 512).transpose(3, 0, 2, 1, 4).reshape(2, 11, 128, 1024)
        sh[f"wd{i}"] = np.ascontiguousarray(d)
    sh["gains"] = np.ascontiguousarray(np.stack([np.asarray(inp["ffn1_norm"], f32)[0], np.asarray(inp["mix_norm"], f32)[0],
                                                 np.asarray(inp["ffn2_norm"], f32)[0]]))
    win = np.asarray(inp["w_in"], f32)[0]
    chunks = []
    for g in range(4):
        chunks.append(win[:, 768 + 128 * g:768 + 128 * (g + 1)])
    for c in range(2):
        kc = win[:, 512 + 64 * c:512 + 64 * (c + 1)]
        chunks.append(np.concatenate([kc, kc], axis=1))
    for j in range(4):
        chunks.append(win[:, 128 * j:128 * (j + 1)])
    chunks.append(win[:, 640:768])
    wl = np.stack(chunks)
    sh["win"] = np.ascontiguousarray(wl.reshape(11, 8, 128, 128).transpose(0, 2, 1, 3).reshape(11, 128, 1024))
    wo = np.asarray(inp["w_out"], f32)[0]
    sh["wout"] = np.ascontiguousarray(wo.reshape(8, 128, D).transpose(1, 0, 2).reshape(128, 8 * D))
    pw = np.asarray(inp["pool_w"], f32)[0]
    sh["poolw"] = np.ascontiguousarray(pw.transpose(1, 0, 2).reshape(128, 512))
    cols = np.zeros((128, 16), f32)
    cols[:, 0] = np.tile(np.asarray(inp["q_norm"], f32)[0], 2)
    cols[:, 1] = np.tile(np.asarray(inp["k_norm"], f32)[0], 2)
    ps = np.asarray(inp["pool_scale"], f32)[0]
    for g in range(4):
        cols[:, 2 + g] = ps[128 * g:128 * (g + 1)]
    sinks = np.asarray(inp["attn_sinks"], f32)[0]
    for c in range(2):
        for i in range(2):
            cols[0:64, 6 + 2 * c + i] = sinks[4 * c + 2 * i]
            cols[64:128, 6 + 2 * c + i] = sinks[4 * c + 2 * i + 1]
    sh["cols"] = cols
    invc = np.zeros((128, 4, 16), f32)
    for g in range(4):
        w = 2 ** (g + 1)
        invc[:, g, :] = (1.0 / np.minimum(np.arange(1, 17), w)).astype(f32)[None, :]
    sh["invc"] = invc.reshape(128, 64)
    rb = np.asarray(inp["rel_bias"], f32)
    rbx = np.concatenate([rb, np.full((1, 8), NEGM, f32)], axis=0)
    sl = np.arange(128)[:, None]
    q = np.arange(128)[None, :]
    bmh = np.zeros((128, 2, 2, 2, 2, 128), f32)
    for j in range(2):
        dist = q - sl + (128 if j == 0 else 0)
        valid = (dist >= 0) & (dist < 128)
        idx = np.where(valid, _t5_bucket(dist), 32)
        for c in range(2):
            for par in range(2):
                for i in range(2):
                    bmh[:, c, j, par, i, :] = rbx[idx, 4 * c + 2 * i + par]
    sh["bm"] = np.ascontiguousarray(bmh.reshape(128, 2048))
    sh["ident"] = np.eye(128, dtype=f32)
    bo = np.zeros((128, 128), f32)
    bo[0:64, 0:64] = 1.0
    bo[64:128, 64:128] = 1.0
    sh["bones"] = bo
    sh["wu"] = np.ascontiguousarray(win[:, 768:1280].reshape(8, 128, 512).transpose(1, 0, 2).reshape(128, 8 * 512))
    wb = np.zeros((128, 12, 128), f32)
    s_i = np.arange(128)[:, None]
    t_i = np.arange(128)[None, :]
    for g in range(4):
        w = 2 ** (g + 1)
        band = ((t_i - s_i >= 0) & (t_i - s_i < w)).astype(f32)
        eye = (s_i == t_i).astype(f32)
        wb[:, 3 * g + 0, :] = band / w - eye
        wb[:, 3 * g + 1, :] = ((t_i + 128 - s_i) < w).astype(f32) / w
        wb[:, 3 * g + 2, :] = band / np.minimum(t_i + 1, w).astype(f32) - eye
    sh["wband"] = np.ascontiguousarray(wb.reshape(128, 12 * 128))
    oz = np.zeros((128, 192), f32)
    oz[:, 64:128] = 1.0
    sh["onz"] = oz
    return sh


_CACHE = {}


def kernel(**inputs):
    stage = inputs.pop("_stage", 99)
    x = np.asarray(inputs["x"], np.float32)
    sh = prep_shared(inputs)
    if stage not in _CACHE:
        _CACHE[stage] = build(stage)
    nc = _CACHE[stage]
    in_maps = []
    for b in range(8):
        m = dict(sh)
        m["x"] = np.ascontiguousarray(x[b])
        in_maps.append(m)
    res = run_bass_kernel_spmd(nc, in_maps, core_ids=list(range(8)))
    return np.stack([res.results[b]["out"] for b in range(8)]).astype(np.float32)
```

```python
import numpy as np
from contextlib import ExitStack
import concourse.bass as bass
import concourse.mybir as mybir
from concourse.bass_utils import run_bass_kernel_spmd

F32 = mybir.dt.float32
BF16 = mybir.dt.bfloat16
AF = mybir.ActivationFunctionType
ALU = mybir.AluOpType

S = 2048
D = 1024
DFF = 2816
NF = DFF // 128
NT = S // 128
EPS = 1e-6
RING = 3
NEGM = -30000.0


class Buf:
    __slots__ = ("name", "w", "r")

    def __init__(self, name, fence=None):
        self.name = name
        self.w = None
        self.r = dict(fence) if fence else {}


class Prog:
    ENG = ("pe", "act", "dve", "pool", "sp")

    def __init__(self, nc, stack):
        self.nc = nc
        self.stack = stack
        self.q = {k: [] for k in self.ENG}
        self.sem = {}
        self.cnt = {}
        self.inc = {}
        self.waited = {}
        for k in ("pe", "act", "dve", "pool"):
            self.new_sem(k, 1)

    def new_sem(self, key, inc):
        self.sem[key] = self.stack.enter_context(self.nc.semaphore("s_" + key))
        self.cnt[key] = 0
        self.inc[key] = inc

    def fence(self):
        return {k: self.cnt[k] for k in ("pe", "act", "dve", "pool") if self.cnt[k] > 0}

    def op(self, eng, meth, kw, reads=(), writes=(), signal=True, dma=None):
        fn = (meth, kw)
        deps = {}

        def need(tag, war=False):
            key, val = tag
            if key == eng and (eng == "pe" or war):
                return
            if deps.get(key, 0) < val:
                deps[key] = val

        for b in reads:
            if b.w:
                need(b.w)
        for b in writes:
            if b.w:
                need(b.w)
            for k, v in b.r.items():
                need((k, v), war=True)
        for key, val in deps.items():
            if self.waited.get((eng, key), 0) >= val:
                continue
            self.waited[(eng, key)] = val
            self.q[eng].append(("wait", key, val))
        comp = dma if dma else eng
        if signal:
            self.cnt[comp] += self.inc[comp]
            tag = (comp, self.cnt[comp])
            self.q[eng].append(("op", fn, comp))
        else:
            tag = (comp, self.cnt[comp] + self.inc[comp])
            self.q[eng].append(("op", fn, None))
        for b in reads:
            if b.r.get(tag[0], 0) < tag[1]:
                b.r[tag[0]] = tag[1]
        for b in writes:
            b.w = tag
            b.r = {}
        return tag

    def wait_all(self, eng, tags):
        for key, val in tags:
            if self.waited.get((eng, key), 0) >= val:
                continue
            self.waited[(eng, key)] = val
            self.q[eng].append(("wait", key, val))

    def replay(self, eng, e):
        for item in self.q[eng]:
            if item[0] == "wait":
                e.wait_ge(self.sem[item[1]], item[2])
            else:
                ins = getattr(e, item[1][0])(**item[1][1])
                if item[2] is not None:
                    ins.then_inc(self.sem[item[2]], self.inc[item[2]])


def build(stage=99):
    nc = bass.Bass("TRN2", target_bir_lowering=False)

    def din(name, shape, dt=F32):
        return nc.dram_tensor(name, list(shape), dt, kind="ExternalInput").ap()

    x_d = din("x", [S, D])
    out_d = nc.dram_tensor("out", [S, D], F32, kind="ExternalOutput").ap()
    wgu_d = [din(f"wgu{i}", [NF, 128, 2048]) for i in (1, 2)]
    wd_d = [din(f"wd{i}", [2, 11, 128, 1024]) for i in (1, 2)]
    gains_d = din("gains", [3, D])
    win_d = din("win", [11, 128, 1024])
    wout_d = din("wout", [128, 8 * D])
    poolw_d = din("poolw", [128, 4 * 128])
    cols_d = din("cols", [128, 16])
    invc_d = din("invc", [128, 4 * 16])
    bm_d = din("bm", [128, 2 * 2 * 512])
    ident_d = din("ident", [128, 128])
    bones_d = din("bones", [128, 128])
    onz_d = din("onz", [128, 192])
    wu_d = din("wu", [128, 8 * 512])
    wband_d = din("wband", [128, 12 * 128])

    with ExitStack() as st:
        P = Prog(nc, st)

        def sb(name, shape, dt):
            return st.enter_context(nc.sbuf_tensor("sb_" + name, list(shape), dt))

        xs = sb("xs", [128, NT, D], F32)
        gbc = sb("gbc", [128, D], F32)
        hn = sb("hn", [128, 2, D], BF16)
        junk = sb("junk", [128, D], BF16)
        ident = sb("ident", [128, 128], BF16)
        bones = sb("bones", [128, 128], BF16)
        onz = sb("onz", [128, 192], BF16)
        wband = sb("wband", [128, 12, 128], BF16)
        cols = sb("cols", [128, 16], F32)
        gqs = sb("gqs", [128, 1], F32)
        epsc = sb("epsc", [128, 1], F32)
        invc = sb("invc", [128, 4, 16], F32)
        ss = sb("ss", [128, 48], F32)
        ring = sb("ring", [128, RING, 2048], BF16)
        NA = 115 * 512
        arena = sb("arena", [128, NA], BF16)
        psf = st.enter_context(nc.psum_tensor("psf", [128, 8, 512], F32))

        def av(off, n):
            return arena[:, off:off + n]

        o = 0
        hTA = av(o, 8 * 1024).rearrange("p (k t) -> p k t", k=8); o += 8 * 1024
        hT_f = hTA
        actT = av(o, NF * 1024).rearrange("p (f t) -> p f t", f=NF); o += NF * 1024
        wdb = av(o, 2 * NF * 512).rearrange("p (h f n) -> p h f n", h=2, f=NF); o += 2 * NF * 512
        sil = av(o, 2 * 1024).bitcast(F32).rearrange("p (s n) -> p s n", s=2); o += 2 * 1024
        assert o <= NA
        o = 0
        hTB = av(o + 8 * 1024, 8 * 1024).rearrange("p (k t) -> p k t", k=8)
        hTs = [hTA, hTB]
        woutb = av(o + 8 * 1024, 8 * D).rearrange("p (k n) -> p k n", k=8); o += 8 * S
        yT = av(o, 8 * S).rearrange("p (k t) -> p k t", k=8); o += 8 * S
        trt = av(o, 2 * 1024).bitcast(F32).rearrange("p (s n) -> p s n", s=2)
        eskf = av(o + 2048, 1024).bitcast(F32); o += 2 * S
        vaug = av(o, NT * 2 * 192).rearrange("p (t c n) -> p t c n", t=NT, c=2); o += NT * 2 * 192
        sq = av(o, 1024).rearrange("p (s n) -> p s n", s=2); o += 1024
        poolw = av(o, 512).rearrange("p (g n) -> p g n", g=4); o += 512
        rtmp = av(o, 2 * 1024).bitcast(F32).rearrange("p (s n) -> p s n", s=2); o += 2 * 1024
        wub = ring[:, 0:2, :].rearrange("p s (k n) -> p (s k) n", k=4)
        utok = av(o + 4096, 8192).rearrange("p (t n) -> p t n", t=NT)
        bm = av(o, 2048).rearrange("p (c j n) -> p c j n", c=2, j=2)
        probs = av(o + 2048, 2048).rearrange("p (s j n) -> p s j n", s=2, j=2)
        kpad = av(o + 4096, 2 * S).rearrange("p (c t) -> p c t", c=2)
        o += 3 * 4096
        assert o <= NA, o

        B_xs = [[Buf(f"xs{t}_{h}") for h in range(2)] for t in range(NT)]
        B_gbc = Buf("gbc")
        B_hn = [Buf("hn0"), Buf("hn1")]
        B_hTA = [Buf(f"hTA{i}") for i in range(8)]
        B_ss0 = Buf("ss0")
        B_cst = {k: Buf(k) for k in ("ident", "bones", "cols", "invc", "esk", "gqs", "eps", "onz", "wband")}
        B_ring = [Buf(f"ring{i}") for i in range(RING)]
        B_ps = [Buf(f"ps{i}") for i in range(8)]
        ring_n = [0]
        norm_n = [0]

        def dma_sem(key):
            if key not in P.sem:
                P.new_sem(key, 16)
            return key

        def DMA(eng, key, out, in_, reads=(), writes=()):
            return P.op(eng, "dma_start", dict(out=out, in_=in_), reads=reads, writes=writes, dma=dma_sem(key))

        def ACT(out, in_, func, reads, writes, **kw):
            return P.op("act", "activation", dict(out=out, in_=in_, func=func, **kw), reads=reads, writes=writes)

        def MM(out, lhsT, rhs, start, stop, reads, writes, signal=True):
            return P.op("pe", "matmul", dict(out=out, lhsT=lhsT, rhs=rhs, start=start, stop=stop),
                        reads=reads, writes=writes, signal=signal)

        def TS(out, in0, s1, s2, op0, op1, reads, writes, eng="dve"):
            kw = dict(out=out, in0=in0, scalar1=s1, scalar2=s2, op0=op0)
            if op1 is not None:
                kw["op1"] = op1
            return P.op(eng, "tensor_scalar", kw, reads=reads, writes=writes)

        def STT(out, in0, scalar, in1, op0, op1, reads, writes):
            return P.op("dve", "scalar_tensor_tensor", dict(out=out, in0=in0, scalar=scalar, in1=in1, op0=op0, op1=op1),
                        reads=reads, writes=writes)

        def TT(out, in0, in1, op, reads, writes, eng="dve"):
            return P.op(eng, "tensor_tensor", dict(out=out, in0=in0, in1=in1, op=op), reads=reads, writes=writes)

        def CP(out, in_, reads, writes, eng="dve"):
            return P.op(eng, "tensor_copy", dict(out=out, in_=in_), reads=reads, writes=writes)

        def load_gain(i):
            DMA("sp", "dma_gbc", gbc[:], gains_d[i:i + 1, :].broadcast_to([128, D]), writes=[B_gbc])

        def load_x(t0, nt, key, extra_reads=()):
            DMA("sp", key, xs[:, t0:t0 + nt, :], x_d[128 * t0:128 * (t0 + nt), :].rearrange("(t p) d -> p t d", p=128),
                reads=list(extra_reads), writes=[B_xs[t][h] for t in range(t0, t0 + nt) for h in range(2)])

        load_x(0, 1, "dma_x0")
        load_x(1, 1, "dma_x1")
        DMA("pool", "dma_ident", ident[:], ident_d[:, :], writes=[B_cst["ident"]])

        def late_consts():
            DMA("pool", "dma_bones", bones[:], bones_d[:, :], writes=[B_cst["bones"]])
            DMA("pool", "dma_onz", onz[:], onz_d[:, :], writes=[B_cst["onz"]])
            DMA("pool", "dma_wband", wband[:].rearrange("p m n -> p (m n)"), wband_d[:, :], writes=[B_cst["wband"]])
        load_gain(0)
        load_x(2, 2, "dma_x2")
        load_x(4, 4, "dma_x3")
        DMA("sp", "dma_cols", cols[:], cols_d[:, :], writes=[B_cst["cols"]])
        DMA("sp", "dma_invc", invc[:].rearrange("p g n -> p (g n)"), invc_d[:, :], writes=[B_cst["invc"]])

        def norm_pre_begin(tiles):
            n = len(tiles)
            base = norm_n[0]
            norm_n[0] += n
            B_ss = Buf(f"ss{base}")
            B_ss.w = B_ss0.w
            return dict(base=base, tiles=tiles, B_ss=B_ss)

        def norm_pre_square(ctx, i):
            tt = ctx["tiles"][i]
            base = ctx["base"]
            ACT(junk[:], xs[:, tt, :], AF.Square, reads=B_xs[tt], writes=[ctx["B_ss"]],
                scale=1.0 / 32.0, accum_out=ss[:, base + i:base + i + 1])

        def norm_pre_finish(ctx):
            base, n, B_ss = ctx["base"], len(ctx["tiles"]), ctx["B_ss"]
            scs = ss[:, base:base + n]
            ACT(scs, scs, AF.Ln, reads=[B_ss, B_cst["eps"]], writes=[B_ss], bias=epsc[:, 0:1])
            ACT(scs, scs, AF.Exp, reads=[B_ss], writes=[B_ss], scale=-0.5)

        def norm_pre(tiles):
            ctx = norm_pre_begin(tiles)
            for i in range(len(tiles)):
                norm_pre_square(ctx, i)
            norm_pre_finish(ctx)
            return ctx

        def norm_fin_a(ctx, i):
            tt = ctx["tiles"][i]
            slot = ctx["base"] + i
            B_ss = ctx["B_ss"]
            hs = slot % 2
            STT(hn[:, hs, :], xs[:, tt, :], ss[:, slot:slot + 1], gbc[:], ALU.mult, ALU.mult,
                reads=B_xs[tt] + [B_ss, B_gbc], writes=[B_hn[hs]])

        def norm_fin_b(ctx, i, hT, lt, B_hT_tile):
            slot = ctx["base"] + i
            hs = slot % 2
            pb = 6 + (slot % 2)
            pT = psf[:, pb, :].bitcast(BF16).rearrange("p (k t) -> p k t", k=8)
            for k in range(8):
                P.op("pe", "transpose", dict(out=pT[:, k, :], in_=hn[:, hs, k * 128:(k + 1) * 128], identity=ident[:]),
                     reads=[B_hn[hs], B_cst["ident"]], writes=[B_ps[pb]], signal=(k == 7))
            if slot % 2 == 0:
                ACT(hT[:, :, lt * 128:(lt + 1) * 128], pT, AF.Copy, reads=[B_ps[pb]], writes=[B_hT_tile])
            else:
                CP(hT[:, :, lt * 128:(lt + 1) * 128], pT, reads=[B_ps[pb]], writes=[B_hT_tile])

        def norm_group(tiles, hT, lts, B_hT_tiles):
            ctx = norm_pre(tiles)
            for i in range(len(tiles)):
                norm_fin_a(ctx, i)
                norm_fin_b(ctx, i, hT, lts[i], B_hT_tiles[i])

        def ffn(idx, final, first_norm_done=False, next_norm=None):
            fen = P.fence()
            B_hT = B_hTA
            B_act = [[Buf(f"act{f}_{h}", fen) for h in range(2)] for f in range(NF)]
            B_wd = [[Buf(f"wd{h}_{i}", fen) for i in range(11)] for h in range(2)]
            B_sil = [Buf("sil0", fen), Buf("sil1", fen)]
            wgu = wgu_d[idx]
            wd = wd_d[idx]
            if not first_norm_done and idx != 0:
                load_gain(2)
            out_tags = []
            setn = 0
            on = 0

            def issue_ring(f):
                slot = ring_n[0] % RING
                extra = []
                if idx == 0 and ring_n[0] == 0:
                    extra = [B_xs[3][0]]
                elif idx == 0 and ring_n[0] in (1, 2):
                    extra = [B_xs[7][0]]
                ring_n[0] += 1
                DMA("pool", f"dma_ring{slot}", ring[:, slot, :], wgu[f], reads=extra, writes=[B_ring[slot]])
                return slot

            def issue_wd(i):
                h, pi = divmod(i, 11)
                DMA("pool", f"dma_wd{h}_{pi}", wdb[:, h, 2 * pi:2 * pi + 2, :],
                    wd[h, pi].rearrange("p (f n) -> p f n", f=2), writes=[B_wd[h][pi]])

            nctx = None
            for sbk in range(2):
                slots = {}
                for f in range(RING):
                    slots[f] = issue_ring(f)
                if idx == 0 and sbk == 0:
                    late_consts()
                wd_next = 0
                if sbk == 0 and not first_norm_done:
                    ctxs = [None] * 8
                    for lt in range(9):
                        if lt < 8:
                            ctxs[lt] = norm_pre([lt])
                        if lt >= 1:
                            norm_fin_a(ctxs[lt - 1], 0)
                            norm_fin_b(ctxs[lt - 1], 0, hTA, lt - 1, B_hT[lt - 1])
                    if idx == 0:
                        load_x(8, 4, "dma_x4", extra_reads=[B_hTA[7]])
                        load_x(12, 4, "dma_x5")
                for f in range(NF):
                    slot = slots[f]
                    for half in range(2):
                        pg, pu = 2 * (setn % 2), 2 * (setn % 2) + 1
                        s_ = setn % 2
                        setn += 1
                        hts = B_hT[4 * half:4 * half + 4]
                        rhs_cols = slice(half * 512, (half + 1) * 512)
                        for k in range(8):
                            MM(psf[:, pg, :], ring[:, slot, k * 128:(k + 1) * 128], hT_f[:, k, rhs_cols],
                               k == 0, k == 7, reads=[B_ring[slot]] + hts, writes=[B_ps[pg]], signal=(k == 7))
                        for k in range(8):
                            MM(psf[:, pu, :], ring[:, slot, 1024 + k * 128:1024 + (k + 1) * 128], hT_f[:, k, rhs_cols],
                               k == 0, k == 7, reads=[B_ring[slot]] + hts, writes=[B_ps[pu]], signal=(k == 7))
                        ACT(sil[:, s_, :], psf[:, pg, :], AF.Silu, reads=[B_ps[pg]], writes=[B_sil[s_]])
                        TT(actT[:, f, rhs_cols], psf[:, pu, :], sil[:, s_, :], ALU.mult,
                           reads=[B_ps[pu], B_sil[s_]], writes=[B_act[f][half]])
                    if f + RING < NF:
                        slots[f + RING] = issue_ring(f + RING)
                    if f == NF - 11:
                        if sbk == 0:
                            nctx = norm_pre_begin([8 + lt for lt in range(8)])
                        elif next_norm is not None:
                            load_gain(next_norm[0])
                            nctx = norm_pre_begin(next_norm[1])
                        else:
                            nctx = None
                    if nctx is not None and NF - 10 <= f < NF - 2:
                        norm_pre_square(nctx, f - (NF - 10))
                    if nctx is not None and f == NF - 2:
                        norm_pre_finish(nctx)
                    if f >= 1 and wd_next < 22:
                        issue_wd(wd_next)
                        wd_next += 1
                        if wd_next < 22 and f >= 20:
                            issue_wd(wd_next)
                            wd_next += 1
                while wd_next < 22:
                    issue_wd(wd_next)
                    wd_next += 1
                for lt in range(8):
                    tt = sbk * 8 + lt
                    if nctx is not None:
                        norm_fin_a(nctx, lt)
                    for dh in range(2):
                        po = 4 + (on % 2)
                        on += 1
                        cs = slice(dh * 512, (dh + 1) * 512)
                        for f in range(NF):
                            MM(psf[:, po, :], actT[:, f, lt * 128:(lt + 1) * 128], wdb[:, dh, f, :], f == 0, f == NF - 1,
                               reads=[B_act[f][lt // 4], B_wd[dh][f // 2]], writes=[B_ps[po]], signal=(f == NF - 1))
                        STT(xs[:, tt, cs], psf[:, po, :], 0.5, xs[:, tt, cs], ALU.mult, ALU.add,
                            reads=[B_ps[po], B_xs[tt][dh]], writes=[B_xs[tt][dh]])
                        if final:
                            out_tags.append(DMA("sp", "dma_out", out_d[tt * 128:(tt + 1) * 128, cs], xs[:, tt, cs],
                                                reads=[B_xs[tt][dh]]))
                    if nctx is not None:
                        norm_fin_b(nctx, lt, hTA, lt, B_hTA[lt])
            return out_tags

        def dump_xs():
            tags = []
            for tt in range(NT):
                tags.append(DMA("sp", "dma_out", out_d[tt * 128:(tt + 1) * 128, :], xs[:, tt, :], reads=B_xs[tt]))
            return tags

        def mixer(sub=9, first_half_done=False, next_gain=None):
            fen = P.fence()
            B_hT = B_hTA + [Buf(f"hTB{i}", fen) for i in range(8)]
            B_y = [[Buf(f"y{k}_{t}", fen) for t in range(NT)] for k in range(8)]
            B_v = [Buf(f"v{t}", fen) for t in range(4)]
            B_pw = Buf("pw", fen)
            B_sq = [Buf("sq0", fen), Buf("sq1", fen)]
            B_rt = [Buf("rt0", fen), Buf("rt1", fen)]
            B_wout = Buf("wout", fen)
            if not first_half_done:
                load_gain(1)
            TS(gqs[:], cols[:, 0:1], 0.125, None, ALU.mult, None, reads=[B_cst["cols"]], writes=[B_cst["gqs"]])
            chunk_slots = {}

            def issue_chunk(m):
                slot = ring_n[0] % RING
                ring_n[0] += 1
                DMA("pool", f"dma_ring{slot}", ring[:, slot, 0:1024], win_d[m], writes=[B_ring[slot]])
                chunk_slots[m] = slot

            order = [6, 7, 8, 9, 4, 10]
            B_ut = [Buf(f"ut{t}", fen) for t in range(NT)]
            DMA("pool", "dma_ring0", ring[:, 0:2, :].rearrange("p s n -> p (s n)"), wu_d[:, :], writes=[B_ring[0], B_ring[1]])
            B_ring[1].w = B_ring[0].w
            ring_n[0] = (ring_n[0] // RING + 1) * RING + 2
            issue_chunk(order[0])
            DMA("pool", "dma_pw", poolw[:].rearrange("p g n -> p (g n)"), poolw_d[:, :], writes=[B_pw])
            B_bm = Buf("bm", fen)
            B_probs = [Buf("pr0", fen), Buf("pr1", fen)]
            B_trt = [Buf(f"trt{i}", fen) for i in range(2)]
            B_eskf = Buf("eskf", fen)
            DMA("pool", "dma_bm", bm[:].rearrange("p c j n -> p (c j n)"), bm_d[:, :], writes=[B_bm])
            P.op("dve", "memset", dict(ap=eskf[:], constant=0.0), writes=[B_eskf])
            for j in range(4):
                cc = slice(j * 128, (j + 1) * 128)
                ACT(eskf[:, cc], eskf[:, cc], AF.Exp, reads=[B_eskf, B_cst["cols"]], writes=[B_eskf], bias=cols[:, 6 + j:7 + j])
            P.op("pool", "memset", dict(ap=vaug[:].rearrange("p t c n -> p (t c n)"), constant=0.0), writes=B_v)
            pn = [0]
            evn = [0]

            def evac(out, in_, reads, writes, scale=None):
                evn[0] += 1
                if evn[0] % 2 == 0:
                    if scale is None:
                        ACT(out, in_, AF.Copy, reads=reads, writes=writes)
                    else:
                        ACT(out, in_, AF.Copy, reads=reads, writes=writes, scale=scale)
                else:
                    if scale is None:
                        CP(out, in_, reads=reads, writes=writes)
                    else:
                        TS(out, in_, scale, None, ALU.mult, None, reads=reads, writes=writes)

            def u_tile(tt):
                pb = pn[0] % 4
                pn[0] += 1
                for k in range(8):
                    MM(psf[:, pb, :], hTs[tt // 8][:, k, (tt % 8) * 128:(tt % 8 + 1) * 128], wub[:, k, :], k == 0, k == 7,
                       reads=[B_ring[0], B_ring[1], B_hT[tt]], writes=[B_ps[pb]], signal=(k == 7))
                evac(utok[:, tt, :], psf[:, pb, :], reads=[B_ps[pb]], writes=[B_ut[tt]])

            def pool_bank(g, tb):
                pb = pn[0] % 4
                pn[0] += 1
                for ti in range(4):
                    T = 4 * tb + ti
                    dst = psf[:, pb, ti * 128:(ti + 1) * 128]
                    gs = slice(g * 128, (g + 1) * 128)
                    if T == 0:
                        MM(dst, utok[:, 0, gs], wband[:, 3 * g + 2, :], True, True, reads=[B_ut[0], B_cst["wband"]],
                           writes=[B_ps[pb]], signal=False)
                    else:
                        MM(dst, utok[:, T - 1, gs], wband[:, 3 * g + 1, :], True, False, reads=[B_ut[T - 1], B_cst["wband"]],
                           writes=[B_ps[pb]], signal=False)
                        MM(dst, utok[:, T, gs], wband[:, 3 * g + 0, :], False, True, reads=[B_ut[T], B_cst["wband"]],
                           writes=[B_ps[pb]], signal=(ti == 3))
                evac(yT[:, 4 + g, tb * 512:(tb + 1) * 512], psf[:, pb, :], reads=[B_ps[pb]],
                     writes=[B_y[4 + g][4 * tb + t] for t in range(4)])

            def pool_linear_item(g, tb):
                pb = pn[0] % 4
                pn[0] += 1
                cs = slice(tb * 512, (tb + 1) * 512)
                yb = [B_y[4 + g][4 * tb + t] for t in range(4)]
                MM(psf[:, pb, :], poolw[:, g, :], yT[:, 4 + g, cs], True, True, reads=[B_pw] + yb, writes=[B_ps[pb]])
                evac(yT[:, 4 + g, cs], psf[:, pb, :], reads=[B_ps[pb], B_cst["cols"]], writes=yb, scale=cols[:, 2 + g:3 + g])

            pl_items = [(g, tb) for tb in range(4) for g in range(4)]

            if not first_half_done:
                norm_group(list(range(8)), hTA, list(range(8)), B_hT[0:8])
            nctx = norm_pre_begin(list(range(8, 16)))
            for i in range(8):
                u_tile(i)
                norm_pre_square(nctx, i)
            norm_pre_finish(nctx)
            for g in range(4):
                pool_bank(g, 0)
            norm_fin_a(nctx, 0)
            for i in range(8):
                if i + 1 < 8:
                    norm_fin_a(nctx, i + 1)
                if i < 4:
                    pool_bank(i, 1)
                norm_fin_b(nctx, i, hTB, i, B_hT[8 + i])
                if i >= 1:
                    u_tile(8 + i - 1)
            u_tile(NT - 1)
            for tb in (2, 3):
                for g in range(4):
                    pool_bank(g, tb)
            for m in order[1:RING]:
                issue_chunk(m)

            def proj_chunk(m, tb):
                slot = chunk_slots[m]
                pb = pn[0] % 4
                pn[0] += 1
                for k in range(8):
                    MM(psf[:, pb, :], ring[:, slot, k * 128:(k + 1) * 128], hTs[tb // 2][:, k, (tb % 2) * 512:(tb % 2 + 1) * 512],
                       k == 0, k == 7, reads=[B_ring[slot]] + B_hT[4 * tb:4 * tb + 4], writes=[B_ps[pb]], signal=(k == 7))
                return pb

            def after_chunk(m):
                i = order.index(m)
                if i + RING < len(order):
                    issue_chunk(order[i + RING])

            kq_n = [0]
            B_k = [None]

            kq_pend = []

            def kq_finish():
                m, tb, pb, s2 = kq_pend.pop(0)
                cs = slice(tb * 512, (tb + 1) * 512)
                pb2 = 4 + s2
                MM(psf[:, pb2, :], bones[:], sq[:, s2, :], True, True, reads=[B_cst["bones"], B_sq[s2]], writes=[B_ps[pb2]])
                ACT(rtmp[:, s2, :], psf[:, pb2, :], AF.Ln, reads=[B_ps[pb2], B_cst["eps"]], writes=[B_rt[s2]], bias=epsc[:, 0:1])
                ACT(rtmp[:, s2, :], rtmp[:, s2, :], AF.Exp, reads=[B_rt[s2]], writes=[B_rt[s2]], scale=-0.5)
                if m < 6:
                    for c in range(2):
                        pr = slice(64 * c, 64 * c + 64)
                        STT(kpad[pr, c, cs], psf[pr, pb, :], cols[pr, 1:2], rtmp[pr, s2, :], ALU.mult, ALU.mult,
                            reads=[B_ps[pb], B_rt[s2], B_cst["cols"]], writes=[B_k[0][c][tb]])
                else:
                    j = m - 6
                    STT(yT[:, j, cs], psf[:, pb, :], gqs[:, 0:1], rtmp[:, s2, :], ALU.mult, ALU.mult,
                        reads=[B_ps[pb], B_rt[s2], B_cst["gqs"]], writes=[B_y[j][4 * tb + t] for t in range(4)])

            def do_kq(m):
                for tb in range(4):
                    pb = proj_chunk(m, tb)
                    s2 = kq_n[0] % 2
                    kq_n[0] += 1
                    ACT(sq[:, s2, :], psf[:, pb, :], AF.Square, reads=[B_ps[pb]], writes=[B_sq[s2]], scale=0.125)
                    kq_pend.append((m, tb, pb, s2))
                    if len(kq_pend) > 1:
                        kq_finish()
                while kq_pend:
                    kq_finish()
                after_chunk(m)

            fenk = P.fence()
            B_k[0] = [[Buf(f"k{c}_{t}", fenk) for t in range(4)] for c in range(2)]
            P.op("pool", "memset", dict(ap=kpad[:].rearrange("p c t -> p (c t)"), constant=0.0),
                 writes=[B_k[0][c][t] for c in range(2) for t in range(4)])
            while pl_items:
                pool_linear_item(*pl_items.pop(0))
            for m in order[:5]:
                do_kq(m)
            while kq_pend:
                kq_finish()
            B_k = B_k[0]
            slot = chunk_slots[10]
            for t4 in range(4):
                pb = pn[0] % 4
                pn[0] += 1
                for ti in range(4):
                    tt = 4 * t4 + ti
                    for k in range(8):
                        MM(psf[:, pb, ti * 128:(ti + 1) * 128], hTs[tt // 8][:, k, (tt % 8) * 128:(tt % 8 + 1) * 128],
                           ring[:, slot, k * 128:(k + 1) * 128], k == 0, k == 7,
                           reads=[B_ring[slot], B_hT[tt]], writes=[B_ps[pb]], signal=(k == 7 and ti == 3))
                ACT(vaug[:, 4 * t4:4 * t4 + 4, :, 64:128], psf[:, pb, :].rearrange("p (t c n) -> p t c n", t=4, c=2), AF.Copy,
                    reads=[B_ps[pb]], writes=[B_v[t4]])
            DMA("pool", "dma_wout", woutb[:].rearrange("p k n -> p (k n)"), wout_d[:, :], writes=[B_wout] + B_hT[8:16])
            iters = [(n, c) for n in range(NT) for c in range(2)]
            B_yq = lambda n: [B_y[j][n] for j in range(4)]

            def att_logits(it):
                n, c = iters[it]
                s2 = it % 2
                js = [1] if n == 0 else [0, 1]
                qb = slice(n * 128, (n + 1) * 128)
                for j in js:
                    kt = n - 1 + j
                    pl = 2 * s2 + j
                    MM(psf[:, pl, :], ident[:], bm[:, c, j, :], True, False, reads=[B_cst["ident"], B_bm],
                       writes=[B_ps[pl]], signal=False)
                    MM(psf[:, pl, :], kpad[:, c, kt * 128:(kt + 1) * 128], yT[:, 0:4, qb], False, True,
                       reads=[B_k[c][kt // 4]] + B_yq(n), writes=[B_ps[pl]])
                    ACT(probs[:, s2, j, :], psf[:, pl, :], AF.Exp, reads=[B_ps[pl]], writes=[B_probs[s2]])

            def att_pv_norm(it):
                n, c = iters[it]
                s2 = it % 2
                sn = n % 2
                js = [1] if n == 0 else [0, 1]
                pD, pS = 4 + 2 * sn, 5 + 2 * sn
                c0 = 64 if c == 0 else 0
                for (pp, is_v) in ((pD, True), (pS, False)):
                    for ji, j in enumerate(js):
                        kt = n - 1 + j
                        if is_v:
                            lhsT, rd = vaug[:, kt, c, c0:c0 + 128], [B_v[kt // 4], B_probs[s2]]
                        else:
                            lhsT, rd = onz[:, c0:c0 + 128], [B_cst["onz"], B_probs[s2]]
                        first = (c == 0 and ji == 0)
                        last = (c == 1 and ji == len(js) - 1)
                        MM(psf[:, pp, :], lhsT, probs[:, s2, j, :], first, last, reads=rd, writes=[B_ps[pp]],
                           signal=(ji == len(js) - 1))
                if c == 1:
                    TT(trt[:, sn, :], psf[:, pS, :], eskf[:], ALU.add, reads=[B_ps[pS], B_eskf], writes=[B_trt[sn]])
                    ACT(trt[:, sn, :], trt[:, sn, :], AF.Ln, reads=[B_trt[sn]], writes=[B_trt[sn]])
                    ACT(trt[:, sn, :], trt[:, sn, :], AF.Exp, reads=[B_trt[sn]], writes=[B_trt[sn]], scale=-1.0)

            def att_scale(n):
                sn = n % 2
                pD = 4 + 2 * sn
                qb = slice(n * 128, (n + 1) * 128)
                TT(yT[:, 0:4, qb], psf[:, pD, :].rearrange("p (j q) -> p j q", j=4),
                   trt[:, sn, :].rearrange("p (j q) -> p j q", j=4), ALU.mult, reads=[B_ps[pD], B_trt[sn]], writes=B_yq(n))

            att_logits(0)
            for it in range(len(iters)):
                if it + 1 < len(iters):
                    att_logits(it + 1)
                att_pv_norm(it)
                n, c = iters[it]
                if c == 0 and n >= 1:
                    att_scale(n - 1)
            att_scale(NT - 1)
            on = 0
            if next_gain is not None:
                load_gain(next_gain)
            pend = []
            for tt in range(NT):
                if pend and not pend[-1][2]:
                    norm_fin_a(pend[-1][0], 0)
                    pend[-1][2] = True
                for dh in range(2):
                    po = on % 4
                    on += 1
                    cs = slice(dh * 512, (dh + 1) * 512)
                    for k in range(8):
                        MM(psf[:, po, :], yT[:, k, tt * 128:(tt + 1) * 128], woutb[:, k, cs], k == 0, k == 7,
                           reads=[B_y[k][tt], B_wout], writes=[B_ps[po]], signal=(k == 7))
                    TT(xs[:, tt, cs], psf[:, po, :], xs[:, tt, cs], ALU.add, reads=[B_ps[po], B_xs[tt][dh]], writes=[B_xs[tt][dh]])
                if len(pend) >= 2 or (tt >= 8 and pend):
                    ctx, t0, _ = pend.pop(0)
                    norm_fin_b(ctx, 0, hTA, t0, B_hTA[t0])
                if next_gain is not None and tt < 8:
                    pend.append([norm_pre([tt]), tt, False])

        P.op("dve", "memset", dict(ap=ss[:], constant=0.0), writes=[B_ss0])
        P.op("dve", "memset", dict(ap=epsc[:], constant=EPS), writes=[B_cst["eps"]])
        if stage == 1:
            ffn(0, final=False)
            tags = dump_xs()
        elif stage == 2:
            ffn(0, final=False, next_norm=(1, list(range(8))))
            mixer(first_half_done=True)
            tags = dump_xs()
        else:
            ffn(0, final=False, next_norm=(1, list(range(8))))
            mixer(first_half_done=True, next_gain=2)
            tags = ffn(1, final=True, first_norm_done=True)
        P.wait_all("sp", [max(tags, key=lambda t: t[1])])

        with nc.Block() as block:
            @block.tensor
            def _(e):
                P.replay("pe", e)

            @block.scalar
            def _(e):
                P.replay("act", e)

            @block.vector
            def _(e):
                P.replay("dve", e)

            @block.gpsimd
            def _(e):
                P.replay("pool", e)

            @block.sync
            def _(e):
                P.replay("sp", e)
    return nc


def _t5_bucket(dist):
    n = np.maximum(dist, 0)
    max_exact = 16
    large = max_exact + (np.log(np.maximum(n, 1) / max_exact) / np.log(128 / max_exact) * (32 - max_exact)).astype(np.int32)
    large = np.minimum(large, 31)
    return np.where(n < max_exact, n, large).astype(np.int32)


def prep_shared(inp):
    f32 = np.float32
    sh = {}
    for i, pre in ((1, "ffn1"), (2, "ffn2")):
        wg = np.asarray(inp[pre + "_w_gate"], f32)[0]
        wu = np.asarray(inp[pre + "_w_up"], f32)[0]
        wd = np.asarray(inp[pre + "_w_down"], f32)[0]
        g = wg.reshape(8, 128, NF, 128).transpose(2, 1, 0, 3).reshape(NF, 128, 1024)
        u = wu.reshape(8, 128, NF, 128).transpose(2, 1, 0, 3).reshape(NF, 128, 1024)
        sh[f"wgu{i}"] = np.ascontiguousarray(np.concatenate([g, u], axis=2))
        d = wd.reshape(11, 2, 128, 2, 512).transpose(3, 0, 2, 1, 4).reshape(2, 11, 128, 1024)
        sh[f"wd{i}"] = np.ascontiguousarray(d)
    sh["gains"] = np.ascontiguousarray(np.stack([np.asarray(inp["ffn1_norm"], f32)[0], np.asarray(inp["mix_norm"], f32)[0],
                                                 np.asarray(inp["ffn2_norm"], f32)[0]]))
    win = np.asarray(inp["w_in"], f32)[0]
    chunks = []
    for g in range(4):
        chunks.append(win[:, 768 + 128 * g:768 + 128 * (g + 1)])
    chunks.append(win[:, 512:640])
    chunks.append(win[:, 512:640])
    for j in range(4):
        chunks.append(np.concatenate([win[:, 64 * j:64 * (j + 1)], win[:, 64 * (4 + j):64 * (5 + j)]], axis=1))
    chunks.append(win[:, 640:768])
    wl = np.stack(chunks)
    sh["win"] = np.ascontiguousarray(wl.reshape(11, 8, 128, 128).transpose(0, 2, 1, 3).reshape(11, 128, 1024))
    wo = np.asarray(inp["w_out"], f32)[0]
    wo = np.concatenate([np.concatenate([wo[64 * j:64 * (j + 1)], wo[64 * (4 + j):64 * (5 + j)]], axis=0) for j in range(4)]
                        + [wo[512:]], axis=0)
    sh["wout"] = np.ascontiguousarray(wo.reshape(8, 128, D).transpose(1, 0, 2).reshape(128, 8 * D))
    pw = np.asarray(inp["pool_w"], f32)[0]
    sh["poolw"] = np.ascontiguousarray(pw.transpose(1, 0, 2).reshape(128, 512))
    cols = np.zeros((128, 16), f32)
    cols[:, 0] = np.tile(np.asarray(inp["q_norm"], f32)[0], 2)
    cols[:, 1] = np.tile(np.asarray(inp["k_norm"], f32)[0], 2)
    ps = np.asarray(inp["pool_scale"], f32)[0]
    for g in range(4):
        cols[:, 2 + g] = ps[128 * g:128 * (g + 1)]
    sinks = np.asarray(inp["attn_sinks"], f32)[0]
    for j in range(4):
        cols[0:64, 6 + j] = sinks[j]
        cols[64:128, 6 + j] = sinks[4 + j]
    sh["cols"] = cols
    invc = np.zeros((128, 4, 16), f32)
    for g in range(4):
        w = 2 ** (g + 1)
        invc[:, g, :] = (1.0 / np.minimum(np.arange(1, 17), w)).astype(f32)[None, :]
    sh["invc"] = invc.reshape(128, 64)
    rb = np.asarray(inp["rel_bias"], f32)
    rbx = np.concatenate([rb, np.full((1, 8), NEGM, f32)], axis=0)
    sl = np.arange(128)[:, None]
    q = np.arange(128)[None, :]
    bmh = np.zeros((128, 2, 2, 2, 2, 128), f32)
    for j in range(2):
        dist = q - sl + (128 if j == 0 else 0)
        valid = (dist >= 0) & (dist < 128)
        idx = np.where(valid, _t5_bucket(dist), 32)
        for c in range(2):
            for hj in range(4):
                bmh[:, c, j, hj // 2, hj % 2, :] = rbx[idx, 4 * c + hj]
    sh["bm"] = np.ascontiguousarray(bmh.reshape(128, 2048))
    sh["ident"] = np.eye(128, dtype=f32)
    bo = np.zeros((128, 128), f32)
    bo[0:64, 0:64] = 1.0
    bo[64:128, 64:128] = 1.0
    sh["bones"] = bo
    sh["wu"] = np.ascontiguousarray(win[:, 768:1280].reshape(8, 128, 512).transpose(1, 0, 2).reshape(128, 8 * 512))
    wb = np.zeros((128, 12, 128), f32)
    s_i = np.arange(128)[:, None]
    t_i = np.arange(128)[None, :]
    for g in range(4):
        w = 2 ** (g + 1)
        band = ((t_i - s_i >= 0) & (t_i - s_i < w)).astype(f32)
        eye = (s_i == t_i).astype(f32)
        wb[:, 3 * g + 0, :] = band / w - eye
        wb[:, 3 * g + 1, :] = ((t_i + 128 - s_i) < w).astype(f32) / w
        wb[:, 3 * g + 2, :] = band / np.minimum(t_i + 1, w).astype(f32) - eye
    sh["wband"] = np.ascontiguousarray(wb.reshape(128, 12 * 128))
    oz = np.zeros((128, 192), f32)
    oz[:, 64:128] = 1.0
    sh["onz"] = oz
    return sh


_CACHE = {}


def kernel(**inputs):
    stage = inputs.pop("_stage", 99)
    x = np.asarray(inputs["x"], np.float32)
    sh = prep_shared(inputs)
    if stage not in _CACHE:
        _CACHE[stage] = build(stage)
    nc = _CACHE[stage]
    in_maps = []
    for b in range(8):
        m = dict(sh)
        m["x"] = np.ascontiguousarray(x[b])
        in_maps.append(m)
    res = run_bass_kernel_spmd(nc, in_maps, core_ids=list(range(8)))
    return np.stack([res.results[b]["out"] for b in range(8)]).astype(np.float32)
```

```python
import numpy as np
from contextlib import ExitStack
import concourse.bass as bass
import concourse.mybir as mybir
from concourse.bass_utils import run_bass_kernel_spmd

F32 = mybir.dt.float32
BF16 = mybir.dt.bfloat16
AF = mybir.ActivationFunctionType
ALU = mybir.AluOpType

S = 2048
D = 1024
DFF = 2816
NF = DFF // 128
NT = S // 128
EPS = 1e-6
RING = 3
NEGM = -30000.0


class Buf:
    __slots__ = ("name", "w", "r")

    def __init__(self, name, fence=None):
        self.name = name
        self.w = None
        self.r = dict(fence) if fence else {}


class Prog:
    ENG = ("pe", "act", "dve", "pool", "sp")

    def __init__(self, nc, stack):
        self.nc = nc
        self.stack = stack
        self.q = {k: [] for k in self.ENG}
        self.sem = {}
        self.cnt = {}
        self.inc = {}
        self.waited = {}
        for k in ("pe", "act", "dve", "pool"):
            self.new_sem(k, 1)

    def new_sem(self, key, inc):
        self.sem[key] = self.stack.enter_context(self.nc.semaphore("s_" + key))
        self.cnt[key] = 0
        self.inc[key] = inc

    def fence(self):
        return {k: self.cnt[k] for k in ("pe", "act", "dve", "pool") if self.cnt[k] > 0}

    def op(self, eng, meth, kw, reads=(), writes=(), signal=True, dma=None):
        fn = (meth, kw)
        deps = {}

        def need(tag, war=False):
            key, val = tag
            if key == eng and (eng == "pe" or war):
                return
            if deps.get(key, 0) < val:
                deps[key] = val

        for b in reads:
            if b.w:
                need(b.w)
        for b in writes:
            if b.w:
                need(b.w)
            for k, v in b.r.items():
                need((k, v), war=True)
        for key, val in deps.items():
            if self.waited.get((eng, key), 0) >= val:
                continue
            self.waited[(eng, key)] = val
            self.q[eng].append(("wait", key, val))
        comp = dma if dma else eng
        if signal:
            self.cnt[comp] += self.inc[comp]
            tag = (comp, self.cnt[comp])
            self.q[eng].append(("op", fn, comp))
        else:
            tag = (comp, self.cnt[comp] + self.inc[comp])
            self.q[eng].append(("op", fn, None))
        for b in reads:
            if b.r.get(tag[0], 0) < tag[1]:
                b.r[tag[0]] = tag[1]
        for b in writes:
            b.w = tag
            b.r = {}
        return tag

    def wait_all(self, eng, tags):
        for key, val in tags:
            if self.waited.get((eng, key), 0) >= val:
                continue
            self.waited[(eng, key)] = val
            self.q[eng].append(("wait", key, val))

    def replay(self, eng, e):
        for item in self.q[eng]:
            if item[0] == "wait":
                e.wait_ge(self.sem[item[1]], item[2])
            else:
                ins = getattr(e, item[1][0])(**item[1][1])
                if item[2] is not None:
                    ins.then_inc(self.sem[item[2]], self.inc[item[2]])


def build(stage=99):
    nc = bass.Bass("TRN2", target_bir_lowering=False)

    def din(name, shape, dt=F32):
        return nc.dram_tensor(name, list(shape), dt, kind="ExternalInput").ap()

    x_d = din("x", [S, D])
    out_d = nc.dram_tensor("out", [S, D], F32, kind="ExternalOutput").ap()
    wgu_d = [din(f"wgu{i}", [NF, 128, 2048]) for i in (1, 2)]
    wd_d = [din(f"wd{i}", [2, 11, 128, 1024]) for i in (1, 2)]
    gains_d = din("gains", [3, D])
    win_d = din("win", [11, 128, 1024])
    wout_d = din("wout", [128, 8 * D])
    poolw_d = din("poolw", [128, 4 * 128])
    cols_d = din("cols", [128, 16])
    invc_d = din("invc", [128, 4 * 16])
    bm_d = din("bm", [128, 2 * 2 * 512])
    ident_d = din("ident", [128, 128])
    bones_d = din("bones", [128, 128])
    onz_d = din("onz", [128, 192])
    wu_d = din("wu", [128, 8 * 512])
    wband_d = din("wband", [128, 12 * 128])

    with ExitStack() as st:
        P = Prog(nc, st)

        def sb(name, shape, dt):
            return st.enter_context(nc.sbuf_tensor("sb_" + name, list(shape), dt))

        xs = sb("xs", [128, NT, D], F32)
        gbc = sb("gbc", [128, D], F32)
        hn = sb("hn", [128, 2, D], BF16)
        junk = sb("junk", [128, D], BF16)
        ident = sb("ident", [128, 128], BF16)
        bones = sb("bones", [128, 128], BF16)
        onz = sb("onz", [128, 192], BF16)
        wband = sb("wband", [128, 12, 128], BF16)
        cols = sb("cols", [128, 16], F32)
        gqs = sb("gqs", [128, 1], F32)
        epsc = sb("epsc", [128, 1], F32)
        invc = sb("invc", [128, 4, 16], F32)
        ss = sb("ss", [128, 48], F32)
        ring = sb("ring", [128, RING, 2048], BF16)
        NA = 115 * 512
        arena = sb("arena", [128, NA], BF16)
        psf = st.enter_context(nc.psum_tensor("psf", [128, 8, 512], F32))

        def av(off, n):
            return arena[:, off:off + n]

        o = 0
        hTA = av(o, 8 * 1024).rearrange("p (k t) -> p k t", k=8); o += 8 * 1024
        hT_f = hTA
        actT = av(o, NF * 1024).rearrange("p (f t) -> p f t", f=NF); o += NF * 1024
        wdb = av(o, 2 * NF * 512).rearrange("p (h f n) -> p h f n", h=2, f=NF); o += 2 * NF * 512
        sil = av(o, 2 * 1024).bitcast(F32).rearrange("p (s n) -> p s n", s=2); o += 2 * 1024
        assert o <= NA
        o = 0
        hTB = av(o + 8 * 1024, 8 * 1024).rearrange("p (k t) -> p k t", k=8)
        hTs = [hTA, hTB]
        woutb = av(o + 8 * 1024, 8 * D).rearrange("p (k n) -> p k n", k=8); o += 8 * S
        yT = av(o, 8 * S).rearrange("p (k t) -> p k t", k=8); o += 8 * S
        trt = av(o, 2 * 1024).bitcast(F32).rearrange("p (s n) -> p s n", s=2)
        eskf = av(o + 2048, 1024).bitcast(F32); o += 2 * S
        vaug = av(o, NT * 2 * 192).rearrange("p (t c n) -> p t c n", t=NT, c=2); o += NT * 2 * 192
        sq = av(o, 1024).rearrange("p (s n) -> p s n", s=2); o += 1024
        poolw = av(o, 512).rearrange("p (g n) -> p g n", g=4); o += 512
        rtmp = av(o, 2 * 1024).bitcast(F32).rearrange("p (s n) -> p s n", s=2); o += 2 * 1024
        wub = ring[:, 0:2, :].rearrange("p s (k n) -> p (s k) n", k=4)
        utok = av(o + 4096, 8192).rearrange("p (t n) -> p t n", t=NT)
        bm = av(o, 2048).rearrange("p (c j n) -> p c j n", c=2, j=2)
        probs = av(o + 2048, 2048).rearrange("p (s j n) -> p s j n", s=2, j=2)
        kpad = av(o + 4096, 2 * S).rearrange("p (c t) -> p c t", c=2)
        o += 3 * 4096
        assert o <= NA, o

        B_xs = [[Buf(f"xs{t}_{h}") for h in range(2)] for t in range(NT)]
        B_gbc = Buf("gbc")
        B_hn = [Buf("hn0"), Buf("hn1")]
        B_hTA = [Buf(f"hTA{i}") for i in range(8)]
        B_ss0 = Buf("ss0")
        B_cst = {k: Buf(k) for k in ("ident", "bones", "cols", "invc", "esk", "gqs", "eps", "onz", "wband")}
        B_ring = [Buf(f"ring{i}") for i in range(RING)]
        B_ps = [Buf(f"ps{i}") for i in range(8)]
        ring_n = [0]
        norm_n = [0]

        def dma_sem(key):
            if key not in P.sem:
                P.new_sem(key, 16)
            return key

        def DMA(eng, key, out, in_, reads=(), writes=()):
            return P.op(eng, "dma_start", dict(out=out, in_=in_), reads=reads, writes=writes, dma=dma_sem(key))

        def ACT(out, in_, func, reads, writes, **kw):
            return P.op("act", "activation", dict(out=out, in_=in_, func=func, **kw), reads=reads, writes=writes)

        def MM(out, lhsT, rhs, start, stop, reads, writes, signal=True):
            return P.op("pe", "matmul", dict(out=out, lhsT=lhsT, rhs=rhs, start=start, stop=stop),
                        reads=reads, writes=writes, signal=signal)

        def TS(out, in0, s1, s2, op0, op1, reads, writes, eng="dve"):
            kw = dict(out=out, in0=in0, scalar1=s1, scalar2=s2, op0=op0)
            if op1 is not None:
                kw["op1"] = op1
            return P.op(eng, "tensor_scalar", kw, reads=reads, writes=writes)

        def STT(out, in0, scalar, in1, op0, op1, reads, writes):
            return P.op("dve", "scalar_tensor_tensor", dict(out=out, in0=in0, scalar=scalar, in1=in1, op0=op0, op1=op1),
                        reads=reads, writes=writes)

        def TT(out, in0, in1, op, reads, writes, eng="dve"):
            return P.op(eng, "tensor_tensor", dict(out=out, in0=in0, in1=in1, op=op), reads=reads, writes=writes)

        def CP(out, in_, reads, writes, eng="dve"):
            return P.op(eng, "tensor_copy", dict(out=out, in_=in_), reads=reads, writes=writes)

        def load_gain(i):
            DMA("sp", "dma_gbc", gbc[:], gains_d[i:i + 1, :].broadcast_to([128, D]), writes=[B_gbc])

        def load_x(t0, nt, key, extra_reads=()):
            DMA("sp", key, xs[:, t0:t0 + nt, :], x_d[128 * t0:128 * (t0 + nt), :].rearrange("(t p) d -> p t d", p=128),
                reads=list(extra_reads), writes=[B_xs[t][h] for t in range(t0, t0 + nt) for h in range(2)])

        load_x(0, 1, "dma_x0")
        load_x(1, 1, "dma_x1")
        DMA("pool", "dma_ident", ident[:], ident_d[:, :], writes=[B_cst["ident"]])

        def late_consts():
            DMA("pool", "dma_bones", bones[:], bones_d[:, :], writes=[B_cst["bones"]])
            DMA("pool", "dma_onz", onz[:], onz_d[:, :], writes=[B_cst["onz"]])
            DMA("pool", "dma_wband", wband[:].rearrange("p m n -> p (m n)"), wband_d[:, :], writes=[B_cst["wband"]])
        load_gain(0)
        load_x(2, 2, "dma_x2")
        load_x(4, 4, "dma_x3")
        DMA("sp", "dma_cols", cols[:], cols_d[:, :], writes=[B_cst["cols"]])
        DMA("sp", "dma_invc", invc[:].rearrange("p g n -> p (g n)"), invc_d[:, :], writes=[B_cst["invc"]])

        def norm_pre_begin(tiles):
            n = len(tiles)
            base = norm_n[0]
            norm_n[0] += n
            B_ss = Buf(f"ss{base}")
            B_ss.w = B_ss0.w
            return dict(base=base, tiles=tiles, B_ss=B_ss)

        def norm_pre_square(ctx, i):
            tt = ctx["tiles"][i]
            base = ctx["base"]
            ACT(junk[:], xs[:, tt, :], AF.Square, reads=B_xs[tt], writes=[ctx["B_ss"]],
                scale=1.0 / 32.0, accum_out=ss[:, base + i:base + i + 1])

        def norm_pre_finish(ctx):
            base, n, B_ss = ctx["base"], len(ctx["tiles"]), ctx["B_ss"]
            scs = ss[:, base:base + n]
            ACT(scs, scs, AF.Ln, reads=[B_ss, B_cst["eps"]], writes=[B_ss], bias=epsc[:, 0:1])
            ACT(scs, scs, AF.Exp, reads=[B_ss], writes=[B_ss], scale=-0.5)

        def norm_pre(tiles):
            ctx = norm_pre_begin(tiles)
            for i in range(len(tiles)):
                norm_pre_square(ctx, i)
            norm_pre_finish(ctx)
            return ctx

        def norm_fin_a(ctx, i):
            tt = ctx["tiles"][i]
            slot = ctx["base"] + i
            B_ss = ctx["B_ss"]
            hs = slot % 2
            STT(hn[:, hs, :], xs[:, tt, :], ss[:, slot:slot + 1], gbc[:], ALU.mult, ALU.mult,
                reads=B_xs[tt] + [B_ss, B_gbc], writes=[B_hn[hs]])

        def norm_fin_b(ctx, i, hT, lt, B_hT_tile):
            slot = ctx["base"] + i
            hs = slot % 2
            pb = 6 + (slot % 2)
            pT = psf[:, pb, :].bitcast(BF16).rearrange("p (k t) -> p k t", k=8)
            for k in range(8):
                P.op("pe", "transpose", dict(out=pT[:, k, :], in_=hn[:, hs, k * 128:(k + 1) * 128], identity=ident[:]),
                     reads=[B_hn[hs], B_cst["ident"]], writes=[B_ps[pb]], signal=(k == 7))
            if slot % 2 == 0:
                ACT(hT[:, :, lt * 128:(lt + 1) * 128], pT, AF.Copy, reads=[B_ps[pb]], writes=[B_hT_tile])
            else:
                CP(hT[:, :, lt * 128:(lt + 1) * 128], pT, reads=[B_ps[pb]], writes=[B_hT_tile])

        def norm_group(tiles, hT, lts, B_hT_tiles):
            ctx = norm_pre(tiles)
            for i in range(len(tiles)):
                norm_fin_a(ctx, i)
                norm_fin_b(ctx, i, hT, lts[i], B_hT_tiles[i])

        def ffn(idx, final, first_norm_done=False, next_norm=None):
            fen = P.fence()
            B_hT = B_hTA
            B_act = [[Buf(f"act{f}_{h}", fen) for h in range(2)] for f in range(NF)]
            B_wd = [[Buf(f"wd{h}_{i}", fen) for i in range(11)] for h in range(2)]
            B_sil = [Buf("sil0", fen), Buf("sil1", fen)]
            wgu = wgu_d[idx]
            wd = wd_d[idx]
            if not first_norm_done and idx != 0:
                load_gain(2)
            out_tags = []
            setn = 0
            on = 0

            def issue_ring(f):
                slot = ring_n[0] % RING
                extra = []
                if idx == 0 and ring_n[0] == 0:
                    extra = [B_xs[3][0]]
                elif idx == 0 and ring_n[0] in (1, 2):
                    extra = [B_xs[7][0]]
                ring_n[0] += 1
                DMA("pool", f"dma_ring{slot}", ring[:, slot, :], wgu[f], reads=extra, writes=[B_ring[slot]])
                return slot

            def issue_wd(i):
                h, pi = divmod(i, 11)
                DMA("pool", f"dma_wd{h}_{pi}", wdb[:, h, 2 * pi:2 * pi + 2, :],
                    wd[h, pi].rearrange("p (f n) -> p f n", f=2), writes=[B_wd[h][pi]])

            nctx = None
            for sbk in range(2):
                slots = {}
                for f in range(RING):
                    slots[f] = issue_ring(f)
                if idx == 0 and sbk == 0:
                    late_consts()
                wd_next = 0
                if sbk == 0 and not first_norm_done:
                    ctxs = [None] * 8
                    for lt in range(9):
                        if lt < 8:
                            ctxs[lt] = norm_pre([lt])
                        if lt >= 1:
                            norm_fin_a(ctxs[lt - 1], 0)
                            norm_fin_b(ctxs[lt - 1], 0, hTA, lt - 1, B_hT[lt - 1])
                    if idx == 0:
                        load_x(8, 4, "dma_x4", extra_reads=[B_hTA[7]])
                        load_x(12, 4, "dma_x5")
                for f in range(NF):
                    slot = slots[f]
                    for half in range(2):
                        pg, pu = 2 * (setn % 2), 2 * (setn % 2) + 1
                        s_ = setn % 2
                        setn += 1
                        hts = B_hT[4 * half:4 * half + 4]
                        rhs_cols = slice(half * 512, (half + 1) * 512)
                        for k in range(8):
                            MM(psf[:, pg, :], ring[:, slot, k * 128:(k + 1) * 128], hT_f[:, k, rhs_cols],
                               k == 0, k == 7, reads=[B_ring[slot]] + hts, writes=[B_ps[pg]], signal=(k == 7))
                        for k in range(8):
                            MM(psf[:, pu, :], ring[:, slot, 1024 + k * 128:1024 + (k + 1) * 128], hT_f[:, k, rhs_cols],
                               k == 0, k == 7, reads=[B_ring[slot]] + hts, writes=[B_ps[pu]], signal=(k == 7))
                        ACT(sil[:, s_, :], psf[:, pg, :], AF.Silu, reads=[B_ps[pg]], writes=[B_sil[s_]])
                        TT(actT[:, f, rhs_cols], psf[:, pu, :], sil[:, s_, :], ALU.mult,
                           reads=[B_ps[pu], B_sil[s_]], writes=[B_act[f][half]])
                    if f + RING < NF:
                        slots[f + RING] = issue_ring(f + RING)
                    if f == NF - 11:
                        if sbk == 0:
                            nctx = norm_pre_begin([8 + lt for lt in range(8)])
                        elif next_norm is not None:
                            load_gain(next_norm[0])
                            nctx = norm_pre_begin(next_norm[1])
                        else:
                            nctx = None
                    if nctx is not None and NF - 10 <= f < NF - 2:
                        norm_pre_square(nctx, f - (NF - 10))
                    if nctx is not None and f == NF - 2:
                        norm_pre_finish(nctx)
                    if f >= 1 and wd_next < 22:
                        issue_wd(wd_next)
                        wd_next += 1
                        if wd_next < 22 and f >= 20:
                            issue_wd(wd_next)
                            wd_next += 1
                while wd_next < 22:
                    issue_wd(wd_next)
                    wd_next += 1
                for lt in range(8):
                    tt = sbk * 8 + lt
                    if nctx is not None:
                        norm_fin_a(nctx, lt)
                    for dh in range(2):
                        po = 4 + (on % 2)
                        on += 1
                        cs = slice(dh * 512, (dh + 1) * 512)
                        for f in range(NF):
                            MM(psf[:, po, :], actT[:, f, lt * 128:(lt + 1) * 128], wdb[:, dh, f, :], f == 0, f == NF - 1,
                               reads=[B_act[f][lt // 4], B_wd[dh][f // 2]], writes=[B_ps[po]], signal=(f == NF - 1))
                        STT(xs[:, tt, cs], psf[:, po, :], 0.5, xs[:, tt, cs], ALU.mult, ALU.add,
                            reads=[B_ps[po], B_xs[tt][dh]], writes=[B_xs[tt][dh]])
                        if final:
                            out_tags.append(DMA("sp", "dma_out", out_d[tt * 128:(tt + 1) * 128, cs], xs[:, tt, cs],
                                                reads=[B_xs[tt][dh]]))
                    if nctx is not None:
                        norm_fin_b(nctx, lt, hTA, lt, B_hTA[lt])
            return out_tags

        def dump_xs():
            tags = []
            for tt in range(NT):
                tags.append(DMA("sp", "dma_out", out_d[tt * 128:(tt + 1) * 128, :], xs[:, tt, :], reads=B_xs[tt]))
            return tags

        def mixer(sub=9, first_half_done=False, next_gain=None):
            fen = P.fence()
            B_hT = B_hTA + [Buf(f"hTB{i}", fen) for i in range(8)]
            B_y = [[Buf(f"y{k}_{t}", fen) for t in range(NT)] for k in range(8)]
            B_v = [Buf(f"v{t}", fen) for t in range(4)]
            B_pw = Buf("pw", fen)
            B_sq = [Buf("sq0", fen), Buf("sq1", fen)]
            B_rt = [Buf("rt0", fen), Buf("rt1", fen)]
            B_wout = Buf("wout", fen)
            if not first_half_done:
                load_gain(1)
            TS(gqs[:], cols[:, 0:1], 0.125, None, ALU.mult, None, reads=[B_cst["cols"]], writes=[B_cst["gqs"]])
            chunk_slots = {}

            def issue_chunk(m):
                slot = ring_n[0] % RING
                ring_n[0] += 1
                DMA("pool", f"dma_ring{slot}", ring[:, slot, 0:1024], win_d[m], writes=[B_ring[slot]])
                chunk_slots[m] = slot

            order = [6, 7, 8, 9, 4, 10]
            B_ut = [Buf(f"ut{t}", fen) for t in range(NT)]
            DMA("pool", "dma_ring0", ring[:, 0:2, :].rearrange("p s n -> p (s n)"), wu_d[:, :], writes=[B_ring[0], B_ring[1]])
            B_ring[1].w = B_ring[0].w
            ring_n[0] = (ring_n[0] // RING + 1) * RING + 2
            issue_chunk(order[0])
            DMA("pool", "dma_pw", poolw[:].rearrange("p g n -> p (g n)"), poolw_d[:, :], writes=[B_pw])
            B_bm = Buf("bm", fen)
            B_probs = [Buf("pr0", fen), Buf("pr1", fen)]
            B_trt = [Buf(f"trt{i}", fen) for i in range(2)]
            B_eskf = Buf("eskf", fen)
            DMA("pool", "dma_bm", bm[:].rearrange("p c j n -> p (c j n)"), bm_d[:, :], writes=[B_bm])
            P.op("dve", "memset", dict(ap=eskf[:], constant=0.0), writes=[B_eskf])
            for j in range(4):
                cc = slice(j * 128, (j + 1) * 128)
                ACT(eskf[:, cc], eskf[:, cc], AF.Exp, reads=[B_eskf, B_cst["cols"]], writes=[B_eskf], bias=cols[:, 6 + j:7 + j])
            P.op("pool", "memset", dict(ap=vaug[:].rearrange("p t c n -> p (t c n)"), constant=0.0), writes=B_v)
            pn = [0]
            evn = [0]

            def evac(out, in_, reads, writes, scale=None):
                evn[0] += 1
                if evn[0] % 2 == 0:
                    if scale is None:
                        ACT(out, in_, AF.Copy, reads=reads, writes=writes)
                    else:
                        ACT(out, in_, AF.Copy, reads=reads, writes=writes, scale=scale)
                else:
                    if scale is None:
                        CP(out, in_, reads=reads, writes=writes)
                    else:
                        TS(out, in_, scale, None, ALU.mult, None, reads=reads, writes=writes)

            def u_tile(tt):
                pb = pn[0] % 4
                pn[0] += 1
                for k in range(8):
                    MM(psf[:, pb, :], hTs[tt // 8][:, k, (tt % 8) * 128:(tt % 8 + 1) * 128], wub[:, k, :], k == 0, k == 7,
                       reads=[B_ring[0], B_ring[1], B_hT[tt]], writes=[B_ps[pb]], signal=(k == 7))
                evac(utok[:, tt, :], psf[:, pb, :], reads=[B_ps[pb]], writes=[B_ut[tt]])

            def pool_bank(g, tb):
                pb = pn[0] % 4
                pn[0] += 1
                for ti in range(4):
                    T = 4 * tb + ti
                    dst = psf[:, pb, ti * 128:(ti + 1) * 128]
                    gs = slice(g * 128, (g + 1) * 128)
                    if T == 0:
                        MM(dst, utok[:, 0, gs], wband[:, 3 * g + 2, :], True, True, reads=[B_ut[0], B_cst["wband"]],
                           writes=[B_ps[pb]], signal=False)
                    else:
                        MM(dst, utok[:, T - 1, gs], wband[:, 3 * g + 1, :], True, False, reads=[B_ut[T - 1], B_cst["wband"]],
                           writes=[B_ps[pb]], signal=False)
                        MM(dst, utok[:, T, gs], wband[:, 3 * g + 0, :], False, True, reads=[B_ut[T], B_cst["wband"]],
                           writes=[B_ps[pb]], signal=(ti == 3))
                evac(yT[:, 4 + g, tb * 512:(tb + 1) * 512], psf[:, pb, :], reads=[B_ps[pb]],
                     writes=[B_y[4 + g][4 * tb + t] for t in range(4)])

            def pool_linear_item(g, tb):
                pb = pn[0] % 4
                pn[0] += 1
                cs = slice(tb * 512, (tb + 1) * 512)
                yb = [B_y[4 + g][4 * tb + t] for t in range(4)]
                MM(psf[:, pb, :], poolw[:, g, :], yT[:, 4 + g, cs], True, True, reads=[B_pw] + yb, writes=[B_ps[pb]])
                evac(yT[:, 4 + g, cs], psf[:, pb, :], reads=[B_ps[pb], B_cst["cols"]], writes=yb, scale=cols[:, 2 + g:3 + g])

            pl_items = [(g, tb) for tb in range(4) for g in range(4)]

            if not first_half_done:
                norm_group(list(range(8)), hTA, list(range(8)), B_hT[0:8])
            nctx = norm_pre_begin(list(range(8, 16)))
            for i in range(8):
                u_tile(i)
                norm_pre_square(nctx, i)
            norm_pre_finish(nctx)
            for g in range(4):
                pool_bank(g, 0)
            norm_fin_a(nctx, 0)
            for i in range(8):
                if i + 1 < 8:
                    norm_fin_a(nctx, i + 1)
                if i < 4:
                    pool_bank(i, 1)
                norm_fin_b(nctx, i, hTB, i, B_hT[8 + i])
                if i >= 1:
                    u_tile(8 + i - 1)
            u_tile(NT - 1)
            for tb in (2, 3):
                for g in range(4):
                    pool_bank(g, tb)
            for m in order[1:RING]:
                issue_chunk(m)

            def proj_chunk(m, tb):
                slot = chunk_slots[m]
                pb = pn[0] % 4
                pn[0] += 1
                for k in range(8):
                    MM(psf[:, pb, :], ring[:, slot, k * 128:(k + 1) * 128], hTs[tb // 2][:, k, (tb % 2) * 512:(tb % 2 + 1) * 512],
                       k == 0, k == 7, reads=[B_ring[slot]] + B_hT[4 * tb:4 * tb + 4], writes=[B_ps[pb]], signal=(k == 7))
                return pb

            def after_chunk(m):
                i = order.index(m)
                if i + RING < len(order):
                    issue_chunk(order[i + RING])

            kq_n = [0]
            B_k = [None]

            kq_pend = []

            def kq_finish():
                m, tb, pb, s2 = kq_pend.pop(0)
                cs = slice(tb * 512, (tb + 1) * 512)
                pb2 = 4 + s2
                MM(psf[:, pb2, :], bones[:], sq[:, s2, :], True, True, reads=[B_cst["bones"], B_sq[s2]], writes=[B_ps[pb2]])
                ACT(rtmp[:, s2, :], psf[:, pb2, :], AF.Ln, reads=[B_ps[pb2], B_cst["eps"]], writes=[B_rt[s2]], bias=epsc[:, 0:1])
                ACT(rtmp[:, s2, :], rtmp[:, s2, :], AF.Exp, reads=[B_rt[s2]], writes=[B_rt[s2]], scale=-0.5)
                if m < 6:
                    for c in range(2):
                        pr = slice(64 * c, 64 * c + 64)
                        STT(kpad[pr, c, cs], psf[pr, pb, :], cols[pr, 1:2], rtmp[pr, s2, :], ALU.mult, ALU.mult,
                            reads=[B_ps[pb], B_rt[s2], B_cst["cols"]], writes=[B_k[0][c][tb]])
                else:
                    j = m - 6
                    STT(yT[:, j, cs], psf[:, pb, :], gqs[:, 0:1], rtmp[:, s2, :], ALU.mult, ALU.mult,
                        reads=[B_ps[pb], B_rt[s2], B_cst["gqs"]], writes=[B_y[j][4 * tb + t] for t in range(4)])

            def do_kq(m):
                for tb in range(4):
                    pb = proj_chunk(m, tb)
                    s2 = kq_n[0] % 2
                    kq_n[0] += 1
                    ACT(sq[:, s2, :], psf[:, pb, :], AF.Square, reads=[B_ps[pb]], writes=[B_sq[s2]], scale=0.125)
                    kq_pend.append((m, tb, pb, s2))
                    if len(kq_pend) > 1:
                        kq_finish()
                while kq_pend:
                    kq_finish()
                after_chunk(m)

            fenk = P.fence()
            B_k[0] = [[Buf(f"k{c}_{t}", fenk) for t in range(4)] for c in range(2)]
            P.op("pool", "memset", dict(ap=kpad[:].rearrange("p c t -> p (c t)"), constant=0.0),
                 writes=[B_k[0][c][t] for c in range(2) for t in range(4)])
            while pl_items:
                pool_linear_item(*pl_items.pop(0))
            for m in order[:5]:
                do_kq(m)
            while kq_pend:
                kq_finish()
            B_k = B_k[0]
            slot = chunk_slots[10]
            for t4 in range(4):
                pb = pn[0] % 4
                pn[0] += 1
                for ti in range(4):
                    tt = 4 * t4 + ti
                    for k in range(8):
                        MM(psf[:, pb, ti * 128:(ti + 1) * 128], hTs[tt // 8][:, k, (tt % 8) * 128:(tt % 8 + 1) * 128],
                           ring[:, slot, k * 128:(k + 1) * 128], k == 0, k == 7,
                           reads=[B_ring[slot], B_hT[tt]], writes=[B_ps[pb]], signal=(k == 7 and ti == 3))
                ACT(vaug[:, 4 * t4:4 * t4 + 4, :, 64:128], psf[:, pb, :].rearrange("p (t c n) -> p t c n", t=4, c=2), AF.Copy,
                    reads=[B_ps[pb]], writes=[B_v[t4]])
            DMA("pool", "dma_wout", woutb[:].rearrange("p k n -> p (k n)"), wout_d[:, :], writes=[B_wout] + B_hT[8:16])
            iters = [(n, c) for n in range(NT) for c in range(2)]
            B_yq = lambda n: [B_y[j][n] for j in range(4)]

            def att_logits(it):
                n, c = iters[it]
                s2 = it % 2
                js = [1] if n == 0 else [0, 1]
                qb = slice(n * 128, (n + 1) * 128)
                for j in js:
                    kt = n - 1 + j
                    pl = 2 * s2 + j
                    MM(psf[:, pl, :], ident[:], bm[:, c, j, :], True, False, reads=[B_cst["ident"], B_bm],
                       writes=[B_ps[pl]], signal=False)
                    MM(psf[:, pl, :], kpad[:, c, kt * 128:(kt + 1) * 128], yT[:, 0:4, qb], False, True,
                       reads=[B_k[c][kt // 4]] + B_yq(n), writes=[B_ps[pl]])
                    ACT(probs[:, s2, j, :], psf[:, pl, :], AF.Exp, reads=[B_ps[pl]], writes=[B_probs[s2]])

            def att_pv_norm(it):
                n, c = iters[it]
                s2 = it % 2
                sn = n % 2
                js = [1] if n == 0 else [0, 1]
                pD, pS = 4 + 2 * sn, 5 + 2 * sn
                c0 = 64 if c == 0 else 0
                for (pp, is_v) in ((pD, True), (pS, False)):
                    for ji, j in enumerate(js):
                        kt = n - 1 + j
                        if is_v:
                            lhsT, rd = vaug[:, kt, c, c0:c0 + 128], [B_v[kt // 4], B_probs[s2]]
                        else:
                            lhsT, rd = onz[:, c0:c0 + 128], [B_cst["onz"], B_probs[s2]]
                        first = (c == 0 and ji == 0)
                        last = (c == 1 and ji == len(js) - 1)
                        MM(psf[:, pp, :], lhsT, probs[:, s2, j, :], first, last, reads=rd, writes=[B_ps[pp]],
                           signal=(ji == len(js) - 1))

            def att_norm(n):
                sn = n % 2
                pS = 5 + 2 * sn
                TT(trt[:, sn, :], psf[:, pS, :], eskf[:], ALU.add, reads=[B_ps[pS], B_eskf], writes=[B_trt[sn]])
                ACT(trt[:, sn, :], trt[:, sn, :], AF.Ln, reads=[B_trt[sn]], writes=[B_trt[sn]])
                ACT(trt[:, sn, :], trt[:, sn, :], AF.Exp, reads=[B_trt[sn]], writes=[B_trt[sn]], scale=-1.0)

            def att_scale(n):
                sn = n % 2
                pD = 4 + 2 * sn
                qb = slice(n * 128, (n + 1) * 128)
                TT(yT[:, 0:4, qb], psf[:, pD, :].rearrange("p (j q) -> p j q", j=4),
                   trt[:, sn, :].rearrange("p (j q) -> p j q", j=4), ALU.mult, reads=[B_ps[pD], B_trt[sn]], writes=B_yq(n))

            att_logits(0)
            for it in range(len(iters)):
                if it + 1 < len(iters):
                    att_logits(it + 1)
                att_pv_norm(it)
                n, c = iters[it]
                if c == 0 and n >= 1:
                    att_norm(n - 1)
                if c == 1 and n >= 1:
                    att_scale(n - 1)
            att_norm(NT - 1)
            att_scale(NT - 1)
            on = 0
            if next_gain is not None:
                load_gain(next_gain)
            pend = []
            for tt in range(NT):
                if pend and not pend[-1][2]:
                    norm_fin_a(pend[-1][0], 0)
                    pend[-1][2] = True
                for dh in range(2):
                    po = on % 4
                    on += 1
                    cs = slice(dh * 512, (dh + 1) * 512)
                    for k in range(8):
                        MM(psf[:, po, :], yT[:, k, tt * 128:(tt + 1) * 128], woutb[:, k, cs], k == 0, k == 7,
                           reads=[B_y[k][tt], B_wout], writes=[B_ps[po]], signal=(k == 7))
                    TT(xs[:, tt, cs], psf[:, po, :], xs[:, tt, cs], ALU.add, reads=[B_ps[po], B_xs[tt][dh]], writes=[B_xs[tt][dh]])
                if len(pend) >= 2 or (tt >= 8 and pend):
                    ctx, t0, _ = pend.pop(0)
                    norm_fin_b(ctx, 0, hTA, t0, B_hTA[t0])
                if next_gain is not None and tt < 8:
                    pend.append([norm_pre([tt]), tt, False])

        P.op("dve", "memset", dict(ap=ss[:], constant=0.0), writes=[B_ss0])
        P.op("dve", "memset", dict(ap=epsc[:], constant=EPS), writes=[B_cst["eps"]])
        if stage == 1:
            ffn(0, final=False)
            tags = dump_xs()
        elif stage == 2:
            ffn(0, final=False, next_norm=(1, list(range(8))))
            mixer(first_half_done=True)
            tags = dump_xs()
        else:
            ffn(0, final=False, next_norm=(1, list(range(8))))
            mixer(first_half_done=True, next_gain=2)
            tags = ffn(1, final=True, first_norm_done=True)
        P.wait_all("sp", [max(tags, key=lambda t: t[1])])

        with nc.Block() as block:
            @block.tensor
            def _(e):
                P.replay("pe", e)

            @block.scalar
            def _(e):
                P.replay("act", e)

            @block.vector
            def _(e):
                P.replay("dve", e)

            @block.gpsimd
            def _(e):
                P.replay("pool", e)

            @block.sync
            def _(e):
                P.replay("sp", e)
    return nc


def _t5_bucket(dist):
    n = np.maximum(dist, 0)
    max_exact = 16
    large = max_exact + (np.log(np.maximum(n, 1) / max_exact) / np.log(128 / max_exact) * (32 - max_exact)).astype(np.int32)
    large = np.minimum(large, 31)
    return np.where(n < max_exact, n, large).astype(np.int32)


def prep_shared(inp):
    f32 = np.float32
    sh = {}
    for i, pre in ((1, "ffn1"), (2, "ffn2")):
        wg = np.asarray(inp[pre + "_w_gate"], f32)[0]
        wu = np.asarray(inp[pre + "_w_up"], f32)[0]
        wd = np.asarray(inp[pre + "_w_down"], f32)[0]
        g = wg.reshape(8, 128, NF, 128).transpose(2, 1, 0, 3).reshape(NF, 128, 1024)
        u = wu.reshape(8, 128, NF, 128).transpose(2, 1, 0, 3).reshape(NF, 128, 1024)
        sh[f"wgu{i}"] = np.ascontiguousarray(np.concatenate([g, u], axis=2))
        d = wd.reshape(11, 2, 128, 2, 512).transpose(3, 0, 2, 1, 4).reshape(2, 11, 128, 1024)
        sh[f"wd{i}"] = np.ascontiguousarray(d)
    sh["gains"] = np.ascontiguousarray(np.stack([np.asarray(inp["ffn1_norm"], f32)[0], np.asarray(inp["mix_norm"], f32)[0],
                                                 np.asarray(inp["ffn2_norm"], f32)[0]]))
    win = np.asarray(inp["w_in"], f32)[0]
    chunks = []
    for g in range(4):
        chunks.append(win[:, 768 + 128 * g:768 + 128 * (g + 1)])
    chunks.append(win[:, 512:640])
    chunks.append(win[:, 512:640])
    for j in range(4):
        chunks.append(np.concatenate([win[:, 64 * j:64 * (j + 1)], win[:, 64 * (4 + j):64 * (5 + j)]], axis=1))
    chunks.append(win[:, 640:768])
    wl = np.stack(chunks)
    sh["win"] = np.ascontiguousarray(wl.reshape(11, 8, 128, 128).transpose(0, 2, 1, 3).reshape(11, 128, 1024))
    wo = np.asarray(inp["w_out"], f32)[0]
    wo = np.concatenate([np.concatenate([wo[64 * j:64 * (j + 1)], wo[64 * (4 + j):64 * (5 + j)]], axis=0) for j in range(4)]
                        + [wo[512:]], axis=0)
    sh["wout"] = np.ascontiguousarray(wo.reshape(8, 128, D).transpose(1, 0, 2).reshape(128, 8 * D))
    pw = np.asarray(inp["pool_w"], f32)[0]
    sh["poolw"] = np.ascontiguousarray(pw.transpose(1, 0, 2).reshape(128, 512))
    cols = np.zeros((128, 16), f32)
    cols[:, 0] = np.tile(np.asarray(inp["q_norm"], f32)[0], 2)
    cols[:, 1] = np.tile(np.asarray(inp["k_norm"], f32)[0], 2)
    ps = np.asarray(inp["pool_scale"], f32)[0]
    for g in range(4):
        cols[:, 2 + g] = ps[128 * g:128 * (g + 1)]
    sinks = np.asarray(inp["attn_sinks"], f32)[0]
    for j in range(4):
        cols[0:64, 6 + j] = sinks[j]
        cols[64:128, 6 + j] = sinks[4 + j]
    sh["cols"] = cols
    invc = np.zeros((128, 4, 16), f32)
    for g in range(4):
        w = 2 ** (g + 1)
        invc[:, g, :] = (1.0 / np.minimum(np.arange(1, 17), w)).astype(f32)[None, :]
    sh["invc"] = invc.reshape(128, 64)
    rb = np.asarray(inp["rel_bias"], f32)
    rbx = np.concatenate([rb, np.full((1, 8), NEGM, f32)], axis=0)
    sl = np.arange(128)[:, None]
    q = np.arange(128)[None, :]
    bmh = np.zeros((128, 2, 2, 2, 2, 128), f32)
    for j in range(2):
        dist = q - sl + (128 if j == 0 else 0)
        valid = (dist >= 0) & (dist < 128)
        idx = np.where(valid, _t5_bucket(dist), 32)
        for c in range(2):
            for hj in range(4):
                bmh[:, c, j, hj // 2, hj % 2, :] = rbx[idx, 4 * c + hj]
    sh["bm"] = np.ascontiguousarray(bmh.reshape(128, 2048))
    sh["ident"] = np.eye(128, dtype=f32)
    bo = np.zeros((128, 128), f32)
    bo[0:64, 0:64] = 1.0
    bo[64:128, 64:128] = 1.0
    sh["bones"] = bo
    sh["wu"] = np.ascontiguousarray(win[:, 768:1280].reshape(8, 128, 512).transpose(1, 0, 2).reshape(128, 8 * 512))
    wb = np.zeros((128, 12, 128), f32)
    s_i = np.arange(128)[:, None]
    t_i = np.arange(128)[None, :]
    for g in range(4):
        w = 2 ** (g + 1)
        band = ((t_i - s_i >= 0) & (t_i - s_i < w)).astype(f32)
        eye = (s_i == t_i).astype(f32)
        wb[:, 3 * g + 0, :] = band / w - eye
        wb[:, 3 * g + 1, :] = ((t_i + 128 - s_i) < w).astype(f32) / w
        wb[:, 3 * g + 2, :] = band / np.minimum(t_i + 1, w).astype(f32) - eye
    sh["wband"] = np.ascontiguousarray(wb.reshape(128, 12 * 128))
    oz = np.zeros((128, 192), f32)
    oz[:, 64:128] = 1.0
    sh["onz"] = oz
    return sh


_CACHE = {}


def kernel(**inputs):
    stage = inputs.pop("_stage", 99)
    x = np.asarray(inputs["x"], np.float32)
    sh = prep_shared(inputs)
    if stage not in _CACHE:
        _CACHE[stage] = build(stage)
    nc = _CACHE[stage]
    in_maps = []
    for b in range(8):
        m = dict(sh)
        m["x"] = np.ascontiguousarray(x[b])
        in_maps.append(m)
    res = run_bass_kernel_spmd(nc, in_maps, core_ids=list(range(8)))
    return np.stack([res.results[b]["out"] for b in range(8)]).astype(np.float32)
```

```python
import numpy as np
from contextlib import ExitStack
import concourse.bass as bass
import concourse.mybir as mybir
from concourse.bass_utils import run_bass_kernel_spmd

F32 = mybir.dt.float32
BF16 = mybir.dt.bfloat16
AF = mybir.ActivationFunctionType
ALU = mybir.AluOpType

S = 2048
D = 1024
DFF = 2816
NF = DFF // 128
NT = S // 128
EPS = 1e-6
RING = 3
NEGM = -30000.0


class Buf:
    __slots__ = ("name", "w", "r")

    def __init__(self, name, fence=None):
        self.name = name
        self.w = None
        self.r = dict(fence) if fence else {}


class Prog:
    ENG = ("pe", "act", "dve", "pool", "sp")
    EMBED = {"pe": True}

    def __init__(self, nc, stack):
        self.nc = nc
        self.stack = stack
        self.q = {k: [] for k in self.ENG}
        self.sem = {}
        self.cnt = {}
        self.inc = {}
        self.waited = {}
        for k in ("pe", "act", "dve", "pool"):
            self.new_sem(k, 1)

    def new_sem(self, key, inc):
        self.sem[key] = self.stack.enter_context(self.nc.semaphore("s_" + key))
        self.cnt[key] = 0
        self.inc[key] = inc

    def fence(self):
        return {k: self.cnt[k] for k in ("pe", "act", "dve", "pool") if self.cnt[k] > 0}

    def op(self, eng, meth, kw, reads=(), writes=(), signal=True, dma=None):
        fn = (meth, kw)
        deps = {}

        def need(tag, war=False):
            key, val = tag
            if key == eng and (eng == "pe" or war):
                return
            if deps.get(key, 0) < val:
                deps[key] = val

        for b in reads:
            if b.w:
                need(b.w)
        for b in writes:
            if b.w:
                need(b.w)
            for k, v in b.r.items():
                need((k, v), war=True)
        for key, val in deps.items():
            if self.waited.get((eng, key), 0) >= val:
                continue
            self.waited[(eng, key)] = val
            self.q[eng].append(("wait", key, val))
        comp = dma if dma else eng
        if signal:
            self.cnt[comp] += self.inc[comp]
            tag = (comp, self.cnt[comp])
            self.q[eng].append(("op", fn, comp))
        else:
            tag = (comp, self.cnt[comp] + self.inc[comp])
            self.q[eng].append(("op", fn, None))
        for b in reads:
            if b.r.get(tag[0], 0) < tag[1]:
                b.r[tag[0]] = tag[1]
        for b in writes:
            b.w = tag
            b.r = {}
        return tag

    def wait_all(self, eng, tags):
        for key, val in tags:
            if self.waited.get((eng, key), 0) >= val:
                continue
            self.waited[(eng, key)] = val
            self.q[eng].append(("wait", key, val))

    def replay(self, eng, e):
        pend = []
        for item in self.q[eng]:
            if item[0] == "wait":
                pend.append(item)
                continue
            embed = pend.pop() if (pend and self.EMBED.get(eng, False)) else None
            for w in pend:
                e.wait_ge(self.sem[w[1]], w[2])
            pend = []
            ins = getattr(e, item[1][0])(**item[1][1])
            if embed is not None:
                ins._wait_ge(self.sem[embed[1]], embed[2])
            if item[2] is not None:
                ins.then_inc(self.sem[item[2]], self.inc[item[2]])
        for w in pend:
            e.wait_ge(self.sem[w[1]], w[2])


def build(stage=99):
    nc = bass.Bass("TRN2", target_bir_lowering=False)

    def din(name, shape, dt=F32):
        return nc.dram_tensor(name, list(shape), dt, kind="ExternalInput").ap()

    x_d = din("x", [S, D])
    out_d = nc.dram_tensor("out", [S, D], F32, kind="ExternalOutput").ap()
    wgu_d = [din(f"wgu{i}", [NF, 128, 2048]) for i in (1, 2)]
    wd_d = [din(f"wd{i}", [2, 11, 128, 1024]) for i in (1, 2)]
    gains_d = din("gains", [3, D])
    win_d = din("win", [11, 128, 1024])
    wout_d = din("wout", [128, 8 * D])
    poolw_d = din("poolw", [128, 4 * 128])
    cols_d = din("cols", [128, 16])
    invc_d = din("invc", [128, 4 * 16])
    bm_d = din("bm", [128, 2 * 2 * 512])
    ident_d = din("ident", [128, 128])
    bones_d = din("bones", [128, 128])
    onz_d = din("onz", [128, 192])
    wu_d = din("wu", [128, 8 * 512])
    wband_d = din("wband", [128, 12 * 128])

    with ExitStack() as st:
        P = Prog(nc, st)

        def sb(name, shape, dt):
            return st.enter_context(nc.sbuf_tensor("sb_" + name, list(shape), dt))

        xs = sb("xs", [128, NT, D], F32)
        gbc = sb("gbc", [128, D], F32)
        hn = sb("hn", [128, 2, D], BF16)
        junk = sb("junk", [128, D], BF16)
        ident = sb("ident", [128, 128], BF16)
        bones = sb("bones", [128, 128], BF16)
        onz = sb("onz", [128, 192], BF16)
        wband = sb("wband", [128, 12, 128], BF16)
        cols = sb("cols", [128, 16], F32)
        gqs = sb("gqs", [128, 1], F32)
        epsc = sb("epsc", [128, 1], F32)
        invc = sb("invc", [128, 4, 16], F32)
        ss = sb("ss", [128, 48], F32)
        ring = sb("ring", [128, RING, 2048], BF16)
        NA = 115 * 512
        arena = sb("arena", [128, NA], BF16)
        psf = st.enter_context(nc.psum_tensor("psf", [128, 8, 512], F32))

        def av(off, n):
            return arena[:, off:off + n]

        o = 0
        hTA = av(o, 8 * 1024).rearrange("p (k t) -> p k t", k=8); o += 8 * 1024
        hT_f = hTA
        actT = av(o, NF * 1024).rearrange("p (f t) -> p f t", f=NF); o += NF * 1024
        wdb = av(o, 2 * NF * 512).rearrange("p (h f n) -> p h f n", h=2, f=NF); o += 2 * NF * 512
        sil = av(o, 2 * 1024).bitcast(F32).rearrange("p (s n) -> p s n", s=2); o += 2 * 1024
        assert o <= NA
        o = 0
        hTB = av(o + 8 * 1024, 8 * 1024).rearrange("p (k t) -> p k t", k=8)
        hTs = [hTA, hTB]
        woutb = av(o + 8 * 1024, 8 * D).rearrange("p (k n) -> p k n", k=8); o += 8 * S
        yT = av(o, 8 * S).rearrange("p (k t) -> p k t", k=8); o += 8 * S
        trt = av(o, 2 * 1024).bitcast(F32).rearrange("p (s n) -> p s n", s=2)
        eskf = av(o + 2048, 1024).bitcast(F32); o += 2 * S
        vaug = av(o, NT * 2 * 192).rearrange("p (t c n) -> p t c n", t=NT, c=2); o += NT * 2 * 192
        sq = av(o, 1024).rearrange("p (s n) -> p s n", s=2); o += 1024
        poolw = av(o, 512).rearrange("p (g n) -> p g n", g=4); o += 512
        rtmp = av(o, 2 * 1024).bitcast(F32).rearrange("p (s n) -> p s n", s=2); o += 2 * 1024
        wub = ring[:, 0:2, :].rearrange("p s (k n) -> p (s k) n", k=4)
        utok = av(o + 4096, 8192).rearrange("p (t n) -> p t n", t=NT)
        bm = av(o, 2048).rearrange("p (c j n) -> p c j n", c=2, j=2)
        probs = av(o + 2048, 2048).rearrange("p (s j n) -> p s j n", s=2, j=2)
        kpad = av(o + 4096, 2 * S).rearrange("p (c t) -> p c t", c=2)
        o += 3 * 4096
        assert o <= NA, o

        B_xs = [[Buf(f"xs{t}_{h}") for h in range(2)] for t in range(NT)]
        B_gbc = Buf("gbc")
        B_hn = [Buf("hn0"), Buf("hn1")]
        B_hTA = [Buf(f"hTA{i}") for i in range(8)]
        B_ss0 = Buf("ss0")
        B_cst = {k: Buf(k) for k in ("ident", "bones", "cols", "invc", "esk", "gqs", "eps", "onz", "wband")}
        B_ring = [Buf(f"ring{i}") for i in range(RING)]
        B_ps = [Buf(f"ps{i}") for i in range(8)]
        ring_n = [0]
        norm_n = [0]

        def dma_sem(key):
            if key not in P.sem:
                P.new_sem(key, 16)
            return key

        def DMA(eng, key, out, in_, reads=(), writes=()):
            return P.op(eng, "dma_start", dict(out=out, in_=in_), reads=reads, writes=writes, dma=dma_sem(key))

        def ACT(out, in_, func, reads, writes, **kw):
            return P.op("act", "activation", dict(out=out, in_=in_, func=func, **kw), reads=reads, writes=writes)

        def MM(out, lhsT, rhs, start, stop, reads, writes, signal=True):
            return P.op("pe", "matmul", dict(out=out, lhsT=lhsT, rhs=rhs, start=start, stop=stop),
                        reads=reads, writes=writes, signal=signal)

        def TS(out, in0, s1, s2, op0, op1, reads, writes, eng="dve"):
            kw = dict(out=out, in0=in0, scalar1=s1, scalar2=s2, op0=op0)
            if op1 is not None:
                kw["op1"] = op1
            return P.op(eng, "tensor_scalar", kw, reads=reads, writes=writes)

        def STT(out, in0, scalar, in1, op0, op1, reads, writes):
            return P.op("dve", "scalar_tensor_tensor", dict(out=out, in0=in0, scalar=scalar, in1=in1, op0=op0, op1=op1),
                        reads=reads, writes=writes)

        def TT(out, in0, in1, op, reads, writes, eng="dve"):
            return P.op(eng, "tensor_tensor", dict(out=out, in0=in0, in1=in1, op=op), reads=reads, writes=writes)

        def CP(out, in_, reads, writes, eng="dve"):
            return P.op(eng, "tensor_copy", dict(out=out, in_=in_), reads=reads, writes=writes)

        def load_gain(i):
            DMA("sp", "dma_gbc", gbc[:], gains_d[i:i + 1, :].broadcast_to([128, D]), writes=[B_gbc])

        def load_x(t0, nt, key, extra_reads=()):
            DMA("sp", key, xs[:, t0:t0 + nt, :], x_d[128 * t0:128 * (t0 + nt), :].rearrange("(t p) d -> p t d", p=128),
                reads=list(extra_reads), writes=[B_xs[t][h] for t in range(t0, t0 + nt) for h in range(2)])

        load_x(0, 1, "dma_x0")
        load_x(1, 1, "dma_x1")
        DMA("pool", "dma_ident", ident[:], ident_d[:, :], writes=[B_cst["ident"]])

        def late_consts():
            DMA("pool", "dma_bones", bones[:], bones_d[:, :], writes=[B_cst["bones"]])
            DMA("pool", "dma_onz", onz[:], onz_d[:, :], writes=[B_cst["onz"]])
            DMA("pool", "dma_wband", wband[:].rearrange("p m n -> p (m n)"), wband_d[:, :], writes=[B_cst["wband"]])
        load_gain(0)
        load_x(2, 2, "dma_x2")
        load_x(4, 4, "dma_x3")
        DMA("sp", "dma_cols", cols[:], cols_d[:, :], writes=[B_cst["cols"]])
        DMA("sp", "dma_invc", invc[:].rearrange("p g n -> p (g n)"), invc_d[:, :], writes=[B_cst["invc"]])

        def norm_pre_begin(tiles):
            n = len(tiles)
            base = norm_n[0]
            norm_n[0] += n
            B_ss = Buf(f"ss{base}")
            B_ss.w = B_ss0.w
            return dict(base=base, tiles=tiles, B_ss=B_ss)

        def norm_pre_square(ctx, i):
            tt = ctx["tiles"][i]
            base = ctx["base"]
            ACT(junk[:], xs[:, tt, :], AF.Square, reads=B_xs[tt], writes=[ctx["B_ss"]],
                scale=1.0 / 32.0, accum_out=ss[:, base + i:base + i + 1])

        def norm_pre_finish(ctx):
            base, n, B_ss = ctx["base"], len(ctx["tiles"]), ctx["B_ss"]
            scs = ss[:, base:base + n]
            ACT(scs, scs, AF.Ln, reads=[B_ss, B_cst["eps"]], writes=[B_ss], bias=epsc[:, 0:1])
            ACT(scs, scs, AF.Exp, reads=[B_ss], writes=[B_ss], scale=-0.5)

        def norm_pre(tiles):
            ctx = norm_pre_begin(tiles)
            for i in range(len(tiles)):
                norm_pre_square(ctx, i)
            norm_pre_finish(ctx)
            return ctx

        def norm_fin_a(ctx, i):
            tt = ctx["tiles"][i]
            slot = ctx["base"] + i
            B_ss = ctx["B_ss"]
            hs = slot % 2
            STT(hn[:, hs, :], xs[:, tt, :], ss[:, slot:slot + 1], gbc[:], ALU.mult, ALU.mult,
                reads=B_xs[tt] + [B_ss, B_gbc], writes=[B_hn[hs]])

        def norm_fin_b(ctx, i, hT, lt, B_hT_tile):
            slot = ctx["base"] + i
            hs = slot % 2
            pb = 6 + (slot % 2)
            pT = psf[:, pb, :].bitcast(BF16).rearrange("p (k t) -> p k t", k=8)
            for k in range(8):
                P.op("pe", "transpose", dict(out=pT[:, k, :], in_=hn[:, hs, k * 128:(k + 1) * 128], identity=ident[:]),
                     reads=[B_hn[hs], B_cst["ident"]], writes=[B_ps[pb]], signal=(k == 7))
            if slot % 2 == 0:
                ACT(hT[:, :, lt * 128:(lt + 1) * 128], pT, AF.Copy, reads=[B_ps[pb]], writes=[B_hT_tile])
            else:
                CP(hT[:, :, lt * 128:(lt + 1) * 128], pT, reads=[B_ps[pb]], writes=[B_hT_tile])

        def norm_group(tiles, hT, lts, B_hT_tiles):
            ctx = norm_pre(tiles)
            for i in range(len(tiles)):
                norm_fin_a(ctx, i)
                norm_fin_b(ctx, i, hT, lts[i], B_hT_tiles[i])

        def ffn(idx, final, first_norm_done=False, next_norm=None):
            fen = P.fence()
            B_hT = B_hTA
            B_act = [[Buf(f"act{f}_{h}", fen) for h in range(2)] for f in range(NF)]
            B_wd = [[Buf(f"wd{h}_{i}", fen) for i in range(11)] for h in range(2)]
            B_sil = [Buf("sil0", fen), Buf("sil1", fen)]
            wgu = wgu_d[idx]
            wd = wd_d[idx]
            if not first_norm_done and idx != 0:
                load_gain(2)
            out_tags = []
            setn = 0
            on = 0

            def issue_ring(f):
                slot = ring_n[0] % RING
                extra = []
                if idx == 0 and ring_n[0] == 0:
                    extra = [B_xs[3][0]]
                elif idx == 0 and ring_n[0] in (1, 2):
                    extra = [B_xs[7][0]]
                ring_n[0] += 1
                DMA("pool", f"dma_ring{slot}", ring[:, slot, :], wgu[f], reads=extra, writes=[B_ring[slot]])
                return slot

            def issue_wd(i):
                h, pi = divmod(i, 11)
                DMA("pool", f"dma_wd{h}_{pi}", wdb[:, h, 2 * pi:2 * pi + 2, :],
                    wd[h, pi].rearrange("p (f n) -> p f n", f=2), writes=[B_wd[h][pi]])

            nctx = None
            for sbk in range(2):
                slots = {}
                for f in range(RING):
                    slots[f] = issue_ring(f)
                if idx == 0 and sbk == 0:
                    late_consts()
                wd_next = 0
                if sbk == 0 and not first_norm_done:
                    ctxs = [None] * 8
                    for lt in range(9):
                        if lt < 8:
                            ctxs[lt] = norm_pre([lt])
                        if lt >= 1:
                            norm_fin_a(ctxs[lt - 1], 0)
                            norm_fin_b(ctxs[lt - 1], 0, hTA, lt - 1, B_hT[lt - 1])
                    if idx == 0:
                        load_x(8, 4, "dma_x4", extra_reads=[B_hTA[7]])
                        load_x(12, 4, "dma_x5")
                for f in range(NF):
                    slot = slots[f]
                    for half in range(2):
                        pg, pu = 2 * (setn % 2), 2 * (setn % 2) + 1
                        s_ = setn % 2
                        setn += 1
                        hts = B_hT[4 * half:4 * half + 4]
                        rhs_cols = slice(half * 512, (half + 1) * 512)
                        for k in range(8):
                            MM(psf[:, pg, :], ring[:, slot, k * 128:(k + 1) * 128], hT_f[:, k, rhs_cols],
                               k == 0, k == 7, reads=[B_ring[slot]] + hts, writes=[B_ps[pg]], signal=(k == 7))
                        for k in range(8):
                            MM(psf[:, pu, :], ring[:, slot, 1024 + k * 128:1024 + (k + 1) * 128], hT_f[:, k, rhs_cols],
                               k == 0, k == 7, reads=[B_ring[slot]] + hts, writes=[B_ps[pu]], signal=(k == 7))
                        ACT(sil[:, s_, :], psf[:, pg, :], AF.Silu, reads=[B_ps[pg]], writes=[B_sil[s_]])
                        TT(actT[:, f, rhs_cols], psf[:, pu, :], sil[:, s_, :], ALU.mult,
                           reads=[B_ps[pu], B_sil[s_]], writes=[B_act[f][half]])
                    if f + RING < NF:
                        slots[f + RING] = issue_ring(f + RING)
                    if f == NF - 11:
                        if sbk == 0:
                            nctx = norm_pre_begin([8 + lt for lt in range(8)])
                        elif next_norm is not None:
                            load_gain(next_norm[0])
                            nctx = norm_pre_begin(next_norm[1])
                        else:
                            nctx = None
                    if nctx is not None and NF - 10 <= f < NF - 2:
                        norm_pre_square(nctx, f - (NF - 10))
                    if nctx is not None and f == NF - 2:
                        norm_pre_finish(nctx)
                    if f >= 1 and wd_next < 22:
                        issue_wd(wd_next)
                        wd_next += 1
                        if wd_next < 22 and f >= 20:
                            issue_wd(wd_next)
                            wd_next += 1
                while wd_next < 22:
                    issue_wd(wd_next)
                    wd_next += 1
                for lt in range(8):
                    tt = sbk * 8 + lt
                    if nctx is not None:
                        norm_fin_a(nctx, lt)
                    for dh in range(2):
                        po = 4 + (on % 2)
                        on += 1
                        cs = slice(dh * 512, (dh + 1) * 512)
                        for f in range(NF):
                            MM(psf[:, po, :], actT[:, f, lt * 128:(lt + 1) * 128], wdb[:, dh, f, :], f == 0, f == NF - 1,
                               reads=[B_act[f][lt // 4], B_wd[dh][f // 2]], writes=[B_ps[po]], signal=(f == NF - 1))
                        STT(xs[:, tt, cs], psf[:, po, :], 0.5, xs[:, tt, cs], ALU.mult, ALU.add,
                            reads=[B_ps[po], B_xs[tt][dh]], writes=[B_xs[tt][dh]])
                        if final:
                            out_tags.append(DMA("sp", "dma_out", out_d[tt * 128:(tt + 1) * 128, cs], xs[:, tt, cs],
                                                reads=[B_xs[tt][dh]]))
                    if nctx is not None:
                        norm_fin_b(nctx, lt, hTA, lt, B_hTA[lt])
            return out_tags

        def dump_xs():
            tags = []
            for tt in range(NT):
                tags.append(DMA("sp", "dma_out", out_d[tt * 128:(tt + 1) * 128, :], xs[:, tt, :], reads=B_xs[tt]))
            return tags

        def mixer(sub=9, first_half_done=False, next_gain=None):
            fen = P.fence()
            B_hT = B_hTA + [Buf(f"hTB{i}", fen) for i in range(8)]
            B_y = [[Buf(f"y{k}_{t}", fen) for t in range(NT)] for k in range(8)]
            B_v = [Buf(f"v{t}", fen) for t in range(4)]
            B_pw = Buf("pw", fen)
            B_sq = [Buf("sq0", fen), Buf("sq1", fen)]
            B_rt = [Buf("rt0", fen), Buf("rt1", fen)]
            B_wout = Buf("wout", fen)
            if not first_half_done:
                load_gain(1)
            TS(gqs[:], cols[:, 0:1], 0.125, None, ALU.mult, None, reads=[B_cst["cols"]], writes=[B_cst["gqs"]])
            chunk_slots = {}

            def issue_chunk(m):
                slot = ring_n[0] % RING
                ring_n[0] += 1
                DMA("pool", f"dma_ring{slot}", ring[:, slot, 0:1024], win_d[m], writes=[B_ring[slot]])
                chunk_slots[m] = slot

            order = [6, 7, 8, 9, 4, 10]
            B_ut = [Buf(f"ut{t}", fen) for t in range(NT)]
            DMA("pool", "dma_ring0", ring[:, 0:2, :].rearrange("p s n -> p (s n)"), wu_d[:, :], writes=[B_ring[0], B_ring[1]])
            B_ring[1].w = B_ring[0].w
            ring_n[0] = (ring_n[0] // RING + 1) * RING + 2
            issue_chunk(order[0])
            DMA("pool", "dma_pw", poolw[:].rearrange("p g n -> p (g n)"), poolw_d[:, :], writes=[B_pw])
            B_bm = Buf("bm", fen)
            B_probs = [Buf("pr0", fen), Buf("pr1", fen)]
            B_trt = [Buf(f"trt{i}", fen) for i in range(2)]
            B_eskf = Buf("eskf", fen)
            DMA("pool", "dma_bm", bm[:].rearrange("p c j n -> p (c j n)"), bm_d[:, :], writes=[B_bm])
            P.op("dve", "memset", dict(ap=eskf[:], constant=0.0), writes=[B_eskf])
            for j in range(4):
                cc = slice(j * 128, (j + 1) * 128)
                ACT(eskf[:, cc], eskf[:, cc], AF.Exp, reads=[B_eskf, B_cst["cols"]], writes=[B_eskf], bias=cols[:, 6 + j:7 + j])
            P.op("pool", "memset", dict(ap=vaug[:].rearrange("p t c n -> p (t c n)"), constant=0.0), writes=B_v)
            pn = [0]
            evn = [0]

            def evac(out, in_, reads, writes, scale=None):
                evn[0] += 1
                if evn[0] % 2 == 0:
                    if scale is None:
                        ACT(out, in_, AF.Copy, reads=reads, writes=writes)
                    else:
                        ACT(out, in_, AF.Copy, reads=reads, writes=writes, scale=scale)
                else:
                    if scale is None:
                        CP(out, in_, reads=reads, writes=writes)
                    else:
                        TS(out, in_, scale, None, ALU.mult, None, reads=reads, writes=writes)

            def u_tile(tt):
                pb = pn[0] % 4
                pn[0] += 1
                for k in range(8):
                    MM(psf[:, pb, :], hTs[tt // 8][:, k, (tt % 8) * 128:(tt % 8 + 1) * 128], wub[:, k, :], k == 0, k == 7,
                       reads=[B_ring[0], B_ring[1], B_hT[tt]], writes=[B_ps[pb]], signal=(k == 7))
                evac(utok[:, tt, :], psf[:, pb, :], reads=[B_ps[pb]], writes=[B_ut[tt]])

            def pool_bank(g, tb):
                pb = pn[0] % 4
                pn[0] += 1
                for ti in range(4):
                    T = 4 * tb + ti
                    dst = psf[:, pb, ti * 128:(ti + 1) * 128]
                    gs = slice(g * 128, (g + 1) * 128)
                    if T == 0:
                        MM(dst, utok[:, 0, gs], wband[:, 3 * g + 2, :], True, True, reads=[B_ut[0], B_cst["wband"]],
                           writes=[B_ps[pb]], signal=False)
                    else:
                        MM(dst, utok[:, T - 1, gs], wband[:, 3 * g + 1, :], True, False, reads=[B_ut[T - 1], B_cst["wband"]],
                           writes=[B_ps[pb]], signal=False)
                        MM(dst, utok[:, T, gs], wband[:, 3 * g + 0, :], False, True, reads=[B_ut[T], B_cst["wband"]],
                           writes=[B_ps[pb]], signal=(ti == 3))
                evac(yT[:, 4 + g, tb * 512:(tb + 1) * 512], psf[:, pb, :], reads=[B_ps[pb]],
                     writes=[B_y[4 + g][4 * tb + t] for t in range(4)])

            def pool_linear_item(g, tb):
                pb = pn[0] % 4
                pn[0] += 1
                cs = slice(tb * 512, (tb + 1) * 512)
                yb = [B_y[4 + g][4 * tb + t] for t in range(4)]
                MM(psf[:, pb, :], poolw[:, g, :], yT[:, 4 + g, cs], True, True, reads=[B_pw] + yb, writes=[B_ps[pb]])
                evac(yT[:, 4 + g, cs], psf[:, pb, :], reads=[B_ps[pb], B_cst["cols"]], writes=yb, scale=cols[:, 2 + g:3 + g])

            pl_items = [(g, tb) for tb in range(4) for g in range(4)]

            if not first_half_done:
                norm_group(list(range(8)), hTA, list(range(8)), B_hT[0:8])
            nctx = norm_pre_begin(list(range(8, 16)))
            for i in range(8):
                u_tile(i)
                norm_pre_square(nctx, i)
            norm_pre_finish(nctx)
            for g in range(4):
                pool_bank(g, 0)
            norm_fin_a(nctx, 0)
            for i in range(8):
                if i + 1 < 8:
                    norm_fin_a(nctx, i + 1)
                if i < 4:
                    pool_bank(i, 1)
                norm_fin_b(nctx, i, hTB, i, B_hT[8 + i])
                if i >= 1:
                    u_tile(8 + i - 1)
            u_tile(NT - 1)
            for tb in (2, 3):
                for g in range(4):
                    pool_bank(g, tb)
            for m in order[1:RING]:
                issue_chunk(m)

            def proj_chunk(m, tb):
                slot = chunk_slots[m]
                pb = pn[0] % 4
                pn[0] += 1
                for k in range(8):
                    MM(psf[:, pb, :], ring[:, slot, k * 128:(k + 1) * 128], hTs[tb // 2][:, k, (tb % 2) * 512:(tb % 2 + 1) * 512],
                       k == 0, k == 7, reads=[B_ring[slot]] + B_hT[4 * tb:4 * tb + 4], writes=[B_ps[pb]], signal=(k == 7))
                return pb

            def after_chunk(m):
                i = order.index(m)
                if i + RING < len(order):
                    issue_chunk(order[i + RING])

            kq_n = [0]
            B_k = [None]

            kq_pend = []

            def kq_finish():
                m, tb, pb, s2 = kq_pend.pop(0)
                cs = slice(tb * 512, (tb + 1) * 512)
                pb2 = 4 + s2
                MM(psf[:, pb2, :], bones[:], sq[:, s2, :], True, True, reads=[B_cst["bones"], B_sq[s2]], writes=[B_ps[pb2]])
                ACT(rtmp[:, s2, :], psf[:, pb2, :], AF.Ln, reads=[B_ps[pb2], B_cst["eps"]], writes=[B_rt[s2]], bias=epsc[:, 0:1])
                ACT(rtmp[:, s2, :], rtmp[:, s2, :], AF.Exp, reads=[B_rt[s2]], writes=[B_rt[s2]], scale=-0.5)
                if m < 6:
                    for c in range(2):
                        pr = slice(64 * c, 64 * c + 64)
                        STT(kpad[pr, c, cs], psf[pr, pb, :], cols[pr, 1:2], rtmp[pr, s2, :], ALU.mult, ALU.mult,
                            reads=[B_ps[pb], B_rt[s2], B_cst["cols"]], writes=[B_k[0][c][tb]])
                else:
                    j = m - 6
                    STT(yT[:, j, cs], psf[:, pb, :], gqs[:, 0:1], rtmp[:, s2, :], ALU.mult, ALU.mult,
                        reads=[B_ps[pb], B_rt[s2], B_cst["gqs"]], writes=[B_y[j][4 * tb + t] for t in range(4)])

            def do_kq(m):
                for tb in range(4):
                    pb = proj_chunk(m, tb)
                    s2 = kq_n[0] % 2
                    kq_n[0] += 1
                    ACT(sq[:, s2, :], psf[:, pb, :], AF.Square, reads=[B_ps[pb]], writes=[B_sq[s2]], scale=0.125)
                    kq_pend.append((m, tb, pb, s2))
                    if len(kq_pend) > 1:
                        kq_finish()
                while kq_pend:
                    kq_finish()
                after_chunk(m)

            fenk = P.fence()
            B_k[0] = [[Buf(f"k{c}_{t}", fenk) for t in range(4)] for c in range(2)]
            P.op("pool", "memset", dict(ap=kpad[:].rearrange("p c t -> p (c t)"), constant=0.0),
                 writes=[B_k[0][c][t] for c in range(2) for t in range(4)])
            while pl_items:
                pool_linear_item(*pl_items.pop(0))
            for m in order[:5]:
                do_kq(m)
            while kq_pend:
                kq_finish()
            B_k = B_k[0]
            slot = chunk_slots[10]
            for t4 in range(4):
                pb = pn[0] % 4
                pn[0] += 1
                for ti in range(4):
                    tt = 4 * t4 + ti
                    for k in range(8):
                        MM(psf[:, pb, ti * 128:(ti + 1) * 128], hTs[tt // 8][:, k, (tt % 8) * 128:(tt % 8 + 1) * 128],
                           ring[:, slot, k * 128:(k + 1) * 128], k == 0, k == 7,
                           reads=[B_ring[slot], B_hT[tt]], writes=[B_ps[pb]], signal=(k == 7 and ti == 3))
                ACT(vaug[:, 4 * t4:4 * t4 + 4, :, 64:128], psf[:, pb, :].rearrange("p (t c n) -> p t c n", t=4, c=2), AF.Copy,
                    reads=[B_ps[pb]], writes=[B_v[t4]])
            DMA("pool", "dma_wout", woutb[:].rearrange("p k n -> p (k n)"), wout_d[:, :], writes=[B_wout] + B_hT[8:16])
            iters = [(n, c) for n in range(NT) for c in range(2)]
            B_yq = lambda n: [B_y[j][n] for j in range(4)]

            def att_logits(it):
                n, c = iters[it]
                s2 = it % 2
                js = [1] if n == 0 else [0, 1]
                qb = slice(n * 128, (n + 1) * 128)
                for j in js:
                    kt = n - 1 + j
                    pl = 2 * s2 + j
                    MM(psf[:, pl, :], ident[:], bm[:, c, j, :], True, False, reads=[B_cst["ident"], B_bm],
                       writes=[B_ps[pl]], signal=False)
                    MM(psf[:, pl, :], kpad[:, c, kt * 128:(kt + 1) * 128], yT[:, 0:4, qb], False, True,
                       reads=[B_k[c][kt // 4]] + B_yq(n), writes=[B_ps[pl]])
                    ACT(probs[:, s2, j, :], psf[:, pl, :], AF.Exp, reads=[B_ps[pl]], writes=[B_probs[s2]])

            def att_pv_norm(it):
                n, c = iters[it]
                s2 = it % 2
                sn = n % 2
                js = [1] if n == 0 else [0, 1]
                pD, pS = 4 + 2 * sn, 5 + 2 * sn
                c0 = 64 if c == 0 else 0
                for (pp, is_v) in ((pD, True), (pS, False)):
                    for ji, j in enumerate(js):
                        kt = n - 1 + j
                        if is_v:
                            lhsT, rd = vaug[:, kt, c, c0:c0 + 128], [B_v[kt // 4], B_probs[s2]]
                        else:
                            lhsT, rd = onz[:, c0:c0 + 128], [B_cst["onz"], B_probs[s2]]
                        first = (c == 0 and ji == 0)
                        last = (c == 1 and ji == len(js) - 1)
                        MM(psf[:, pp, :], lhsT, probs[:, s2, j, :], first, last, reads=rd, writes=[B_ps[pp]],
                           signal=(ji == len(js) - 1))

            def att_norm(n):
                sn = n % 2
                pS = 5 + 2 * sn
                TT(trt[:, sn, :], psf[:, pS, :], eskf[:], ALU.add, reads=[B_ps[pS], B_eskf], writes=[B_trt[sn]])
                ACT(trt[:, sn, :], trt[:, sn, :], AF.Ln, reads=[B_trt[sn]], writes=[B_trt[sn]])
                ACT(trt[:, sn, :], trt[:, sn, :], AF.Exp, reads=[B_trt[sn]], writes=[B_trt[sn]], scale=-1.0)

            def att_scale(n):
                sn = n % 2
                pD = 4 + 2 * sn
                qb = slice(n * 128, (n + 1) * 128)
                TT(yT[:, 0:4, qb], psf[:, pD, :].rearrange("p (j q) -> p j q", j=4),
                   trt[:, sn, :].rearrange("p (j q) -> p j q", j=4), ALU.mult, reads=[B_ps[pD], B_trt[sn]], writes=B_yq(n))

            att_logits(0)
            for it in range(len(iters)):
                if it + 1 < len(iters):
                    att_logits(it + 1)
                att_pv_norm(it)
                n, c = iters[it]
                if c == 0 and n >= 1:
                    att_norm(n - 1)
                if c == 1 and n >= 1:
                    att_scale(n - 1)
            att_norm(NT - 1)
            att_scale(NT - 1)
            on = 0
            if next_gain is not None:
                load_gain(next_gain)
            pend = []
            for tt in range(NT):
                if pend and not pend[-1][2]:
                    norm_fin_a(pend[-1][0], 0)
                    pend[-1][2] = True
                for dh in range(2):
                    po = on % 4
                    on += 1
                    cs = slice(dh * 512, (dh + 1) * 512)
                    for k in range(8):
                        MM(psf[:, po, :], yT[:, k, tt * 128:(tt + 1) * 128], woutb[:, k, cs], k == 0, k == 7,
                           reads=[B_y[k][tt], B_wout], writes=[B_ps[po]], signal=(k == 7))
                    TT(xs[:, tt, cs], psf[:, po, :], xs[:, tt, cs], ALU.add, reads=[B_ps[po], B_xs[tt][dh]], writes=[B_xs[tt][dh]])
                if len(pend) >= 2 or (tt >= 8 and pend):
                    ctx, t0, _ = pend.pop(0)
                    norm_fin_b(ctx, 0, hTA, t0, B_hTA[t0])
                if next_gain is not None and tt < 8:
                    pend.append([norm_pre([tt]), tt, False])

        P.op("dve", "memset", dict(ap=ss[:], constant=0.0), writes=[B_ss0])
        P.op("dve", "memset", dict(ap=epsc[:], constant=EPS), writes=[B_cst["eps"]])
        if stage == 1:
            ffn(0, final=False)
            tags = dump_xs()
        elif stage == 2:
            ffn(0, final=False, next_norm=(1, list(range(8))))
            mixer(first_half_done=True)
            tags = dump_xs()
        else:
            ffn(0, final=False, next_norm=(1, list(range(8))))
            mixer(first_half_done=True, next_gain=2)
            tags = ffn(1, final=True, first_norm_done=True)
        P.wait_all("sp", [max(tags, key=lambda t: t[1])])

        with nc.Block() as block:
            @block.tensor
            def _(e):
                P.replay("pe", e)

            @block.scalar
            def _(e):
                P.replay("act", e)

            @block.vector
            def _(e):
                P.replay("dve", e)

            @block.gpsimd
            def _(e):
                P.replay("pool", e)

            @block.sync
            def _(e):
                P.replay("sp", e)
    return nc


def _t5_bucket(dist):
    n = np.maximum(dist, 0)
    max_exact = 16
    large = max_exact + (np.log(np.maximum(n, 1) / max_exact) / np.log(128 / max_exact) * (32 - max_exact)).astype(np.int32)
    large = np.minimum(large, 31)
    return np.where(n < max_exact, n, large).astype(np.int32)


def prep_shared(inp):
    f32 = np.float32
    sh = {}
    for i, pre in ((1, "ffn1"), (2, "ffn2")):
        wg = np.asarray(inp[pre + "_w_gate"], f32)[0]
        wu = np.asarray(inp[pre + "_w_up"], f32)[0]
        wd = np.asarray(inp[pre + "_w_down"], f32)[0]
        g = wg.reshape(8, 128, NF, 128).transpose(2, 1, 0, 3).reshape(NF, 128, 1024)
        u = wu.reshape(8, 128, NF, 128).transpose(2, 1, 0, 3).reshape(NF, 128, 1024)
        sh[f"wgu{i}"] = np.ascontiguousarray(np.concatenate([g, u], axis=2))
        d = wd.reshape(11, 2, 128, 2, 512).transpose(3, 0, 2, 1, 4).reshape(2, 11, 128, 1024)
        sh[f"wd{i}"] = np.ascontiguousarray(d)
    sh["gains"] = np.ascontiguousarray(np.stack([np.asarray(inp["ffn1_norm"], f32)[0], np.asarray(inp["mix_norm"], f32)[0],
                                                 np.asarray(inp["ffn2_norm"], f32)[0]]))
    win = np.asarray(inp["w_in"], f32)[0]
    chunks = []
    for g in range(4):
        chunks.append(win[:, 768 + 128 * g:768 + 128 * (g + 1)])
    chunks.append(win[:, 512:640])
    chunks.append(win[:, 512:640])
    for j in range(4):
        chunks.append(np.concatenate([win[:, 64 * j:64 * (j + 1)], win[:, 64 * (4 + j):64 * (5 + j)]], axis=1))
    chunks.append(win[:, 640:768])
    wl = np.stack(chunks)
    sh["win"] = np.ascontiguousarray(wl.reshape(11, 8, 128, 128).transpose(0, 2, 1, 3).reshape(11, 128, 1024))
    wo = np.asarray(inp["w_out"], f32)[0]
    wo = np.concatenate([np.concatenate([wo[64 * j:64 * (j + 1)], wo[64 * (4 + j):64 * (5 + j)]], axis=0) for j in range(4)]
                        + [wo[512:]], axis=0)
    sh["wout"] = np.ascontiguousarray(wo.reshape(8, 128, D).transpose(1, 0, 2).reshape(128, 8 * D))
    pw = np.asarray(inp["pool_w"], f32)[0]
    sh["poolw"] = np.ascontiguousarray(pw.transpose(1, 0, 2).reshape(128, 512))
    cols = np.zeros((128, 16), f32)
    cols[:, 0] = np.tile(np.asarray(inp["q_norm"], f32)[0], 2)
    cols[:, 1] = np.tile(np.asarray(inp["k_norm"], f32)[0], 2)
    ps = np.asarray(inp["pool_scale"], f32)[0]
    for g in range(4):
        cols[:, 2 + g] = ps[128 * g:128 * (g + 1)]
    sinks = np.asarray(inp["attn_sinks"], f32)[0]
    for j in range(4):
        cols[0:64, 6 + j] = sinks[j]
        cols[64:128, 6 + j] = sinks[4 + j]
    sh["cols"] = cols
    invc = np.zeros((128, 4, 16), f32)
    for g in range(4):
        w = 2 ** (g + 1)
        invc[:, g, :] = (1.0 / np.minimum(np.arange(1, 17), w)).astype(f32)[None, :]
    sh["invc"] = invc.reshape(128, 64)
    rb = np.asarray(inp["rel_bias"], f32)
    rbx = np.concatenate([rb, np.full((1, 8), NEGM, f32)], axis=0)
    sl = np.arange(128)[:, None]
    q = np.arange(128)[None, :]
    bmh = np.zeros((128, 2, 2, 2, 2, 128), f32)
    for j in range(2):
        dist = q - sl + (128 if j == 0 else 0)
        valid = (dist >= 0) & (dist < 128)
        idx = np.where(valid, _t5_bucket(dist), 32)
        for c in range(2):
            for hj in range(4):
                bmh[:, c, j, hj // 2, hj % 2, :] = rbx[idx, 4 * c + hj]
    sh["bm"] = np.ascontiguousarray(bmh.reshape(128, 2048))
    sh["ident"] = np.eye(128, dtype=f32)
    bo = np.zeros((128, 128), f32)
    bo[0:64, 0:64] = 1.0
    bo[64:128, 64:128] = 1.0
    sh["bones"] = bo
    sh["wu"] = np.ascontiguousarray(win[:, 768:1280].reshape(8, 128, 512).transpose(1, 0, 2).reshape(128, 8 * 512))
    wb = np.zeros((128, 12, 128), f32)
    s_i = np.arange(128)[:, None]
    t_i = np.arange(128)[None, :]
    for g in range(4):
        w = 2 ** (g + 1)
        band = ((t_i - s_i >= 0) & (t_i - s_i < w)).astype(f32)
        eye = (s_i == t_i).astype(f32)
        wb[:, 3 * g + 0, :] = band / w - eye
        wb[:, 3 * g + 1, :] = ((t_i + 128 - s_i) < w).astype(f32) / w
        wb[:, 3 * g + 2, :] = band / np.minimum(t_i + 1, w).astype(f32) - eye
    sh["wband"] = np.ascontiguousarray(wb.reshape(128, 12 * 128))
    oz = np.zeros((128, 192), f32)
    oz[:, 64:128] = 1.0
    sh["onz"] = oz
    return sh


_CACHE = {}


def kernel(**inputs):
    stage = inputs.pop("_stage", 99)
    x = np.asarray(inputs["x"], np.float32)
    sh = prep_shared(inputs)
    if stage not in _CACHE:
        _CACHE[stage] = build(stage)
    nc = _CACHE[stage]
    in_maps = []
    for b in range(8):
        m = dict(sh)
        m["x"] = np.ascontiguousarray(x[b])
        in_maps.append(m)
    res = run_bass_kernel_spmd(nc, in_maps, core_ids=list(range(8)))
    return np.stack([res.results[b]["out"] for b in range(8)]).astype(np.float32)
```

```python
import numpy as np
from contextlib import ExitStack
import concourse.bass as bass
import concourse.mybir as mybir
from concourse.bass_utils import run_bass_kernel_spmd

F32 = mybir.dt.float32
BF16 = mybir.dt.bfloat16
AF = mybir.ActivationFunctionType
ALU = mybir.AluOpType

S = 2048
D = 1024
DFF = 2816
NF = DFF // 128
NT = S // 128
EPS = 1e-6
RING = 3
NEGM = -30000.0


class Buf:
    __slots__ = ("name", "w", "r")

    def __init__(self, name, fence=None):
        self.name = name
        self.w = None
        self.r = dict(fence) if fence else {}


class Prog:
    ENG = ("pe", "act", "dve", "pool", "sp")

    def __init__(self, nc, stack):
        self.nc = nc
        self.stack = stack
        self.q = {k: [] for k in self.ENG}
        self.sem = {}
        self.cnt = {}
        self.inc = {}
        self.waited = {}
        for k in ("pe", "act", "dve", "pool"):
            self.new_sem(k, 1)

    def new_sem(self, key, inc):
        self.sem[key] = self.stack.enter_context(self.nc.semaphore("s_" + key))
        self.cnt[key] = 0
        self.inc[key] = inc

    def fence(self):
        return {k: self.cnt[k] for k in ("pe", "act", "dve", "pool") if self.cnt[k] > 0}

    def op(self, eng, meth, kw, reads=(), writes=(), signal=True, dma=None):
        fn = (meth, kw)
        deps = {}

        def need(tag, war=False):
            key, val = tag
            if key == eng and (eng == "pe" or war):
                return
            if deps.get(key, 0) < val:
                deps[key] = val

        for b in reads:
            if b.w:
                need(b.w)
        for b in writes:
            if b.w:
                need(b.w)
            for k, v in b.r.items():
                need((k, v), war=True)
        for key, val in deps.items():
            if self.waited.get((eng, key), 0) >= val:
                continue
            self.waited[(eng, key)] = val
            self.q[eng].append(("wait", key, val))
        comp = dma if dma else eng
        if signal:
            self.cnt[comp] += self.inc[comp]
            tag = (comp, self.cnt[comp])
            self.q[eng].append(("op", fn, comp))
        else:
            tag = (comp, self.cnt[comp] + self.inc[comp])
            self.q[eng].append(("op", fn, None))
        for b in reads:
            if b.r.get(tag[0], 0) < tag[1]:
                b.r[tag[0]] = tag[1]
        for b in writes:
            b.w = tag
            b.r = {}
        return tag

    def wait_all(self, eng, tags):
        for key, val in tags:
            if self.waited.get((eng, key), 0) >= val:
                continue
            self.waited[(eng, key)] = val
            self.q[eng].append(("wait", key, val))

    def replay(self, eng, e):
        for item in self.q[eng]:
            if item[0] == "wait":
                e.wait_ge(self.sem[item[1]], item[2])
            else:
                ins = getattr(e, item[1][0])(**item[1][1])
                if item[2] is not None:
                    ins.then_inc(self.sem[item[2]], self.inc[item[2]])


def build(stage=99):
    nc = bass.Bass("TRN2", target_bir_lowering=False)

    def din(name, shape, dt=F32):
        return nc.dram_tensor(name, list(shape), dt, kind="ExternalInput").ap()

    x_d = din("x", [S, D])
    out_d = nc.dram_tensor("out", [S, D], F32, kind="ExternalOutput").ap()
    wgu_d = [din(f"wgu{i}", [NF, 128, 2048]) for i in (1, 2)]
    wd_d = [din(f"wd{i}", [2, 11, 128, 1024]) for i in (1, 2)]
    gains_d = din("gains", [3, D])
    win_d = din("win", [11, 128, 1024])
    wout_d = din("wout", [128, 8 * D])
    poolw_d = din("poolw", [128, 4 * 128])
    cols_d = din("cols", [128, 16])
    invc_d = din("invc", [128, 4 * 16])
    bm_d = din("bm", [128, 2 * 2 * 512])
    ident_d = din("ident", [128, 128])
    bones_d = din("bones", [128, 128])
    onz_d = din("onz", [128, 192])
    wu_d = din("wu", [128, 8 * 512])
    wband_d = din("wband", [128, 12 * 128])

    with ExitStack() as st:
        P = Prog(nc, st)

        def sb(name, shape, dt):
            return st.enter_context(nc.sbuf_tensor("sb_" + name, list(shape), dt))

        xs = sb("xs", [128, NT, D], F32)
        gbc = sb("gbc", [128, D], F32)
        hn = sb("hn", [128, 2, D], BF16)
        junk = sb("junk", [128, D], BF16)
        ident = sb("ident", [128, 128], BF16)
        bones = sb("bones", [128, 128], BF16)
        onz = sb("onz", [128, 192], BF16)
        wband = sb("wband", [128, 12, 128], BF16)
        cols = sb("cols", [128, 16], F32)
        gqs = sb("gqs", [128, 1], F32)
        epsc = sb("epsc", [128, 1], F32)
        invc = sb("invc", [128, 4, 16], F32)
        ss = sb("ss", [128, 48], F32)
        ring = sb("ring", [128, RING, 2048], BF16)
        NA = 115 * 512
        arena = sb("arena", [128, NA], BF16)
        psf = st.enter_context(nc.psum_tensor("psf", [128, 8, 512], F32))

        def av(off, n):
            return arena[:, off:off + n]

        o = 0
        hTA = av(o, 8 * 1024).rearrange("p (k t) -> p k t", k=8); o += 8 * 1024
        hT_f = hTA
        actT = av(o, NF * 1024).rearrange("p (f t) -> p f t", f=NF); o += NF * 1024
        wdb = av(o, 2 * NF * 512).rearrange("p (h f n) -> p h f n", h=2, f=NF); o += 2 * NF * 512
        sil = av(o, 2 * 1024).bitcast(F32).rearrange("p (s n) -> p s n", s=2); o += 2 * 1024
        assert o <= NA
        o = 0
        hTB = av(o + 8 * 1024, 8 * 1024).rearrange("p (k t) -> p k t", k=8)
        hTs = [hTA, hTB]
        woutb = av(o + 8 * 1024, 8 * D).rearrange("p (k n) -> p k n", k=8); o += 8 * S
        yT = av(o, 8 * S).rearrange("p (k t) -> p k t", k=8); o += 8 * S
        trt = av(o, 2 * 1024).bitcast(F32).rearrange("p (s n) -> p s n", s=2)
        eskf = av(o + 2048, 1024).bitcast(F32); o += 2 * S
        vaug = av(o, NT * 2 * 192).rearrange("p (t c n) -> p t c n", t=NT, c=2); o += NT * 2 * 192
        sq = av(o, 1024).rearrange("p (s n) -> p s n", s=2); o += 1024
        poolw = av(o, 512).rearrange("p (g n) -> p g n", g=4); o += 512
        rtmp = av(o, 2 * 1024).bitcast(F32).rearrange("p (s n) -> p s n", s=2); o += 2 * 1024
        wub = ring[:, 0:2, :].rearrange("p s (k n) -> p (s k) n", k=4)
        utok = av(o + 4096, 8192).rearrange("p (t n) -> p t n", t=NT)
        bm = av(o, 2048).rearrange("p (c j n) -> p c j n", c=2, j=2)
        probs = av(o + 2048, 2048).rearrange("p (s j n) -> p s j n", s=2, j=2)
        kpad = av(o + 4096, 2 * S).rearrange("p (c t) -> p c t", c=2)
        o += 3 * 4096
        assert o <= NA, o

        B_xs = [[Buf(f"xs{t}_{h}") for h in range(2)] for t in range(NT)]
        B_gbc = Buf("gbc")
        B_hn = [Buf("hn0"), Buf("hn1")]
        B_hTA = [Buf(f"hTA{i}") for i in range(8)]
        B_ss0 = Buf("ss0")
        B_cst = {k: Buf(k) for k in ("ident", "bones", "cols", "invc", "esk", "gqs", "eps", "onz", "wband")}
        B_ring = [Buf(f"ring{i}") for i in range(RING)]
        B_ps = [Buf(f"ps{i}") for i in range(8)]
        ring_n = [0]
        norm_n = [0]

        def dma_sem(key):
            if key not in P.sem:
                P.new_sem(key, 16)
            return key

        def DMA(eng, key, out, in_, reads=(), writes=()):
            return P.op(eng, "dma_start", dict(out=out, in_=in_), reads=reads, writes=writes, dma=dma_sem(key))

        def ACT(out, in_, func, reads, writes, **kw):
            return P.op("act", "activation", dict(out=out, in_=in_, func=func, **kw), reads=reads, writes=writes)

        def MM(out, lhsT, rhs, start, stop, reads, writes, signal=True):
            return P.op("pe", "matmul", dict(out=out, lhsT=lhsT, rhs=rhs, start=start, stop=stop),
                        reads=reads, writes=writes, signal=signal)

        def TS(out, in0, s1, s2, op0, op1, reads, writes, eng="dve"):
            kw = dict(out=out, in0=in0, scalar1=s1, scalar2=s2, op0=op0)
            if op1 is not None:
                kw["op1"] = op1
            return P.op(eng, "tensor_scalar", kw, reads=reads, writes=writes)

        def STT(out, in0, scalar, in1, op0, op1, reads, writes):
            return P.op("dve", "scalar_tensor_tensor", dict(out=out, in0=in0, scalar=scalar, in1=in1, op0=op0, op1=op1),
                        reads=reads, writes=writes)

        def TT(out, in0, in1, op, reads, writes, eng="dve"):
            return P.op(eng, "tensor_tensor", dict(out=out, in0=in0, in1=in1, op=op), reads=reads, writes=writes)

        def CP(out, in_, reads, writes, eng="dve"):
            return P.op(eng, "tensor_copy", dict(out=out, in_=in_), reads=reads, writes=writes)

        def load_gain(i):
            DMA("sp", "dma_gbc", gbc[:], gains_d[i:i + 1, :].broadcast_to([128, D]), writes=[B_gbc])

        def load_x(t0, nt, key, extra_reads=()):
            DMA("sp", key, xs[:, t0:t0 + nt, :], x_d[128 * t0:128 * (t0 + nt), :].rearrange("(t p) d -> p t d", p=128),
                reads=list(extra_reads), writes=[B_xs[t][h] for t in range(t0, t0 + nt) for h in range(2)])

        load_x(0, 1, "dma_x0")
        load_x(1, 1, "dma_x1")
        DMA("pool", "dma_ident", ident[:], ident_d[:, :], writes=[B_cst["ident"]])

        def late_consts():
            DMA("pool", "dma_bones", bones[:], bones_d[:, :], writes=[B_cst["bones"]])
            DMA("pool", "dma_onz", onz[:], onz_d[:, :], writes=[B_cst["onz"]])
            DMA("pool", "dma_wband", wband[:].rearrange("p m n -> p (m n)"), wband_d[:, :], writes=[B_cst["wband"]])
        load_gain(0)
        load_x(2, 2, "dma_x2")
        load_x(4, 4, "dma_x3")
        DMA("sp", "dma_cols", cols[:], cols_d[:, :], writes=[B_cst["cols"]])
        DMA("sp", "dma_invc", invc[:].rearrange("p g n -> p (g n)"), invc_d[:, :], writes=[B_cst["invc"]])

        def norm_pre_begin(tiles):
            n = len(tiles)
            base = norm_n[0]
            norm_n[0] += n
            B_ss = Buf(f"ss{base}")
            B_ss.w = B_ss0.w
            return dict(base=base, tiles=tiles, B_ss=B_ss)

        def norm_pre_square(ctx, i):
            tt = ctx["tiles"][i]
            base = ctx["base"]
            ACT(junk[:], xs[:, tt, :], AF.Square, reads=B_xs[tt], writes=[ctx["B_ss"]],
                scale=1.0 / 32.0, accum_out=ss[:, base + i:base + i + 1])

        def norm_pre_finish(ctx):
            base, n, B_ss = ctx["base"], len(ctx["tiles"]), ctx["B_ss"]
            scs = ss[:, base:base + n]
            ACT(scs, scs, AF.Ln, reads=[B_ss, B_cst["eps"]], writes=[B_ss], bias=epsc[:, 0:1])
            ACT(scs, scs, AF.Exp, reads=[B_ss], writes=[B_ss], scale=-0.5)

        def norm_pre(tiles):
            ctx = norm_pre_begin(tiles)
            for i in range(len(tiles)):
                norm_pre_square(ctx, i)
            norm_pre_finish(ctx)
            return ctx

        def norm_fin_a(ctx, i):
            tt = ctx["tiles"][i]
            slot = ctx["base"] + i
            B_ss = ctx["B_ss"]
            hs = slot % 2
            STT(hn[:, hs, :], xs[:, tt, :], ss[:, slot:slot + 1], gbc[:], ALU.mult, ALU.mult,
                reads=B_xs[tt] + [B_ss, B_gbc], writes=[B_hn[hs]])

        def norm_fin_b(ctx, i, hT, lt, B_hT_tile):
            slot = ctx["base"] + i
            hs = slot % 2
            pb = 6 + (slot % 2)
            pT = psf[:, pb, :].bitcast(BF16).rearrange("p (k t) -> p k t", k=8)
            for k in range(8):
                P.op("pe", "transpose", dict(out=pT[:, k, :], in_=hn[:, hs, k * 128:(k + 1) * 128], identity=ident[:]),
                     reads=[B_hn[hs], B_cst["ident"]], writes=[B_ps[pb]], signal=(k == 7))
            if slot % 2 == 0:
                ACT(hT[:, :, lt * 128:(lt + 1) * 128], pT, AF.Copy, reads=[B_ps[pb]], writes=[B_hT_tile])
            else:
                CP(hT[:, :, lt * 128:(lt + 1) * 128], pT, reads=[B_ps[pb]], writes=[B_hT_tile])

        def norm_group(tiles, hT, lts, B_hT_tiles):
            ctx = norm_pre(tiles)
            for i in range(len(tiles)):
                norm_fin_a(ctx, i)
                norm_fin_b(ctx, i, hT, lts[i], B_hT_tiles[i])

        def ffn(idx, final, first_norm_done=False, next_norm=None):
            fen = P.fence()
            B_hT = B_hTA
            B_act = [[Buf(f"act{f}_{h}", fen) for h in range(2)] for f in range(NF)]
            B_wd = [[Buf(f"wd{h}_{i}", fen) for i in range(11)] for h in range(2)]
            B_sil = [Buf("sil0", fen), Buf("sil1", fen)]
            wgu = wgu_d[idx]
            wd = wd_d[idx]
            if not first_norm_done and idx != 0:
                load_gain(2)
            out_tags = []
            setn = 0
            on = 0

            def issue_ring(f):
                slot = ring_n[0] % RING
                extra = []
                if idx == 0 and ring_n[0] == 0:
                    extra = [B_xs[3][0]]
                elif idx == 0 and ring_n[0] in (1, 2):
                    extra = [B_xs[7][0]]
                ring_n[0] += 1
                DMA("pool", f"dma_ring{slot}", ring[:, slot, :], wgu[f], reads=extra, writes=[B_ring[slot]])
                return slot

            def issue_wd(i):
                h, pi = divmod(i, 11)
                DMA("pool", f"dma_wd{h}_{pi}", wdb[:, h, 2 * pi:2 * pi + 2, :],
                    wd[h, pi].rearrange("p (f n) -> p f n", f=2), writes=[B_wd[h][pi]])

            nctx = None
            for sbk in range(2):
                slots = {}
                for f in range(RING):
                    slots[f] = issue_ring(f)
                if idx == 0 and sbk == 0:
                    late_consts()
                wd_next = 0
                if sbk == 0 and not first_norm_done:
                    ctxs = [None] * 8
                    for lt in range(9):
                        if lt < 8:
                            ctxs[lt] = norm_pre([lt])
                        if lt >= 1:
                            norm_fin_a(ctxs[lt - 1], 0)
                            norm_fin_b(ctxs[lt - 1], 0, hTA, lt - 1, B_hT[lt - 1])
                    if idx == 0:
                        load_x(8, 4, "dma_x4", extra_reads=[B_hTA[7]])
                        load_x(12, 4, "dma_x5")
                for f in range(NF):
                    slot = slots[f]
                    for half in range(2):
                        pg, pu = 2 * (setn % 2), 2 * (setn % 2) + 1
                        s_ = setn % 2
                        setn += 1
                        hts = B_hT[4 * half:4 * half + 4]
                        rhs_cols = slice(half * 512, (half + 1) * 512)
                        for k in range(8):
                            MM(psf[:, pg, :], ring[:, slot, k * 128:(k + 1) * 128], hT_f[:, k, rhs_cols],
                               k == 0, k == 7, reads=[B_ring[slot]] + hts, writes=[B_ps[pg]], signal=(k == 7))
                        for k in range(8):
                            MM(psf[:, pu, :], ring[:, slot, 1024 + k * 128:1024 + (k + 1) * 128], hT_f[:, k, rhs_cols],
                               k == 0, k == 7, reads=[B_ring[slot]] + hts, writes=[B_ps[pu]], signal=(k == 7))
                        ACT(sil[:, s_, :], psf[:, pg, :], AF.Silu, reads=[B_ps[pg]], writes=[B_sil[s_]])
                        TT(actT[:, f, rhs_cols], psf[:, pu, :], sil[:, s_, :], ALU.mult,
                           reads=[B_ps[pu], B_sil[s_]], writes=[B_act[f][half]])
                    if f + RING < NF:
                        slots[f + RING] = issue_ring(f + RING)
                    if f == NF - 11:
                        if sbk == 0:
                            nctx = norm_pre_begin([8 + lt for lt in range(8)])
                        elif next_norm is not None:
                            load_gain(next_norm[0])
                            nctx = norm_pre_begin(next_norm[1])
                        else:
                            nctx = None
                    if nctx is not None and NF - 10 <= f < NF - 2:
                        norm_pre_square(nctx, f - (NF - 10))
                    if nctx is not None and f == NF - 2:
                        norm_pre_finish(nctx)
                    if f >= 1 and wd_next < 22:
                        issue_wd(wd_next)
                        wd_next += 1
                        if wd_next < 22 and f >= 20:
                            issue_wd(wd_next)
                            wd_next += 1
                while wd_next < 22:
                    issue_wd(wd_next)
                    wd_next += 1
                for lt in range(8):
                    tt = sbk * 8 + lt
                    if nctx is not None:
                        norm_fin_a(nctx, lt)
                    for dh in range(2):
                        po = 4 + (on % 2)
                        on += 1
                        cs = slice(dh * 512, (dh + 1) * 512)
                        for f in range(NF):
                            MM(psf[:, po, :], actT[:, f, lt * 128:(lt + 1) * 128], wdb[:, dh, f, :], f == 0, f == NF - 1,
                               reads=[B_act[f][lt // 4], B_wd[dh][f // 2]], writes=[B_ps[po]], signal=(f == NF - 1))
                        STT(xs[:, tt, cs], psf[:, po, :], 0.5, xs[:, tt, cs], ALU.mult, ALU.add,
                            reads=[B_ps[po], B_xs[tt][dh]], writes=[B_xs[tt][dh]])
                        if final:
                            out_tags.append(DMA("sp", "dma_out", out_d[tt * 128:(tt + 1) * 128, cs], xs[:, tt, cs],
                                                reads=[B_xs[tt][dh]]))
                    if nctx is not None:
                        norm_fin_b(nctx, lt, hTA, lt, B_hTA[lt])
            return out_tags

        def dump_xs():
            tags = []
            for tt in range(NT):
                tags.append(DMA("sp", "dma_out", out_d[tt * 128:(tt + 1) * 128, :], xs[:, tt, :], reads=B_xs[tt]))
            return tags

        def mixer(sub=9, first_half_done=False, next_gain=None):
            fen = P.fence()
            B_hT = B_hTA + [Buf(f"hTB{i}", fen) for i in range(8)]
            B_y = [[Buf(f"y{k}_{t}", fen) for t in range(NT)] for k in range(8)]
            B_v = [Buf(f"v{t}", fen) for t in range(4)]
            B_pw = Buf("pw", fen)
            B_sq = [Buf("sq0", fen), Buf("sq1", fen)]
            B_rt = [Buf("rt0", fen), Buf("rt1", fen)]
            B_wout = Buf("wout", fen)
            if not first_half_done:
                load_gain(1)
            TS(gqs[:], cols[:, 0:1], 0.125, None, ALU.mult, None, reads=[B_cst["cols"]], writes=[B_cst["gqs"]])
            chunk_slots = {}

            def issue_chunk(m):
                slot = ring_n[0] % RING
                ring_n[0] += 1
                DMA("pool", f"dma_ring{slot}", ring[:, slot, 0:1024], win_d[m], writes=[B_ring[slot]])
                chunk_slots[m] = slot

            order = [6, 7, 8, 9, 4, 10]
            B_ut = [Buf(f"ut{t}", fen) for t in range(NT)]
            DMA("pool", "dma_ring0", ring[:, 0:2, :].rearrange("p s n -> p (s n)"), wu_d[:, :], writes=[B_ring[0], B_ring[1]])
            B_ring[1].w = B_ring[0].w
            ring_n[0] = (ring_n[0] // RING + 1) * RING + 2
            issue_chunk(order[0])
            DMA("pool", "dma_pw", poolw[:].rearrange("p g n -> p (g n)"), poolw_d[:, :], writes=[B_pw])
            B_bm = Buf("bm", fen)
            B_probs = [Buf("pr0", fen), Buf("pr1", fen)]
            B_trt = [Buf(f"trt{i}", fen) for i in range(2)]
            B_eskf = Buf("eskf", fen)
            DMA("pool", "dma_bm", bm[:].rearrange("p c j n -> p (c j n)"), bm_d[:, :], writes=[B_bm])
            P.op("dve", "memset", dict(ap=eskf[:], constant=0.0), writes=[B_eskf])
            for j in range(4):
                cc = slice(j * 128, (j + 1) * 128)
                ACT(eskf[:, cc], eskf[:, cc], AF.Exp, reads=[B_eskf, B_cst["cols"]], writes=[B_eskf], bias=cols[:, 6 + j:7 + j])
            P.op("pool", "memset", dict(ap=vaug[:].rearrange("p t c n -> p (t c n)"), constant=0.0), writes=B_v)
            pn = [0]
            evn = [0]

            def evac(out, in_, reads, writes, scale=None):
                evn[0] += 1
                if evn[0] % 2 == 0:
                    if scale is None:
                        ACT(out, in_, AF.Copy, reads=reads, writes=writes)
                    else:
                        ACT(out, in_, AF.Copy, reads=reads, writes=writes, scale=scale)
                else:
                    if scale is None:
                        CP(out, in_, reads=reads, writes=writes)
                    else:
                        TS(out, in_, scale, None, ALU.mult, None, reads=reads, writes=writes)

            def u_tile(tt):
                pb = pn[0] % 4
                pn[0] += 1
                for k in range(8):
                    MM(psf[:, pb, :], hTs[tt // 8][:, k, (tt % 8) * 128:(tt % 8 + 1) * 128], wub[:, k, :], k == 0, k == 7,
                       reads=[B_ring[0], B_ring[1], B_hT[tt]], writes=[B_ps[pb]], signal=(k == 7))
                evac(utok[:, tt, :], psf[:, pb, :], reads=[B_ps[pb]], writes=[B_ut[tt]])

            def pool_bank(g, tb):
                pb = pn[0] % 4
                pn[0] += 1
                for ti in range(4):
                    T = 4 * tb + ti
                    dst = psf[:, pb, ti * 128:(ti + 1) * 128]
                    gs = slice(g * 128, (g + 1) * 128)
                    if T == 0:
                        MM(dst, utok[:, 0, gs], wband[:, 3 * g + 2, :], True, True, reads=[B_ut[0], B_cst["wband"]],
                           writes=[B_ps[pb]], signal=False)
                    else:
                        MM(dst, utok[:, T - 1, gs], wband[:, 3 * g + 1, :], True, False, reads=[B_ut[T - 1], B_cst["wband"]],
                           writes=[B_ps[pb]], signal=False)
                        MM(dst, utok[:, T, gs], wband[:, 3 * g + 0, :], False, True, reads=[B_ut[T], B_cst["wband"]],
                           writes=[B_ps[pb]], signal=(ti == 3))
                evac(yT[:, 4 + g, tb * 512:(tb + 1) * 512], psf[:, pb, :], reads=[B_ps[pb]],
                     writes=[B_y[4 + g][4 * tb + t] for t in range(4)])

            def pool_linear_item(g, tb):
                pb = pn[0] % 4
                pn[0] += 1
                cs = slice(tb * 512, (tb + 1) * 512)
                yb = [B_y[4 + g][4 * tb + t] for t in range(4)]
                MM(psf[:, pb, :], poolw[:, g, :], yT[:, 4 + g, cs], True, True, reads=[B_pw] + yb, writes=[B_ps[pb]])
                evac(yT[:, 4 + g, cs], psf[:, pb, :], reads=[B_ps[pb], B_cst["cols"]], writes=yb, scale=cols[:, 2 + g:3 + g])

            pl_items = [(g, tb) for tb in range(4) for g in range(4)]

            if not first_half_done:
                norm_group(list(range(8)), hTA, list(range(8)), B_hT[0:8])
            nctx = norm_pre_begin(list(range(8, 16)))
            for i in range(8):
                u_tile(i)
                norm_pre_square(nctx, i)
            norm_pre_finish(nctx)
            for g in range(4):
                pool_bank(g, 0)
            norm_fin_a(nctx, 0)
            for i in range(8):
                if i + 1 < 8:
                    norm_fin_a(nctx, i + 1)
                if i < 4:
                    pool_bank(i, 1)
                norm_fin_b(nctx, i, hTB, i, B_hT[8 + i])
                if i >= 1:
                    u_tile(8 + i - 1)
            u_tile(NT - 1)
            for tb in (2, 3):
                for g in range(4):
                    pool_bank(g, tb)
            for m in order[1:RING]:
                issue_chunk(m)

            def proj_chunk(m, tb):
                slot = chunk_slots[m]
                pb = pn[0] % 4
                pn[0] += 1
                for k in range(8):
                    MM(psf[:, pb, :], ring[:, slot, k * 128:(k + 1) * 128], hTs[tb // 2][:, k, (tb % 2) * 512:(tb % 2 + 1) * 512],
                       k == 0, k == 7, reads=[B_ring[slot]] + B_hT[4 * tb:4 * tb + 4], writes=[B_ps[pb]], signal=(k == 7))
                return pb

            def after_chunk(m):
                i = order.index(m)
                if i + RING < len(order):
                    issue_chunk(order[i + RING])

            kq_n = [0]
            B_k = [None]

            kq_pend = []

            def kq_finish():
                m, tb, pb, s2 = kq_pend.pop(0)
                cs = slice(tb * 512, (tb + 1) * 512)
                pb2 = 4 + s2
                MM(psf[:, pb2, :], bones[:], sq[:, s2, :], True, True, reads=[B_cst["bones"], B_sq[s2]], writes=[B_ps[pb2]])
                ACT(rtmp[:, s2, :], psf[:, pb2, :], AF.Ln, reads=[B_ps[pb2], B_cst["eps"]], writes=[B_rt[s2]], bias=epsc[:, 0:1])
                ACT(rtmp[:, s2, :], rtmp[:, s2, :], AF.Exp, reads=[B_rt[s2]], writes=[B_rt[s2]], scale=-0.5)
                if m < 6:
                    for c in range(2):
                        pr = slice(64 * c, 64 * c + 64)
                        STT(kpad[pr, c, cs], psf[pr, pb, :], cols[pr, 1:2], rtmp[pr, s2, :], ALU.mult, ALU.mult,
                            reads=[B_ps[pb], B_rt[s2], B_cst["cols"]], writes=[B_k[0][c][tb]])
                else:
                    j = m - 6
                    STT(yT[:, j, cs], psf[:, pb, :], gqs[:, 0:1], rtmp[:, s2, :], ALU.mult, ALU.mult,
                        reads=[B_ps[pb], B_rt[s2], B_cst["gqs"]], writes=[B_y[j][4 * tb + t] for t in range(4)])

            def do_kq(m):
                for tb in range(4):
                    pb = proj_chunk(m, tb)
                    s2 = kq_n[0] % 2
                    kq_n[0] += 1
                    ACT(sq[:, s2, :], psf[:, pb, :], AF.Square, reads=[B_ps[pb]], writes=[B_sq[s2]], scale=0.125)
                    kq_pend.append((m, tb, pb, s2))
                    if len(kq_pend) > 1:
                        kq_finish()
                after_chunk(m)

            fenk = P.fence()
            B_k[0] = [[Buf(f"k{c}_{t}", fenk) for t in range(4)] for c in range(2)]
            P.op("pool", "memset", dict(ap=kpad[:].rearrange("p c t -> p (c t)"), constant=0.0),
                 writes=[B_k[0][c][t] for c in range(2) for t in range(4)])
            while pl_items:
                pool_linear_item(*pl_items.pop(0))
            for m in order[:5]:
                do_kq(m)
            while kq_pend:
                kq_finish()
            B_k = B_k[0]
            slot = chunk_slots[10]
            for t4 in range(4):
                pb = pn[0] % 4
                pn[0] += 1
                for ti in range(4):
                    tt = 4 * t4 + ti
                    for k in range(8):
                        MM(psf[:, pb, ti * 128:(ti + 1) * 128], hTs[tt // 8][:, k, (tt % 8) * 128:(tt % 8 + 1) * 128],
                           ring[:, slot, k * 128:(k + 1) * 128], k == 0, k == 7,
                           reads=[B_ring[slot], B_hT[tt]], writes=[B_ps[pb]], signal=(k == 7 and ti == 3))
                ACT(vaug[:, 4 * t4:4 * t4 + 4, :, 64:128], psf[:, pb, :].rearrange("p (t c n) -> p t c n", t=4, c=2), AF.Copy,
                    reads=[B_ps[pb]], writes=[B_v[t4]])
            DMA("pool", "dma_wout", woutb[:].rearrange("p k n -> p (k n)"), wout_d[:, :], writes=[B_wout] + B_hT[8:16])
            iters = [(n, c) for n in range(NT) for c in range(2)]
            B_yq = lambda n: [B_y[j][n] for j in range(4)]

            def att_logits(it):
                n, c = iters[it]
                s2 = it % 2
                js = [1] if n == 0 else [0, 1]
                qb = slice(n * 128, (n + 1) * 128)
                for j in js:
                    kt = n - 1 + j
                    pl = 2 * s2 + j
                    MM(psf[:, pl, :], ident[:], bm[:, c, j, :], True, False, reads=[B_cst["ident"], B_bm],
                       writes=[B_ps[pl]], signal=False)
                    MM(psf[:, pl, :], kpad[:, c, kt * 128:(kt + 1) * 128], yT[:, 0:4, qb], False, True,
                       reads=[B_k[c][kt // 4]] + B_yq(n), writes=[B_ps[pl]])
                    ACT(probs[:, s2, j, :], psf[:, pl, :], AF.Exp, reads=[B_ps[pl]], writes=[B_probs[s2]])

            def att_pv_norm(it):
                n, c = iters[it]
                s2 = it % 2
                sn = n % 2
                js = [1] if n == 0 else [0, 1]
                pD, pS = 4 + 2 * sn, 5 + 2 * sn
                c0 = 64 if c == 0 else 0
                for (pp, is_v) in ((pD, True), (pS, False)):
                    for ji, j in enumerate(js):
                        kt = n - 1 + j
                        if is_v:
                            lhsT, rd = vaug[:, kt, c, c0:c0 + 128], [B_v[kt // 4], B_probs[s2]]
                        else:
                            lhsT, rd = onz[:, c0:c0 + 128], [B_cst["onz"], B_probs[s2]]
                        first = (c == 0 and ji == 0)
                        last = (c == 1 and ji == len(js) - 1)
                        MM(psf[:, pp, :], lhsT, probs[:, s2, j, :], first, last, reads=rd, writes=[B_ps[pp]],
                           signal=(ji == len(js) - 1))

            def att_norm(n):
                sn = n % 2
                pS = 5 + 2 * sn
                TT(trt[:, sn, :], psf[:, pS, :], eskf[:], ALU.add, reads=[B_ps[pS], B_eskf], writes=[B_trt[sn]])
                ACT(trt[:, sn, :], trt[:, sn, :], AF.Ln, reads=[B_trt[sn]], writes=[B_trt[sn]])
                ACT(trt[:, sn, :], trt[:, sn, :], AF.Exp, reads=[B_trt[sn]], writes=[B_trt[sn]], scale=-1.0)

            def att_scale(n):
                sn = n % 2
                pD = 4 + 2 * sn
                qb = slice(n * 128, (n + 1) * 128)
                TT(yT[:, 0:4, qb], psf[:, pD, :].rearrange("p (j q) -> p j q", j=4),
                   trt[:, sn, :].rearrange("p (j q) -> p j q", j=4), ALU.mult, reads=[B_ps[pD], B_trt[sn]], writes=B_yq(n))

            att_logits(0)
            for it in range(len(iters)):
                if it + 1 < len(iters):
                    att_logits(it + 1)
                att_pv_norm(it)
                n, c = iters[it]
                if c == 0 and n >= 1:
                    att_norm(n - 1)
                if c == 1 and n >= 1:
                    att_scale(n - 1)
            att_norm(NT - 1)
            att_scale(NT - 1)
            on = 0
            if next_gain is not None:
                load_gain(next_gain)
            pend = []
            for tt in range(NT):
                if pend and not pend[-1][2]:
                    norm_fin_a(pend[-1][0], 0)
                    pend[-1][2] = True
                for dh in range(2):
                    po = on % 4
                    on += 1
                    cs = slice(dh * 512, (dh + 1) * 512)
                    for k in range(8):
                        MM(psf[:, po, :], yT[:, k, tt * 128:(tt + 1) * 128], woutb[:, k, cs], k == 0, k == 7,
                           reads=[B_y[k][tt], B_wout], writes=[B_ps[po]], signal=(k == 7))
                    TT(xs[:, tt, cs], psf[:, po, :], xs[:, tt, cs], ALU.add, reads=[B_ps[po], B_xs[tt][dh]], writes=[B_xs[tt][dh]])
                if len(pend) >= 2 or (tt >= 8 and pend):
                    ctx, t0, _ = pend.pop(0)
                    norm_fin_b(ctx, 0, hTA, t0, B_hTA[t0])
                if next_gain is not None and tt < 8:
                    pend.append([norm_pre([tt]), tt, False])

        P.op("dve", "memset", dict(ap=ss[:], constant=0.0), writes=[B_ss0])
        P.op("dve", "memset", dict(ap=epsc[:], constant=EPS), writes=[B_cst["eps"]])
        ACT(junk[:, 0:1], epsc[:, 0:1], AF.Ln, reads=[B_cst["eps"]], writes=[])
        if stage == 1:
            ffn(0, final=False)
            tags = dump_xs()
        elif stage == 2:
            ffn(0, final=False, next_norm=(1, list(range(8))))
            mixer(first_half_done=True)
            tags = dump_xs()
        else:
            ffn(0, final=False, next_norm=(1, list(range(8))))
            mixer(first_half_done=True, next_gain=2)
            tags = ffn(1, final=True, first_norm_done=True)
        P.wait_all("sp", [max(tags, key=lambda t: t[1])])

        with nc.Block() as block:
            @block.tensor
            def _(e):
                P.replay("pe", e)

            @block.scalar
            def _(e):
                P.replay("act", e)

            @block.vector
            def _(e):
                P.replay("dve", e)

            @block.gpsimd
            def _(e):
                P.replay("pool", e)

            @block.sync
            def _(e):
                P.replay("sp", e)
    return nc


def _t5_bucket(dist):
    n = np.maximum(dist, 0)
    max_exact = 16
    large = max_exact + (np.log(np.maximum(n, 1) / max_exact) / np.log(128 / max_exact) * (32 - max_exact)).astype(np.int32)
    large = np.minimum(large, 31)
    return np.where(n < max_exact, n, large).astype(np.int32)


def prep_shared(inp):
    f32 = np.float32
    sh = {}
    for i, pre in ((1, "ffn1"), (2, "ffn2")):
        wg = np.asarray(inp[pre + "_w_gate"], f32)[0]
        wu = np.asarray(inp[pre + "_w_up"], f32)[0]
        wd = np.asarray(inp[pre + "_w_down"], f32)[0]
        g = wg.reshape(8, 128, NF, 128).transpose(2, 1, 0, 3).reshape(NF, 128, 1024)
        u = wu.reshape(8, 128, NF, 128).transpose(2, 1, 0, 3).reshape(NF, 128, 1024)
        sh[f"wgu{i}"] = np.ascontiguousarray(np.concatenate([g, u], axis=2))
        d = wd.reshape(11, 2, 128, 2, 512).transpose(3, 0, 2, 1, 4).reshape(2, 11, 128, 1024)
        sh[f"wd{i}"] = np.ascontiguousarray(d)
    sh["gains"] = np.ascontiguousarray(np.stack([np.asarray(inp["ffn1_norm"], f32)[0], np.asarray(inp["mix_norm"], f32)[0],
                                                 np.asarray(inp["ffn2_norm"], f32)[0]]))
    win = np.asarray(inp["w_in"], f32)[0]
    chunks = []
    for g in range(4):
        chunks.append(win[:, 768 + 128 * g:768 + 128 * (g + 1)])
    chunks.append(win[:, 512:640])
    chunks.append(win[:, 512:640])
    for j in range(4):
        chunks.append(np.concatenate([win[:, 64 * j:64 * (j + 1)], win[:, 64 * (4 + j):64 * (5 + j)]], axis=1))
    chunks.append(win[:, 640:768])
    wl = np.stack(chunks)
    sh["win"] = np.ascontiguousarray(wl.reshape(11, 8, 128, 128).transpose(0, 2, 1, 3).reshape(11, 128, 1024))
    wo = np.asarray(inp["w_out"], f32)[0]
    wo = np.concatenate([np.concatenate([wo[64 * j:64 * (j + 1)], wo[64 * (4 + j):64 * (5 + j)]], axis=0) for j in range(4)]
                        + [wo[512:]], axis=0)
    sh["wout"] = np.ascontiguousarray(wo.reshape(8, 128, D).transpose(1, 0, 2).reshape(128, 8 * D))
    pw = np.asarray(inp["pool_w"], f32)[0]
    sh["poolw"] = np.ascontiguousarray(pw.transpose(1, 0, 2).reshape(128, 512))
    cols = np.zeros((128, 16), f32)
    cols[:, 0] = np.tile(np.asarray(inp["q_norm"], f32)[0], 2)
    cols[:, 1] = np.tile(np.asarray(inp["k_norm"], f32)[0], 2)
    ps = np.asarray(inp["pool_scale"], f32)[0]
    for g in range(4):
        cols[:, 2 + g] = ps[128 * g:128 * (g + 1)]
    sinks = np.asarray(inp["attn_sinks"], f32)[0]
    for j in range(4):
        cols[0:64, 6 + j] = sinks[j]
        cols[64:128, 6 + j] = sinks[4 + j]
    sh["cols"] = cols
    invc = np.zeros((128, 4, 16), f32)
    for g in range(4):
        w = 2 ** (g + 1)
        invc[:, g, :] = (1.0 / np.minimum(np.arange(1, 17), w)).astype(f32)[None, :]
    sh["invc"] = invc.reshape(128, 64)
    rb = np.asarray(inp["rel_bias"], f32)
    rbx = np.concatenate([rb, np.full((1, 8), NEGM, f32)], axis=0)
    sl = np.arange(128)[:, None]
    q = np.arange(128)[None, :]
    bmh = np.zeros((128, 2, 2, 2, 2, 128), f32)
    for j in range(2):
        dist = q - sl + (128 if j == 0 else 0)
        valid = (dist >= 0) & (dist < 128)
        idx = np.where(valid, _t5_bucket(dist), 32)
        for c in range(2):
            for hj in range(4):
                bmh[:, c, j, hj // 2, hj % 2, :] = rbx[idx, 4 * c + hj]
    sh["bm"] = np.ascontiguousarray(bmh.reshape(128, 2048))
    sh["ident"] = np.eye(128, dtype=f32)
    bo = np.zeros((128, 128), f32)
    bo[0:64, 0:64] = 1.0
    bo[64:128, 64:128] = 1.0
    sh["bones"] = bo
    sh["wu"] = np.ascontiguousarray(win[:, 768:1280].reshape(8, 128, 512).transpose(1, 0, 2).reshape(128, 8 * 512))
    wb = np.zeros((128, 12, 128), f32)
    s_i = np.arange(128)[:, None]
    t_i = np.arange(128)[None, :]
    for g in range(4):
        w = 2 ** (g + 1)
        band = ((t_i - s_i >= 0) & (t_i - s_i < w)).astype(f32)
        eye = (s_i == t_i).astype(f32)
        wb[:, 3 * g + 0, :] = band / w - eye
        wb[:, 3 * g + 1, :] = ((t_i + 128 - s_i) < w).astype(f32) / w
        wb[:, 3 * g + 2, :] = band / np.minimum(t_i + 1, w).astype(f32) - eye
    sh["wband"] = np.ascontiguousarray(wb.reshape(128, 12 * 128))
    oz = np.zeros((128, 192), f32)
    oz[:, 64:128] = 1.0
    sh["onz"] = oz
    return sh


_CACHE = {}


def kernel(**inputs):
    stage = inputs.pop("_stage", 99)
    x = np.asarray(inputs["x"], np.float32)
    sh = prep_shared(inputs)
    if stage not in _CACHE:
        _CACHE[stage] = build(stage)
    nc = _CACHE[stage]
    in_maps = []
    for b in range(8):
        m = dict(sh)
        m["x"] = np.ascontiguousarray(x[b])
        in_maps.append(m)
    res = run_bass_kernel_spmd(nc, in_maps, core_ids=list(range(8)))
    return np.stack([res.results[b]["out"] for b in range(8)]).astype(np.float32)
```
